# Optimizing a Trainium2 kernel written in Bass

```python
import jax, jax.numpy as jnp
from jax import lax
import numpy as np

D_MODEL = 1024
BATCH = 16
SEQ = 256
DEPTH = 4
DEC_BATCH = 4
DEC_SEQ = 1024
PAST_LEN = 512

GRID_W = 64
D_RNN = 1024
H_RNN = 16
BS_RNN = D_RNN // H_RNN
RNN_CONV = 4
C_RG = 8.0
D_SG = 1024
SG_GROUPS = 8
SG_GD = D_SG // SG_GROUPS
CHUNK = 128
D_CF = 1024
CF_CONV = 31
D_FF = 4096
FFN_CONV = 3
N_BRANCH = 3
N_IN = 2 * D_RNN + 2 * D_SG + 2 * D_CF + N_BRANCH * D_MODEL
EPS = 1e-6
POS_BASE = 10000.0

kernel_name = 'hybrid_rglru_gmlp_conformer_dit_step'


def _rms(x, g):
    xf = x.astype(jnp.float32)
    y = xf * lax.rsqrt(jnp.mean(xf * xf, axis=-1, keepdims=True) + EPS)
    return (y * g.astype(jnp.float32)).astype(x.dtype)


def _ln(x, g, b=None):
    xf = x.astype(jnp.float32)
    mu = jnp.mean(xf, axis=-1, keepdims=True)
    xc = xf - mu
    y = xc * lax.rsqrt(jnp.mean(xc * xc, axis=-1, keepdims=True) + EPS) * g.astype(jnp.float32)
    if b is not None:
        y = y + b.astype(jnp.float32)
    return y.astype(x.dtype)


def _dwconv(x, w, b, pad):
    C = x.shape[-1]
    y = lax.conv_general_dilated(x, w[:, None, :].astype(x.dtype), window_strides=(1,), padding=[pad],
                                 dimension_numbers=('NWC', 'WIO', 'NWC'), feature_group_count=C)
    return y + b.astype(x.dtype)


def _pos_2d(seq_len, d, dtype):
    rows = seq_len // GRID_W
    t = jnp.arange(rows * GRID_W)
    r = (t // GRID_W).astype(jnp.float32)
    col = (t % GRID_W).astype(jnp.float32)
    q = d // 4
    omega = 1.0 / (POS_BASE ** (jnp.arange(q, dtype=jnp.float32) / q))
    def emb(p):
        ang = p[:, None] * omega[None, :]
        return jnp.concatenate([jnp.sin(ang), jnp.cos(ang)], axis=-1)
    return jnp.concatenate([emb(r), emb(col)], axis=-1).astype(dtype)


def _lin_comb(e1, e2):
    a1, b1 = e1
    a2, b2 = e2
    return a1 * a2, a2 * b1 + b2


def _rglru(xc, h0, rg_w, rg_b, rg_lambda):
    B, S, _ = xc.shape
    xh = xc.reshape(B, S, H_RNN, BS_RNN)
    g = jnp.einsum('bshi,dkhij->dkbshj', xh, rg_w.astype(xc.dtype)).reshape(2, 2, B, S, D_RNN)
    g = jax.nn.sigmoid(g.astype(jnp.float32) + rg_b.astype(jnp.float32)[:, :, None, None, :])
    r, i = g[:, 0], g[:, 1]
    log_a = -C_RG * r * jax.nn.softplus(-rg_lambda.astype(jnp.float32))[:, None, None, :]
    a = jnp.exp(log_a)
    u = jnp.sqrt(-jnp.expm1(2.0 * log_a)) * i * xc.astype(jnp.float32)[None]
    A_f, B_f = lax.associative_scan(_lin_comb, (a[0], u[0]), axis=1)
    A_b, B_b = lax.associative_scan(_lin_comb, (a[1], u[1]), axis=1, reverse=True)
    h_f = A_f * h0[:, 0, None, :] + B_f
    h_b = A_b * h0[:, 1, None, :] + B_b
    return h_f, h_b


def _layer(x, cond, h0, p):
    dt = x.dtype
    B, S, _ = x.shape
    mod = jax.nn.silu(cond) @ p['w_mod'] + p['b_mod']
    sh1, sc1, gt1, sh2, sc2, gt2 = [m[:, None, :] for m in jnp.split(mod, 6, axis=-1)]
    gn = p['g_norm']

    h = _rms(x, gn[0]) * (1 + sc1) + sh1
    z = h @ p['w_in'] + p['b_in']
    o = np.cumsum([D_RNN, D_RNN, 2 * D_SG, 2 * D_CF])
    z_rx, z_rg, z_sg, z_cf, z_gate = jnp.split(z, [int(v) for v in o], axis=-1)

    xc = _dwconv(z_rx, p['rnn_conv_w'], p['rnn_conv_b'], (2, 1))
    h_f, h_b = _rglru(xc, h0, p['rg_w'], p['rg_b'], p['rg_lambda'])
    y_a = (h_f + h_b).astype(dt) * jax.nn.gelu(z_rg)

    uv = jax.nn.gelu(z_sg)
    su, sv = jnp.split(uv, 2, axis=-1)
    sv = _ln(sv, p['sg_norm_g']).reshape(B, S // CHUNK, CHUNK, SG_GROUPS, SG_GD)
    mixed = jnp.einsum('gpq,bnqgc->bnpgc', p['sg_w'].astype(dt), sv) + p['sg_b'].T[None, None, :, :, None]
    y_b = su * mixed.reshape(B, S, D_SG)

    ca, cg = jnp.split(z_cf, 2, axis=-1)
    yc = ca * jax.nn.sigmoid(cg)
    yc = _dwconv(yc, p['cf_conv_w'], p['cf_conv_b'], (CF_CONV // 2, CF_CONV // 2))
    y_c = jax.nn.silu(_ln(yc, p['cf_ln_g'], p['cf_ln_b']))

    g_a, g_b, g_c = jnp.split(jax.nn.sigmoid(z_gate), 3, axis=-1)
    wb = p['w_branch']
    merged = g_a * (y_a @ wb[0]) + g_b * (y_b @ wb[1]) + g_c * (y_c @ wb[2])
    out = merged @ p['w_out']
    x = x + gt1 * _rms(out, gn[1])

    h = _rms(x, gn[2]) * (1 + sc2) + sh2
    up = _dwconv(h @ p['ffn_up'], p['ffn_conv_w'], p['ffn_conv_b'], (FFN_CONV // 2, FFN_CONV // 2))
    fg, fv = jnp.split(up, 2, axis=-1)
    f = (jax.nn.gelu(fg) * fv) @ p['ffn_down']
    x = x + gt2 * _rms(f, gn[3])
    return x, h_f, h_b


def setup_inputs(seed: int = 0) -> dict:
    key = jax.random.key(seed)
    ks = jax.random.split(key, 32)
    nrm = lambda k, shape, s: jax.random.normal(k, shape, jnp.float32) * s
    L, D = DEPTH, D_MODEL
    a8 = jax.random.uniform(ks[10], (L, 2, D_RNN), jnp.float32, 0.9, 0.999)
    s = a8 ** (1.0 / C_RG)
    rg_lambda = jnp.log(s) - jnp.log1p(-s)
    return {
        'x_prompt': nrm(ks[0], (BATCH, SEQ, D), 1.0),
        'x_sample': nrm(ks[1], (DEC_BATCH, DEC_SEQ, D), 1.0),
        'state_rglru': nrm(ks[2], (DEC_BATCH, DEPTH, 2, D_RNN), 0.5),
        'c': nrm(ks[3], (DEC_BATCH, D), 1.0),
        'c_ctx': nrm(ks[4], (D,), 1.0),
        'w_mod': nrm(ks[5], (L, D, 6 * D), D ** -0.5),
        'b_mod': nrm(ks[6], (L, 6 * D), 0.01),
        'g_norm': 1.0 + nrm(ks[7], (L, 4, D), 0.05),
        'w_in': nrm(ks[8], (L, D, N_IN), D ** -0.5),
        'b_in': nrm(ks[9], (L, N_IN), 0.01),
        'rnn_conv_w': nrm(ks[11], (L, RNN_CONV, D_RNN), RNN_CONV ** -0.5),
        'rnn_conv_b': nrm(ks[12], (L, D_RNN), 0.01),
        'rg_w': nrm(ks[13], (L, 2, 2, H_RNN, BS_RNN, BS_RNN), BS_RNN ** -0.5),
        'rg_b': nrm(ks[14], (L, 2, 2, D_RNN), 0.01),
        'rg_lambda': rg_lambda,
        'sg_norm_g': 1.0 + nrm(ks[15], (L, D_SG), 0.05),
        'sg_w': nrm(ks[16], (L, SG_GROUPS, CHUNK, CHUNK), 0.5 * CHUNK ** -0.5),
        'sg_b': 1.0 + nrm(ks[17], (L, SG_GROUPS, CHUNK), 0.1),
        'cf_conv_w': nrm(ks[18], (L, CF_CONV, D_CF), CF_CONV ** -0.5),
        'cf_conv_b': nrm(ks[19], (L, D_CF), 0.01),
        'cf_ln_g': 1.0 + nrm(ks[20], (L, D_CF), 0.05),
        'cf_ln_b': nrm(ks[21], (L, D_CF), 0.01),
        'w_branch': nrm(ks[22], (L, N_BRANCH, D_RNN, D), D_RNN ** -0.5),
        'w_out': nrm(ks[23], (L, D, D), D ** -0.5),
        'ffn_up': nrm(ks[24], (L, D, 2 * D_FF), D ** -0.5),
        'ffn_conv_w': nrm(ks[25], (L, FFN_CONV, 2 * D_FF), FFN_CONV ** -0.5),
        'ffn_conv_b': nrm(ks[26], (L, 2 * D_FF), 0.01),
        'ffn_down': nrm(ks[27], (L, D_FF, D), D_FF ** -0.5),
    }


def reference(x_prompt, x_sample, state_rglru, c, c_ctx, w_mod, b_mod, g_norm, w_in, b_in,
              rnn_conv_w, rnn_conv_b, rg_w, rg_b, rg_lambda, sg_norm_g, sg_w, sg_b,
              cf_conv_w, cf_conv_b, cf_ln_g, cf_ln_b, w_branch, w_out, ffn_up, ffn_conv_w,
              ffn_conv_b, ffn_down):
    def layer_params(l):
        return {'w_mod': w_mod[l], 'b_mod': b_mod[l], 'g_norm': g_norm[l], 'w_in': w_in[l],
                'b_in': b_in[l], 'rnn_conv_w': rnn_conv_w[l], 'rnn_conv_b': rnn_conv_b[l],
                'rg_w': rg_w[l], 'rg_b': rg_b[l], 'rg_lambda': rg_lambda[l],
                'sg_norm_g': sg_norm_g[l], 'sg_w': sg_w[l], 'sg_b': sg_b[l],
                'cf_conv_w': cf_conv_w[l], 'cf_conv_b': cf_conv_b[l], 'cf_ln_g': cf_ln_g[l],
                'cf_ln_b': cf_ln_b[l], 'w_branch': w_branch[l], 'w_out': w_out[l],
                'ffn_up': ffn_up[l], 'ffn_conv_w': ffn_conv_w[l], 'ffn_conv_b': ffn_conv_b[l],
                'ffn_down': ffn_down[l]}

    y_prompt = x_prompt
    h0_ctx = jnp.zeros((x_prompt.shape[0], 2, D_RNN), jnp.float32)
    cond_ctx = c_ctx[None, :]
    states = []
    for l in range(DEPTH):
        y_prompt, h_f, h_b = _layer(y_prompt, cond_ctx, h0_ctx, layer_params(l))
        states.append(jnp.stack([h_f[:, -1], h_b[:, 0]], axis=1))
    new_state_rglru = jnp.stack(states, axis=1)

    y_sample = x_sample + _pos_2d(x_sample.shape[1], x_sample.shape[2], x_sample.dtype)[None]
    for l in range(DEPTH):
        y_sample, _, _ = _layer(y_sample, c, state_rglru[:, l].astype(jnp.float32), layer_params(l))

    return (y_prompt, y_sample, new_state_rglru)
```

```python
import bisect
import numpy as np
import concourse.bass as bass
import concourse.mybir as mybir
from concourse.bass_utils import run_bass_kernel_spmd

F32 = mybir.dt.float32
BF16 = mybir.dt.bfloat16
AF = mybir.ActivationFunctionType
ALU = mybir.AluOpType


class _Op:
    __slots__ = ("eng", "emit", "deps", "idx", "ticket", "dmakey", "needs_inc")


def _region(ap):
    t = ap.tensor
    dims = ap.ap
    pstride = dims[0][0]
    off = ap.offset
    if pstride > 0:
        p0 = off // pstride
        f0 = off % pstride
    else:
        p0 = 0
        f0 = off
    lo = f0
    hi = f0
    for st, cnt in dims[1:]:
        ext = st * (cnt - 1)
        if ext < 0:
            lo += ext
        else:
            hi += ext
    return (t.name, p0, p0 + dims[0][1], lo, hi + 1)


def _ovl(a, b):
    return a[1] < b[2] and b[1] < a[2] and a[3] < b[4] and b[3] < a[4]


def _covers(a, b):
    return a[1] <= b[1] and a[2] >= b[2] and a[3] <= b[3] and a[4] >= b[4]


class Prog:
    ENGS = ("pe", "act", "dve", "pool", "sp")

    def __init__(self, nc):
        self.nc = nc
        self.ops = []
        self.hist = {}
        self.dma_keys = {}

    def add(self, eng, emit, reads=(), writes=(), dma=None):
        op = _Op()
        op.eng = eng
        op.emit = emit
        op.idx = len(self.ops)
        op.dmakey = dma
        op.needs_inc = dma is not None
        op.ticket = None
        deps = set()
        rr = [_region(a) for a in reads]
        wr = [_region(a) for a in writes]
        for r in rr:
            h = self.hist.get(r[0])
            if h:
                for (reg, i, e, _d) in h[0]:
                    if _ovl(reg, r):
                        deps.add(i)
        for w in wr:
            h = self.hist.get(w[0])
            if h:
                for (reg, i, e, _d) in h[0]:
                    if _ovl(reg, w):
                        deps.add(i)
                for (reg, i, e, _d) in h[1]:
                    if _ovl(reg, w):
                        deps.add(i)
        for r in rr:
            h = self.hist.setdefault(r[0], [[], []])
            if dma is None:
                h[1] = [x for x in h[1] if not (x[2] == eng and not x[3] and _covers(r, x[0]))]
            h[1].append((r, op.idx, eng, dma is not None))
        for w in wr:
            h = self.hist.setdefault(w[0], [[], []])
            h[0] = [x for x in h[0] if not _covers(w, x[0])]
            h[1] = [x for x in h[1] if not _covers(w, x[0])]
            h[0].append((w, op.idx, eng, dma is not None))
        deps.discard(op.idx)
        op.deps = deps
        if dma is not None:
            self.dma_keys.setdefault(dma, []).append(op.idx)
        self.ops.append(op)
        return op

    def finalize(self, ctx):
        nc = self.nc
        ops = self.ops
        for op in ops:
            for d in op.deps:
                p = ops[d]
                if p.eng == "pe" and op.eng == "pe":
                    continue
                p.needs_inc = True
        cnt = {e: 0 for e in self.ENGS}
        for op in ops:
            if op.dmakey is None and op.needs_inc:
                cnt[op.eng] += 1
                op.ticket = cnt[op.eng]
        esem = {e: ctx.enter_context(nc.semaphore("sem_" + e)) for e in self.ENGS}
        dsem = {k: ctx.enter_context(nc.semaphore("dsem_" + str(k))) for k in self.dma_keys}
        block = ctx.enter_context(nc.Block())
        streams = {e: [o for o in ops if o.eng == e] for e in self.ENGS}
        dma_keys = self.dma_keys

        def run_stream(ename, eh):
            waited = {}
            for op in streams[ename]:
                need = {}
                for d in op.deps:
                    p = ops[d]
                    if p.dmakey is not None:
                        lst = dma_keys[p.dmakey]
                        val = 16 * bisect.bisect_left(lst, op.idx)
                        key = ("d", p.dmakey)
                    else:
                        if p.eng == "pe" and ename == "pe":
                            continue
                        val = p.ticket
                        key = ("e", p.eng)
                    if need.get(key, 0) < val:
                        need[key] = val
                for key, val in need.items():
                    if waited.get(key, 0) >= val:
                        continue
                    waited[key] = val
                    s = dsem[key[1]] if key[0] == "d" else esem[key[1]]
                    eh.wait_ge(s, val)
                ins = op.emit(eh)
                if op.dmakey is not None:
                    ins.then_inc(dsem[op.dmakey], 16)
                elif op.needs_inc:
                    ins.then_inc(esem[ename], 1)
            if ename == "sp":
                for k, lst in dma_keys.items():
                    eh.wait_ge(dsem[k], 16 * len(lst))

        @block.tensor
        def _(e):
            run_stream("pe", e)

        @block.scalar
        def _(e):
            run_stream("act", e)

        @block.vector
        def _(e):
            run_stream("dve", e)

        @block.gpsimd
        def _(e):
            run_stream("pool", e)

        @block.sync
        def _(e):
            run_stream("sp", e)


D = 1024
T = 1024
NCH = 8
SEG = 256
NSEG = 4
N_IN = 9216
DFF = 4096
EPS = 1e-6
NW = 3
TWO_PI = 6.283185307179586

ROWS = [("b_in", 72), ("rcw", 32), ("rcb", 8), ("rgb", 32), ("lam", 16), ("sgg", 8), ("cfw", 248),
        ("cfb", 8), ("clg", 8), ("clb", 8), ("gn", 32), ("fcw", 192), ("fcb", 64), ("bmod", 48)]
COL = {}
_o = 0
for _n, _c in ROWS:
    COL[_n] = _o
    _o += _c
NROWS = _o
NBLK = (NROWS + 127) // 128
_o = NBLK * 128
for _n, _c in [("mod", 48), ("gs1", 8), ("gg1", 8), ("gs2", 8), ("gg2", 8), ("clh", 16), ("cl1", 16),
               ("w0f", 64), ("w2f", 64), ("hb", 32), ("tmp", 16), ("rcf", 32)]:
    COL[_n] = _o
    _o += _c
NPRM = _o

WNAMES = ["w_mod", "b_mod", "g_norm", "w_in", "b_in", "rnn_conv_w", "rnn_conv_b", "rg_w", "rg_b",
          "rg_lambda", "sg_norm_g", "sg_w", "sg_b", "cf_conv_w", "cf_conv_b", "cf_ln_g", "cf_ln_b",
          "w_branch", "w_out", "ffn_up", "ffn_conv_w", "ffn_conv_b", "ffn_down"]
WSHAPES = {"w_mod": [D, 6 * D], "b_mod": [6 * D], "g_norm": [4, D], "w_in": [D, N_IN], "b_in": [N_IN],
           "rnn_conv_w": [4, D], "rnn_conv_b": [D], "rg_w": [2, 2, 16, 64, 64], "rg_b": [2, 2, D],
           "rg_lambda": [2, D], "sg_norm_g": [D], "sg_w": [8, 128, 128], "sg_b": [8, 128],
           "cf_conv_w": [31, D], "cf_conv_b": [D], "cf_ln_g": [D], "cf_ln_b": [D],
           "w_branch": [3, D, D], "w_out": [D, D], "ffn_up": [D, 2 * DFF], "ffn_conv_w": [3, 2 * DFF],
           "ffn_conv_b": [2 * DFF], "ffn_down": [NCH, 128, 32 * 128]}


class WStream:
    def __init__(self, P, slots, plan):
        self.P = P
        self.slots = slots
        self.plan = plan
        self.rec = []
        self.i = 0
        self.issued = 0
        self.done = set()
        self.done_upto = -1

    def acquire(self, pieces):
        idx = self.i
        self.i += 1
        self.rec.append(pieces)
        if self.plan is not None:
            self._pump()
            assert self.issued > idx, "weight ring stalled: release earlier units first"
        return idx, self.slots[idx % NW]

    def release(self, idx):
        self.done.add(idx)
        while (self.done_upto + 1) in self.done:
            self.done_upto += 1
        if self.plan is not None:
            self._pump()

    def _pump(self):
        while self.issued < len(self.plan) and self.issued - NW <= self.done_upto:
            j = self.issued
            slot = self.slots[j % NW]
            for dstf, src in self.plan[j]:
                dst = dstf(slot)
                self.P.add("pool", (lambda d, s: lambda e: e.dma_start(out=d, in_=s))(dst, src),
                           writes=[dst], dma="w%d" % (j % NW))
            self.issued += 1


class _DryProg:
    def add(self, *a, **k):
        return None


def build_nc(depth=4, stop_after=None, dbg=()):
    nc = bass.Bass("TRN2", target_bir_lowering=False)
    from contextlib import ExitStack
    Lr = depth
    din = {}
    din["x"] = nc.dram_tensor("x", [T, D], F32, kind="ExternalInput").ap()
    din["cond"] = nc.dram_tensor("cond", [8, 128], F32, kind="ExternalInput").ap()
    din["flag"] = nc.dram_tensor("flag", [128, 1], F32, kind="ExternalInput").ap()
    din["h0"] = nc.dram_tensor("h0", [Lr * 16, 128], F32, kind="ExternalInput").ap()
    for n in WNAMES:
        din[n] = nc.dram_tensor(n, [Lr] + WSHAPES[n], F32, kind="ExternalInput").ap()
    y_out = nc.dram_tensor("y", [T, D], F32, kind="ExternalOutput").ap()
    st_out = nc.dram_tensor("st", [NSEG * Lr * 16, 128], F32, kind="ExternalOutput").ap()
    dbg_out = {n: nc.dram_tensor("dbg_" + n, [128, NCH * T], F32, kind="ExternalOutput").ap() for n in dbg}

    with ExitStack() as ctx:
        def sb(name, shape, dt=F32):
            return ctx.enter_context(nc.sbuf_tensor(name, shape, dt))

        xT = sb("xT", [128, NCH, T])
        prm = sb("prm", [128, NPRM])
        ident_f = sb("ident_f", [128, 128])
        ident_b = sb("ident_b", [128, 128], BF16)
        ones_b = sb("ones_b", [128, 128], BF16)
        one11 = sb("one11", [1, 1])
        nhalf = sb("nhalf", [128, 1])
        flag = sb("flag_sb", [128, 1])
        condT = sb("condT", [128, 8])
        scb = sb("scb", [128, 8], BF16)
        h0T = sb("h0T", [128, Lr * 16])
        ST = sb("ST", [128, NSEG, Lr, 2, 8])
        rgw = sb("rgw", [128, NCH, 4, 128], BF16)
        sgwT = sb("sgwT", [128, 8, 128], BF16)
        bsg = sb("bsg", [128, 8, 128])
        small = sb("small", [128, 32])
        gg2k = sb("gg2k", [128, 8])
        wslots = [sb("wslot%d" % i, [128, 4096], BF16) for i in range(NW)]
        AR = sb("AR", [128, 24576])
        ARb = AR[:].bitcast(BF16)
        def bfv(off_b, n1, n2):
            return ARb[:, off_b // 2: off_b // 2 + n1 * n2].rearrange("p (a b) -> p a b", b=n2)
        def f32v(off_b, n1, n2):
            return AR[:, off_b // 4: off_b // 4 + n1 * n2].rearrange("p (a b) -> p a b", b=n2)
        K16 = 16384
        h_b = bfv(0, NCH, T)
        big32 = f32v(K16, NCH, T)
        ya = bfv(3 * K16, NCH, T)
        yb = bfv(4 * K16, NCH, T)
        yc = bfv(5 * K16, NCH, T)
        svn = bfv(5 * K16, 8, 1024)
        mg = ya
        f_b = bfv(0, 32, T)
        h2_b = bfv(4 * K16, NCH, T)
        fo32 = f32v(4 * K16, NCH, T)
        TA = sb("TA", [128, T]); TB = sb("TB", [128, T]); TC = sb("TC", [128, T]); TD = sb("TD", [128, T])
        TR = sb("TR", [128, T])
        ones_f = sb("ones_f", [128, 128])
        TE = TR
        TM = TC
        zpf = sb("zpf", [128, NSEG * 260])
        zp = zpf[:].rearrange("p (s t) -> p s t", t=260)
        rows = zpf[:, 0:1024].rearrange("p (q c) -> p q c", c=128)
        brow = zpf[0:1, 0:1024]
        mrow = TC[0:1, 0:512]
        sqb = [sb("sqb%d" % i, [128, T], BF16) for i in range(2)]
        xcb = sqb[0]
        ycp = [sb("ycp%d" % i, [128, NSEG, 286], BF16) for i in range(2)]
        NDG = 4
        dg = [sb("dg%d" % i, [128, 128], BF16) for i in range(NDG)]
        PS = [ctx.enter_context(nc.psum_tensor("ps%d" % i, [128, T], F32)) for i in range(3)]
        PSX = ctx.enter_context(nc.psum_tensor("psx", [128, T], F32))
        PSS = [0]

        def emit_all(P, W):
            PSS[0] = 0
            sqi = [0]
            dgi = [0]

            ring4 = [False]

            def psn():
                if ring4[0]:
                    t = (PS + [PSX])[PSS[0] % 4]
                else:
                    t = PS[PSS[0] % 3]
                PSS[0] += 1
                return t

            def aps(*a):
                return [v for v in a if v is not None and not isinstance(v, (int, float))]

            def mm(out, lhsT, rhs, start, stop):
                P.add("pe", lambda e: e.matmul(out, lhsT=lhsT, rhs=rhs, start=start, stop=stop),
                      reads=[lhsT, rhs], writes=[out])

            def tr(out, in_, ident):
                P.add("pe", lambda e: e.transpose(out=out, in_=in_, identity=ident), reads=[in_, ident], writes=[out])

            def act(out, in_, func, bias=None, scale=None):
                kw = {}
                if bias is not None:
                    kw["bias"] = bias
                if scale is not None:
                    kw["scale"] = scale
                P.add("act", lambda e: e.activation(out=out, in_=in_, func=func, **kw),
                      reads=aps(in_, bias, scale), writes=[out])

            def tt(out, in0, in1, op, eng="dve"):
                P.add(eng, lambda e: e.tensor_tensor(out=out, in0=in0, in1=in1, op=op), reads=[in0, in1], writes=[out])

            def ts(out, in0, s1, s2, op0, op1=None, eng="dve"):
                if op1 is None:
                    P.add(eng, lambda e: e.tensor_scalar(out=out, in0=in0, scalar1=s1, scalar2=None, op0=op0),
                          reads=aps(in0, s1), writes=[out])
                else:
                    P.add(eng, lambda e: e.tensor_scalar(out=out, in0=in0, scalar1=s1, scalar2=s2, op0=op0, op1=op1),
                          reads=aps(in0, s1, s2), writes=[out])

            def stt(out, in0, sc, in1, op0, op1):
                P.add("dve", lambda e: e.scalar_tensor_tensor(out=out, in0=in0, scalar=sc, in1=in1, op0=op0, op1=op1),
                      reads=aps(in0, sc, in1), writes=[out])

            def cp(out, in_, eng="dve"):
                P.add(eng, lambda e: e.tensor_copy(out=out, in_=in_), reads=[in_], writes=[out])

            def memset(t, v, eng="dve"):
                P.add(eng, lambda e: e.memset(t, v), writes=[t])

            def dma(out, in_, key, eng="sp", rd=None, wr=None):
                P.add(eng, lambda e: e.dma_start(out=out, in_=in_), reads=rd or [], writes=wr or [], dma=key)

            def dump(name, ap3):
                if name not in dbg_out:
                    return
                for c in range(NCH):
                    cp(TA[:], ap3[:, c, :])
                    dma(dbg_out[name][:, c * T:(c + 1) * T], TA[:], "out", rd=[TA[:]])

            def pc(name, i=0, n=1):
                return prm[:, COL[name] + i: COL[name] + i + n]

            def seg4(ap):
                return ap.rearrange("p (s t) -> p s t", t=SEG)

            def wacq(view, K, n):
                idx, slot = W.acquire([((lambda K=K, n=n: lambda s: s[:, 0:K * n].rearrange("p (k n) -> p k n", n=n))(), view)])
                return idx, slot[:, 0:K * n].rearrange("p (k n) -> p k n", n=n)

            def wacq2(pieces, K):
                pl = [((lambda o=o, n=v.shape[2]: lambda s: s[:, :].rearrange("p (k n) -> p k n", n=512)[:, 0:K, o:o + n])(), v)
                      for o, v in pieces]
                idx, slot = W.acquire(pl)
                return idx, slot[:, :].rearrange("p (k n) -> p k n", n=512)

            def pipeline(gens, lag, maxact=2):
                active = []
                pending = list(gens)
                while active or pending:
                    if pending and len(active) < maxact and (not active or active[-1][1] >= lag):
                        active.append([pending.pop(0), 0])
                    for a in list(active):
                        try:
                            next(a[0])
                            a[1] += 1
                        except StopIteration:
                            active.remove(a)

            def proj_cols(ps_t, wu, col0, src_b, nk=8):
                for hf_ in range(2):
                    for k in range(nk):
                        mm(ps_t[:, hf_ * 512:(hf_ + 1) * 512], wu[:, k, col0:col0 + 128],
                           src_b[:, k, hf_ * 512:(hf_ + 1) * 512], k == 0, k == nk - 1)

            def rstd_from(ps_t, dst):
                act(dst[:], ps_t[:], AF.Ln, bias=float(EPS), scale=1.0 / D)
                act(dst[:], dst[:], AF.Exp, scale=-0.5)

            memset(ident_f[:], 0.0, "pool")
            P.add("pool", lambda e: e.affine_select(out=ident_f[:], in_=ident_f[:], compare_op=ALU.not_equal, fill=1.0,
                                                    base=0, pattern=[[-1, 128]], channel_multiplier=1),
                  reads=[ident_f[:]], writes=[ident_f[:]])
            cp(ident_b[:], ident_f[:])
            memset(ones_b[:], 1.0)
            memset(ones_f[:], 1.0)
            memset(one11[:], 1.0)
            memset(nhalf[:], -0.5)
            for i in range(2):
                memset(ycp[i][:], 0.0)
            memset(rgw[:], 0.0, "pool")
            memset(ST[:], 0.0)
            dma(flag[:], din["flag"], "in", wr=[flag[:]])
            dma(rows[0:8, 0, :], din["cond"], "in", wr=[rows[0:8, 0, :]])
            p0 = psn()
            tr(p0[:, 0:8], rows[0:8, 0, :], ident_f[0:8, 0:8])
            cp(condT[:], p0[:, 0:8])
            act(scb[:], condT[:], AF.Silu)
            nh = Lr * 16
            dma(rows[0:nh, 1, :], din["h0"], "in", wr=[rows[0:nh, 1, :]])
            p0 = psn()
            tr(p0[:, 0:nh], rows[0:nh, 1, :], ident_f[0:nh, 0:nh])
            cp(h0T[:], p0[:, 0:nh])
            for tq in range(8):
                stg = TA if tq % 2 == 0 else TB
                dma(stg[:], din["x"][tq * 128:(tq + 1) * 128, :], "in", wr=[stg[:]])
                p0 = psn()
                for k in range(NCH):
                    tr(p0[:, k * 128:(k + 1) * 128], stg[:, k * 128:(k + 1) * 128], ident_f[:])
                dst = xT[:, :, tq * 128:(tq + 1) * 128]
                src = p0[:].rearrange("p (k t) -> p k t", t=128)
                if tq % 2 == 0:
                    act(dst, src, AF.Copy)
                else:
                    cp(dst, src)
            ji = small[:, 0:2].bitcast(mybir.dt.int32)
            P.add("pool", lambda e: e.iota(ji, pattern=[[128, 2]], base=0, channel_multiplier=1), writes=[small[:, 0:2]])
            cp(small[:, 2:4], ji)
            act(small[:, 4:6], small[:, 2:4], AF.Exp, scale=-float(np.log(10000.0)) / 256.0)
            ti = TC[:].bitcast(mybir.dt.int32)
            for kind in range(4):
                if kind % 2 == 0:
                    pat = [[1, 16], [0, 64]] if kind == 0 else [[0, 16], [1, 64]]
                    P.add("pool", (lambda pat: lambda e: e.iota(ti, pattern=pat, base=0, channel_multiplier=0))(pat),
                          writes=[TC[:]])
                    cp(TD[:], ti)
                shift = 0.0 if kind % 2 == 0 else float(np.pi / 2)
                for m in range(2):
                    k = kind * 2 + m
                    ts(TA[:], TD[:], small[:, 4 + m:5 + m], shift, ALU.mult, ALU.add)
                    ts(TB[:], TA[:], 1.0 / TWO_PI, 12582912.0, ALU.mult, ALU.add)
                    ts(TB[:], TB[:], -12582912.0, None, ALU.add)
                    stt(TA[:], TB[:], -TWO_PI, TA[:], ALU.mult, ALU.add)
                    ts(TA[:], TA[:], 3.14159, -3.14159, ALU.min, ALU.max)
                    act(TB[:], TA[:], AF.Sin)
                    stt(xT[:, k, :], TB[:], flag[:], xT[:, k, :], ALU.mult, ALU.add)

            dump("xT0", xT)
            def layer_params(l):
                r0 = 0
                srcs = [din["b_in"][l].rearrange("(r c) -> r c", c=128),
                        din["rnn_conv_w"][l].rearrange("k (r c) -> (k r) c", c=128),
                        din["rnn_conv_b"][l].rearrange("(r c) -> r c", c=128),
                        din["rg_b"][l].rearrange("d k (r c) -> (d k r) c", c=128),
                        din["rg_lambda"][l].rearrange("d (r c) -> (d r) c", c=128),
                        din["sg_norm_g"][l].rearrange("(r c) -> r c", c=128),
                        din["cf_conv_w"][l].rearrange("k (r c) -> (k r) c", c=128),
                        din["cf_conv_b"][l].rearrange("(r c) -> r c", c=128),
                        din["cf_ln_g"][l].rearrange("(r c) -> r c", c=128),
                        din["cf_ln_b"][l].rearrange("(r c) -> r c", c=128),
                        din["g_norm"][l].rearrange("k (r c) -> (k r) c", c=128),
                        din["ffn_conv_w"][l].rearrange("k (r c) -> (k r) c", c=128),
                        din["ffn_conv_b"][l].rearrange("(r c) -> r c", c=128),
                        din["b_mod"][l].rearrange("(r c) -> r c", c=128)]
                for (nm, cnt), src in zip(ROWS, srcs):
                    done = 0
                    while done < cnt:
                        g = r0 + done
                        q, r = g // 128, g % 128
                        n = min(cnt - done, 128 - r)
                        dma(rows[r:r + n, q, :], src[done:done + n, :], "in", wr=[rows[r:r + n, q, :]])
                        done += n
                    r0 += cnt
                p0 = psn()
                for q in range(NBLK):
                    nr = min(128, NROWS - q * 128)
                    tr(p0[:, q * 128:q * 128 + nr], rows[0:nr, q, :], ident_f[0:nr, 0:nr])
                cp(prm[:, 0:NROWS], p0[:, 0:NROWS])
                for dk in range(4):
                    for hh in range(2):
                        src = din["rg_w"][l, dk // 2, dk % 2].rearrange("(c two) i j -> two i c j", two=2)[hh]
                        dst = rgw[hh * 64:(hh + 1) * 64, :, dk, hh * 64:(hh + 1) * 64]
                        dma(dst, src, "rgw", eng="pool", wr=[dst])
                sgv = TD[:].rearrange("p (g q) -> p g q", q=128)
                dma(sgv, din["sg_w"][l].rearrange("g p q -> p g q"), "in", wr=[TD[:]])
                p0 = psn()
                for g in range(8):
                    tr(p0[:, g * 128:(g + 1) * 128], sgv[:, g, :], ident_f[:])
                cp(sgwT[:].rearrange("p g q -> p (g q)"), p0[:])
                dma(brow[:], din["sg_b"][l].rearrange("g p -> (g p)").rearrange("(o n) -> o n", o=1), "in", wr=[brow[:]])
                p0 = psn()
                for hf_ in range(2):
                    mm(p0[:, hf_ * 512:(hf_ + 1) * 512], ones_f[0:1, :], brow[0:1, hf_ * 512:(hf_ + 1) * 512], True, True)
                cp(bsg[:].rearrange("p g q -> p (g q)"), p0[:])
                act(pc("tmp", 0, 16), pc("lam", 0, 16), AF.Exp, scale=-1.0)
                act(pc("tmp", 0, 16), pc("tmp", 0, 16), AF.Ln, bias=1.0)
                ts(pc("cl1", 0, 16), pc("tmp", 0, 16), -8.0, None, ALU.mult)
                ts(pc("clh", 0, 16), pc("tmp", 0, 16), -4.0, None, ALU.mult)
                ts(pc("hb", 0, 32), pc("rgb", 0, 32), 0.5, None, ALU.mult)
                ts(pc("w0f", 0, 64), pc("fcw", 0, 64), flag[:], None, ALU.mult)
                ts(pc("w2f", 0, 64), pc("fcw", 128, 64), flag[:], None, ALU.mult)
                ts(pc("rcf", 0, 32), pc("rcw", 0, 32), flag[:], None, ALU.mult)

            def layer_mod_gen(l):
                wv = din["w_mod"][l].rearrange("(k p) n -> p k n", p=128)
                for ug in range(12):
                    ui, u = wacq(wv[:, :, ug * 512:(ug + 1) * 512], 8, 512)
                    pm = psn()
                    for q in range(4):
                        for k in range(8):
                            mm(pm[:, q:q + 1], u[:, k, q * 128:(q + 1) * 128], scb[:, k:k + 1], k == 0, k == 7)
                    W.release(ui)
                    tt(pc("mod", ug * 4, 4), pm[:, 0:4], pc("bmod", ug * 4, 4), ALU.add)
                    yield
                stt(pc("gs1", 0, 8), pc("mod", 8, 8), 1.0, pc("gn", 0, 8), ALU.add, ALU.mult)
                tt(pc("gg1", 0, 8), pc("mod", 16, 8), pc("gn", 8, 8), ALU.mult)
                stt(pc("gs2", 0, 8), pc("mod", 32, 8), 1.0, pc("gn", 16, 8), ALU.add, ALU.mult)
                tt(pc("gg2", 0, 8), pc("mod", 40, 8), pc("gn", 24, 8), ALU.mult)
                yield

            def x_stats():
                pS = PSX
                for c in range(NCH):
                    q = sqb[sqi[0] % 2]
                    sqi[0] += 1
                    act(q[:], xT[:, c, :], AF.Square)
                    for hf_ in range(2):
                        mm(pS[:, hf_ * 512:(hf_ + 1) * 512], ones_b[:], q[:, hf_ * 512:(hf_ + 1) * 512], c == 0, c == NCH - 1)
                return pS

            def modnorm(dst_b, gsn, shoff):
                pS = x_stats()
                rstd_from(pS, TR)
                for c in range(NCH):
                    tmp = TA if c % 2 == 0 else TB
                    tt(tmp[:], xT[:, c, :], TR[:], ALU.mult)
                    act(dst_b[:, c, :], tmp[:], AF.Identity, bias=pc("mod", shoff + c), scale=pc(gsn, c))

            def proj_chunk(ps_t, wu, cc, src_b, nk=8):
                for hf_ in range(2):
                    for k in range(nk):
                        mm(ps_t[:, hf_ * 512:(hf_ + 1) * 512], wu[:, k, cc * 128:(cc + 1) * 128],
                           src_b[:, k, hf_ * 512:(hf_ + 1) * 512], k == 0, k == nk - 1)

            def residual_update(src32, ggn):
                for j in range(NCH):
                    tmp = TA if j % 2 == 0 else TB
                    tt(tmp[:], src32[:, j, :], TR[:], ALU.mult)
                    stt(xT[:, j, :], tmp[:], pc(ggn, j), xT[:, j, :], ALU.mult, ALU.add)

            layer_params(0)
            for _ in layer_mod_gen(0):
                pass
            for l in range(Lr):
                wi = din["w_in"][l].rearrange("(k p) n -> p k n", p=128)

                def win_unit(u):
                    return wacq(wi[:, :, u * 512:(u + 1) * 512], 8, 512)

                modnorm(h_b, "gs1", 0)

                dump("h", h_b)
                arB = f32v(4 * K16, NCH, T)
                SA = dict(z=TD[:], xc=(arB[:, 0, :], arB[:, 1, :]), r=(arB[:, 2, :], arB[:, 5, :]),
                          i=(arB[:, 3, :], arB[:, 6, :]), q=(arB[:, 4, :], arB[:, 7, :]), hf=TR[:],
                          xcb=(sqb[0][:], sqb[1][:]))
                unitsA = {}
                unitsC = {}
                ycc = big32

                def A_unit(c):
                    u, cc2 = c // 2, c % 2
                    if cc2 == 0:
                        unitsA[u] = wacq2([(0, wi[:, :, 2 * u * 128:(2 * u + 2) * 128]),
                                           (256, wi[:, :, 1024 + 2 * u * 128:1024 + (2 * u + 2) * 128])], 8)
                    return unitsA[u]

                def C_unit(c):
                    u, cc2 = c // 2, c % 2
                    if cc2 == 0:
                        unitsC[u] = wacq2([(0, wi[:, :, 4096 + 2 * u * 128:4096 + (2 * u + 2) * 128]),
                                           (256, wi[:, :, 5120 + 2 * u * 128:5120 + (2 * u + 2) * 128])], 8)
                    return unitsC[u]

                def C_front(c):
                    iu, wu = C_unit(c)
                    pg = psn()
                    proj_cols(pg, wu, 256 + (c % 2) * 128, h_b)
                    pa_ = psn()
                    proj_cols(pa_, wu, (c % 2) * 128, h_b)
                    if c % 2 == 1:
                        W.release(iu)
                    act(TA[:], pg[:], AF.Sigmoid, bias=pc("b_in", 40 + c))
                    yp = ycp[c % 2]
                    stt(yp[:, :, 15:271], seg4(pa_[:]), pc("b_in", 32 + c), seg4(TA[:]), ALU.add, ALU.mult)
                    ts(yp[:, 1:4, 0:15], yp[:, 0:3, 256:271], flag[:], None, ALU.mult)
                    ts(yp[:, 0:3, 271:286], yp[:, 1:4, 15:30], flag[:], None, ALU.mult)

                def C_conv(c):
                    yp = ycp[c % 2]
                    pcv = psn()
                    for k in range(31):
                        dgt = dg[dgi[0] % NDG]
                        dgi[0] += 1
                        ts(dgt[:], ident_b[:], pc("cfw", k * 8 + c), None, ALU.mult)
                        for hf_ in range(2):
                            mm(pcv[:, hf_ * 512:(hf_ + 1) * 512].rearrange("p (s t) -> p s t", t=SEG), dgt[:],
                               yp[:, 2 * hf_:2 * hf_ + 2, k:k + 256], k == 0, k == 30)
                    act(ycc[:, c, :], pcv[:], AF.Identity, bias=pc("cfb", c))
                    if c == 0:
                        act(TC[:], pcv[:], AF.Square, bias=pc("cfb", c))
                        cp(TB[:], ycc[:, c, :])
                    else:
                        act(TA[:], pcv[:], AF.Square, bias=pc("cfb", c))

                def C_lnacc(c):
                    tt(TB[:], TB[:], ycc[:, c, :], ALU.add)
                    tt(TC[:], TC[:], TA[:], ALU.add)

                def A_front_a(c):
                    iu, wu = A_unit(c)
                    p1 = psn()
                    proj_cols(p1, wu, (c % 2) * 128, h_b)
                    act(SA["z"], p1[:], AF.Identity, bias=pc("b_in", c))

                def A_front_b(c):
                    z, xc = SA["z"], SA["xc"][c % 2]
                    z4, xc4 = seg4(z), seg4(xc)
                    ts(xc, z, pc("rcw", 16 + c), pc("rcb", c), ALU.mult, ALU.add)
                    stt(xc4[:, :, 2:256], z4[:, :, 0:254], pc("rcw", c), xc4[:, :, 2:256], ALU.mult, ALU.add)
                    stt(xc4[:, 1:4, 0:2], z4[:, 0:3, 254:256], pc("rcf", c), xc4[:, 1:4, 0:2], ALU.mult, ALU.add)
                    stt(xc4[:, :, 1:256], z4[:, :, 0:255], pc("rcw", 8 + c), xc4[:, :, 1:256], ALU.mult, ALU.add)
                    stt(xc4[:, 1:4, 0:1], z4[:, 0:3, 255:256], pc("rcf", 8 + c), xc4[:, 1:4, 0:1], ALU.mult, ALU.add)
                    stt(xc4[:, :, 0:255], z4[:, :, 1:256], pc("rcw", 24 + c), xc4[:, :, 0:255], ALU.mult, ALU.add)
                    stt(xc4[:, 0:3, 255:256], z4[:, 1:4, 0:1], pc("rcf", 24 + c), xc4[:, 0:3, 255:256], ALU.mult, ALU.add)
                    cp(SA["xcb"][c % 2], xc)

                def A_gates(c):
                    xcb_ = SA["xcb"][c % 2]
                    for d in range(2):
                        pr_ = psn()
                        pi_ = psn()
                        for hf_ in range(2):
                            sl = slice(hf_ * 512, (hf_ + 1) * 512)
                            mm(pr_[:, sl], rgw[:, c, d * 2 + 0, :], xcb_[:, sl], True, True)
                            mm(pi_[:, sl], rgw[:, c, d * 2 + 1, :], xcb_[:, sl], True, True)
                        act(SA["r"][d], pr_[:], AF.Tanh, bias=pc("hb", (d * 2 + 0) * 8 + c), scale=0.5)
                        act(SA["i"][d], pi_[:], AF.Tanh, bias=pc("hb", (d * 2 + 1) * 8 + c), scale=0.5)

                def A_gates_b(c):
                    for d in range(2):
                        act(SA["q"][d], SA["r"][d], AF.Exp, bias=pc("cl1", d * 8 + c), scale=pc("cl1", d * 8 + c))
                        act(SA["r"][d], SA["r"][d], AF.Exp, bias=pc("clh", d * 8 + c), scale=pc("clh", d * 8 + c))
                    for d in range(2):
                        act(SA["q"][d], SA["q"][d], AF.Sqrt, bias=1.0, scale=-1.0)

                def A_scan(c):
                    xc, hf = SA["xc"][c % 2], SA["hf"]
                    for d in range(2):
                        a_, u_, s_ = SA["r"][d], SA["i"][d], SA["q"][d]
                        stt(u_, u_, 1.0, s_, ALU.add, ALU.mult)
                        stt(u_, u_, 0.5, xc, ALU.mult, ALU.mult)
                        col = (l * 2 + d) * 8 + c
                        if d == 0:
                            bnd = a_[:, 256:1024:256]
                            ts(bnd, bnd, flag[:], None, ALU.mult)
                            P.add("dve", (lambda a_=a_, u_=u_, hf=hf, col=col: lambda e: e.tensor_tensor_scan(
                                out=hf, data0=a_, data1=u_, initial=h0T[:, col:col + 1], op0=ALU.mult, op1=ALU.add))(),
                                reads=[a_, u_, h0T[:]], writes=[hf])
                            cp(ST[:, :, l, 0, c], hf[:, 255:1024:256])
                        else:
                            bnd = a_[:, 255:1023:256]
                            ts(bnd, bnd, flag[:], None, ALU.mult)
                            P.add("dve", (lambda a_=a_, u_=u_, s_=s_, col=col: lambda e: e.tensor_tensor_scan(
                                out=s_[:, ::-1], data0=a_[:, ::-1], data1=u_[:, ::-1], initial=h0T[:, col:col + 1],
                                op0=ALU.mult, op1=ALU.add))(),
                                reads=[a_, u_, h0T[:]], writes=[s_])
                            cp(ST[:, :, l, 1, c], s_[:, 0:1024:256])
                            tt(hf, hf, s_, ALU.add)

                def A_out(c):
                    iu, wu = unitsA[c // 2]
                    p2 = psn()
                    proj_cols(p2, wu, 256 + (c % 2) * 128, h_b)
                    if c % 2 == 1:
                        W.release(iu)
                    act(SA["r"][0], p2[:], AF.Gelu_apprx_tanh, bias=pc("b_in", 8 + c))
                    tt(ya[:, c, :], SA["hf"], SA["r"][0], ALU.mult)

                ring4[0] = True
                C_front(0)
                A_front_a(0)
                C_conv(0)
                A_front_b(0)
                for c in range(NCH):
                    A_gates(c)
                    if c + 1 < NCH:
                        C_front(c + 1)
                    A_gates_b(c)
                    if c + 1 < NCH:
                        A_front_a(c + 1)
                        C_conv(c + 1)
                        A_front_b(c + 1)
                    A_scan(c)
                    A_out(c)
                    if c + 1 < NCH:
                        C_lnacc(c + 1)
                dump("ya", ya)

                ring4[0] = False
                pS1 = PSX
                pS2 = psn()
                for hf_ in range(2):
                    sl = slice(hf_ * 512, (hf_ + 1) * 512)
                    mm(pS1[:, sl], ones_f[:], TB[:, sl], True, True)
                    mm(pS2[:, sl], ones_f[:], TC[:, sl], True, True)
                act(TM[:], pS1[:], AF.Identity, scale=1.0 / D)
                act(TD[:], pS1[:], AF.Square, scale=1.0 / D)
                stt(TR[:], pS2[:], 1.0 / D, TD[:], ALU.mult, ALU.subtract)
                act(TR[:], TR[:], AF.Ln, bias=float(EPS))
                act(TR[:], TR[:], AF.Exp, scale=-0.5)
                stt(TM[:], TM[:], -1.0, TR[:], ALU.mult, ALU.mult)

                def ctail_gen():
                    for c in range(NCH):
                        tt(TD[:], ycc[:, c, :], TR[:], ALU.mult)
                        tt(TD[:], TD[:], TM[:], ALU.add)
                        act(yc[:, c, :], TD[:], AF.Silu, bias=pc("clb", c), scale=pc("clg", c))
                        yield

                ring4[0] = True
                svn2 = bfv(K16, 8, 1024)
                dma(brow[:], din["b_in"][l, 3072:4096].rearrange("(o n) -> o n", o=1), "in", wr=[brow[:]])
                iv0, wv0 = win_unit(6)
                iv1, wv1 = win_unit(7)
                def bsv_gen():
                    for tb in range(4):
                        for tq in (2 * tb, 2 * tb + 1):
                            p1 = psn()
                            for hh, wv in enumerate((wv0, wv1)):
                                sl = slice(hh * 512, (hh + 1) * 512)
                                for k in range(8):
                                    mm(p1[:, sl], h_b[:, k, tq * 128:(tq + 1) * 128], wv[:, k, :], k == 0, False)
                                mm(p1[:, sl], ones_f[0:1, :], brow[0:1, sl], False, True)
                            svt = TA if tq % 2 == 0 else TB
                            act(svt[:], p1[:], AF.Gelu_apprx_tanh)
                            so = (tq % 2) * 16
                            P.add("dve", (lambda svt, so: lambda e: e.bn_stats(out=small[:, so:so + 6], in_=svt[:, 0:512]))(svt, so),
                                  reads=[svt[:, 0:512]], writes=[small[:, so:so + 6]])
                            P.add("dve", (lambda svt, so: lambda e: e.bn_stats(out=small[:, so + 6:so + 12], in_=svt[:, 512:1024]))(svt, so),
                                  reads=[svt[:, 512:1024]], writes=[small[:, so + 6:so + 12]])
                            P.add("dve", (lambda so: lambda e: e.bn_aggr(out=small[:, so + 12:so + 14], in_=small[:, so:so + 12]))(so),
                                  reads=[small[:, so:so + 12]], writes=[small[:, so + 12:so + 14]])
                        for tq in (2 * tb, 2 * tb + 1):
                            so = (tq % 2) * 16
                            act(small[:, so + 13:so + 14], small[:, so + 13:so + 14], AF.Sqrt, bias=float(EPS))
                        for tq in (2 * tb, 2 * tb + 1):
                            svt = TA if tq % 2 == 0 else TB
                            so = (tq % 2) * 16
                            P.add("dve", (lambda so: lambda e: e.reciprocal(out=small[:, so + 13:so + 14], in_=small[:, so + 13:so + 14]))(so),
                                  reads=[small[:, so + 13:so + 14]], writes=[small[:, so + 13:so + 14]])
                            stt(small[:, so + 14:so + 15], small[:, so + 12:so + 13], -1.0, small[:, so + 13:so + 14], ALU.mult, ALU.mult)
                            ts(svn2[:, tq, :], svt[:], small[:, so + 13:so + 14], small[:, so + 14:so + 15], ALU.mult, ALU.add)
                        yield

                ring4[0] = True
                pipeline([ctail_gen(), bsv_gen()], lag=1)
                dump("yc", yc)
                W.release(iv0)
                W.release(iv1)
                for ug in range(2):
                    iu, wu = win_unit(4 + ug)
                    for cc in range(4):
                        g = ug * 4 + cc
                        p2 = psn()
                        proj_chunk(p2, wu, cc, h_b)
                        act(TB[:], p2[:], AF.Gelu_apprx_tanh, bias=pc("b_in", 16 + g))
                        p1 = psn()
                        for n in range(8):
                            mm(p1[:, n * 128:(n + 1) * 128], svn2[:, n, g * 128:(g + 1) * 128], sgwT[:, g, :], True, True)
                        stt(TA[:].rearrange("p (n q) -> p n q", q=128), p1[:].rearrange("p (n q) -> p n q", q=128),
                            pc("sgg", g), bsg[:, g, :].unsqueeze(1).broadcast_to([128, 8, 128]), ALU.mult, ALU.add)
                        tt(yb[:, g, :], TA[:], TB[:], ALU.mult)
                    W.release(iu)
                dump("yb", yb)
                macc = big32
                ysrc = (ya, yb, yc)
                for br in range(3):
                    wbv = din["w_branch"][l, br].rearrange("(k p) n -> p k n", p=128)
                    for jp in range(4):
                        gc0 = 6144 + br * 1024 + jp * 256
                        imu, wmu = wacq2([(0, wi[:, :, gc0:gc0 + 256]), (256, wbv[:, :, jp * 256:(jp + 1) * 256])], 8)
                        for cc in range(2):
                            j = jp * 2 + cc
                            pg = psn()
                            proj_cols(pg, wmu, cc * 128, h_b)
                            pp = psn()
                            proj_cols(pp, wmu, 256 + cc * 128, ysrc[br])
                            sg_ = TA if j % 2 == 0 else TB
                            act(sg_[:], pg[:], AF.Sigmoid, bias=pc("b_in", 48 + br * 8 + j))
                            if br == 0:
                                tt(macc[:, j, :], sg_[:], pp[:], ALU.mult)
                            elif br == 1:
                                tt(sg_[:], sg_[:], pp[:], ALU.mult)
                                tt(macc[:, j, :], macc[:, j, :], sg_[:], ALU.add)
                            else:
                                tt(sg_[:], sg_[:], pp[:], ALU.mult)
                                tt(mg[:, j, :], macc[:, j, :], sg_[:], ALU.add)
                        W.release(imu)

                dump("mg", mg)
                ring4[0] = False
                of32 = big32
                wov = din["w_out"][l].rearrange("(k p) n -> p k n", p=128)
                pS = PSX
                for ug in range(2):
                    io, wo = wacq(wov[:, :, ug * 512:(ug + 1) * 512], 8, 512)
                    for cc in range(4):
                        j = ug * 4 + cc
                        p1 = psn()
                        proj_chunk(p1, wo, cc, mg)
                        act(of32[:, j, :], p1[:], AF.Copy)
                        q = sqb[sqi[0] % 2]
                        sqi[0] += 1
                        act(q[:], p1[:], AF.Square)
                        for hf_ in range(2):
                            sl = slice(hf_ * 512, (hf_ + 1) * 512)
                            mm(pS[:, sl], ones_b[:], q[:, sl], j == 0, j == 7)
                    W.release(io)
                rstd_from(pS, TR)
                residual_update(of32, "gg1")

                if stop_after == "s1":
                    break

                modnorm(h2_b, "gs2", 24)
                wup = din["ffn_up"][l].rearrange("(k p) n -> p k n", p=128)
                ring4[0] = True
                for ug in range(16):
                    ifu, wfu = wacq2([(0, wup[:, :, ug * 256:(ug + 1) * 256]),
                                      (256, wup[:, :, DFF + ug * 256: DFF + (ug + 1) * 256])], 8)
                    for cc in range(2):
                        i = ug * 2 + cc
                        Ys = []
                        for side in (0, 1):
                            cch = side * 32 + i
                            p1 = psn()
                            proj_cols(p1, wfu, side * 256 + cc * 128, h2_b)
                            Y = (TA, TB)[side] if i % 2 == 0 else (TC, TD)[side]
                            Ys.append(Y)
                            act(Y[:], p1[:], AF.Identity, bias=pc("fcb", cch), scale=pc("fcw", 64 + cch))
                            Y4 = seg4(Y[:])
                            p4 = seg4(p1[:])
                            stt(Y4[:, :, 1:256], p4[:, :, 0:255], pc("fcw", cch), Y4[:, :, 1:256], ALU.mult, ALU.add)
                            stt(Y4[:, :, 0:255], p4[:, :, 1:256], pc("fcw", 128 + cch), Y4[:, :, 0:255], ALU.mult, ALU.add)
                            stt(Y[:, 256:1024:256], p1[:, 255:1023:256], pc("w0f", cch), Y[:, 256:1024:256], ALU.mult, ALU.add)
                            stt(Y[:, 255:1023:256], p1[:, 256:1024:256], pc("w2f", cch), Y[:, 255:1023:256], ALU.mult, ALU.add)
                        act(Ys[0][:], Ys[0][:], AF.Gelu_apprx_tanh)
                        tt(f_b[:, i, :], Ys[0][:], Ys[1][:], ALU.mult, eng="pool")
                    W.release(ifu)
                ring4[0] = False
                cp(gg2k[:], pc("gg2", 0, 8))
                modg = None
                if l + 1 < Lr:
                    layer_params(l + 1)
                    modg = layer_mod_gen(l + 1)
                wdn = din["ffn_down"][l]
                pS = PSX
                nmod = 0
                for j in range(NCH):
                    while modg is not None and nmod < (13 * (j + 1) + 7) // 8:
                        next(modg, None)
                        nmod += 1
                    idn, wd = wacq(wdn[j].rearrange("p (k n) -> p k n", n=128), 32, 128)
                    p1 = psn()
                    proj_chunk(p1, wd, 0, f_b, nk=32)
                    W.release(idn)
                    act(fo32[:, j, :], p1[:], AF.Copy)
                    q = sqb[sqi[0] % 2]
                    sqi[0] += 1
                    act(q[:], p1[:], AF.Square)
                    for hf_ in range(2):
                        sl = slice(hf_ * 512, (hf_ + 1) * 512)
                        mm(pS[:, sl], ones_b[:], q[:, sl], j == 0, j == 7)
                rstd_from(pS, TR)
                for j in range(NCH):
                    tmp = TA if j % 2 == 0 else TB
                    tt(tmp[:], fo32[:, j, :], TR[:], ALU.mult)
                    stt(xT[:, j, :], tmp[:], gg2k[:, j:j + 1], xT[:, j, :], ALU.mult, ALU.add)

            for tq in range(8):
                p0 = psn()
                for k in range(NCH):
                    tr(p0[:, k * 128:(k + 1) * 128], xT[:, k, tq * 128:(tq + 1) * 128], ident_f[:])
                stg = TA if tq % 2 == 0 else TB
                if tq % 2 == 0:
                    act(stg[:], p0[:], AF.Copy)
                else:
                    cp(stg[:], p0[:])
                dma(y_out[tq * 128:(tq + 1) * 128, :], stg[:], "out", rd=[stg[:]])
            ncol = NSEG * Lr * 16
            stf = ST[:].rearrange("p s l d c -> p (s l d c)")
            nb = (ncol + 127) // 128
            p0 = psn()
            for q in range(nb):
                n = min(128, ncol - q * 128)
                tr(p0[0:n, q * 128:(q + 1) * 128], stf[:, q * 128:q * 128 + n], ident_f[:])
            for q in range(nb):
                n = min(128, ncol - q * 128)
                cp(TC[0:n, q * 128:(q + 1) * 128], p0[0:n, q * 128:(q + 1) * 128])
                dma(st_out[q * 128:q * 128 + n, :], TC[0:n, q * 128:(q + 1) * 128], "out", rd=[TC[0:n, q * 128:(q + 1) * 128]])

        Wd = WStream(_DryProg(), wslots, None)
        emit_all(_DryProg(), Wd)
        P = Prog(nc)
        W = WStream(P, wslots, Wd.rec)
        emit_all(P, W)
        P.finalize(ctx)
    return nc


_NC_CACHE = {}


def _run(inputs, depth=4, stop_after=None, dbg=(), ret_raw=False):
    key = (depth, stop_after, tuple(dbg))
    if key not in _NC_CACHE:
        _NC_CACHE[key] = build_nc(depth, stop_after, dbg)
    nc = _NC_CACHE[key]
    f32 = lambda a: np.ascontiguousarray(np.asarray(a, dtype=np.float32))
    xp = f32(inputs["x_prompt"])
    xs = f32(inputs["x_sample"])
    st = f32(inputs["state_rglru"])
    c = f32(inputs["c"])
    cctx = f32(inputs["c_ctx"])
    wts = {n: f32(inputs[n])[:depth] for n in WNAMES}
    wts["ffn_down"] = np.ascontiguousarray(
        wts["ffn_down"].reshape(depth, 32, 128, NCH, 128).transpose(0, 3, 2, 1, 4)).reshape(depth, NCH, 128, 32 * 128)
    in_maps = []
    for core in range(8):
        m = dict(wts)
        if core < 4:
            m["x"] = np.ascontiguousarray(xp[4 * core:4 * core + 4].reshape(T, D))
            m["cond"] = cctx.reshape(8, 128)
            m["flag"] = np.zeros((128, 1), np.float32)
            m["h0"] = np.zeros((depth * 16, 128), np.float32)
        else:
            j = core - 4
            m["x"] = xs[j]
            m["cond"] = np.ascontiguousarray(c[j].reshape(8, 128))
            m["flag"] = np.ones((128, 1), np.float32)
            m["h0"] = np.ascontiguousarray(st[j, :depth].reshape(depth * 16, 128))
        in_maps.append(m)
    res = run_bass_kernel_spmd(nc, in_maps, core_ids=list(range(8)))
    outs = res.results
    if ret_raw:
        return outs
    y_prompt = np.stack([outs[i]["y"] for i in range(4)]).reshape(16, 256, D)
    y_sample = np.stack([outs[4 + i]["y"] for i in range(4)])
    new_state = np.concatenate([outs[i]["st"].reshape(NSEG, depth, 2, D) for i in range(4)], axis=0)
    return (y_prompt.astype(np.float32), y_sample.astype(np.float32), new_state.astype(np.float32))


def kernel(**inputs):
    return _run(inputs, depth=4)
```

```python
import bisect
import numpy as np
import concourse.bass as bass
import concourse.mybir as mybir
from concourse.bass_utils import run_bass_kernel_spmd

F32 = mybir.dt.float32
BF16 = mybir.dt.bfloat16
AF = mybir.ActivationFunctionType
ALU = mybir.AluOpType


class _Op:
    __slots__ = ("eng", "emit", "deps", "idx", "ticket", "dmakey", "needs_inc")


def _region(ap):
    t = ap.tensor
    dims = ap.ap
    pstride = dims[0][0]
    off = ap.offset
    if pstride > 0:
        p0 = off // pstride
        f0 = off % pstride
    else:
        p0 = 0
        f0 = off
    lo = f0
    hi = f0
    for st, cnt in dims[1:]:
        ext = st * (cnt - 1)
        if ext < 0:
            lo += ext
        else:
            hi += ext
    return (t.name, p0, p0 + dims[0][1], lo, hi + 1)


def _ovl(a, b):
    return a[1] < b[2] and b[1] < a[2] and a[3] < b[4] and b[3] < a[4]


def _covers(a, b):
    return a[1] <= b[1] and a[2] >= b[2] and a[3] <= b[3] and a[4] >= b[4]


class Prog:
    ENGS = ("pe", "act", "dve", "pool", "sp")

    def __init__(self, nc):
        self.nc = nc
        self.ops = []
        self.hist = {}
        self.dma_keys = {}

    def add(self, eng, emit, reads=(), writes=(), dma=None):
        op = _Op()
        op.eng = eng
        op.emit = emit
        op.idx = len(self.ops)
        op.dmakey = dma
        op.needs_inc = dma is not None
        op.ticket = None
        deps = set()
        rr = [_region(a) for a in reads]
        wr = [_region(a) for a in writes]
        for r in rr:
            h = self.hist.get(r[0])
            if h:
                for (reg, i, e, _d) in h[0]:
                    if _ovl(reg, r):
                        deps.add(i)
        for w in wr:
            h = self.hist.get(w[0])
            if h:
                for (reg, i, e, _d) in h[0]:
                    if _ovl(reg, w):
                        deps.add(i)
                for (reg, i, e, _d) in h[1]:
                    if _ovl(reg, w):
                        deps.add(i)
        for r in rr:
            h = self.hist.setdefault(r[0], [[], []])
            if dma is None:
                h[1] = [x for x in h[1] if not (x[2] == eng and not x[3] and _covers(r, x[0]))]
            h[1].append((r, op.idx, eng, dma is not None))
        for w in wr:
            h = self.hist.setdefault(w[0], [[], []])
            h[0] = [x for x in h[0] if not _covers(w, x[0])]
            h[1] = [x for x in h[1] if not _covers(w, x[0])]
            h[0].append((w, op.idx, eng, dma is not None))
        deps.discard(op.idx)
        op.deps = deps
        if dma is not None:
            self.dma_keys.setdefault(dma, []).append(op.idx)
        self.ops.append(op)
        return op

    def finalize(self, ctx):
        nc = self.nc
        ops = self.ops
        for op in ops:
            for d in op.deps:
                p = ops[d]
                if p.eng == "pe" and op.eng == "pe":
                    continue
                p.needs_inc = True
        cnt = {e: 0 for e in self.ENGS}
        for op in ops:
            if op.dmakey is None and op.needs_inc:
                cnt[op.eng] += 1
                op.ticket = cnt[op.eng]
        esem = {e: ctx.enter_context(nc.semaphore("sem_" + e)) for e in self.ENGS}
        dsem = {k: ctx.enter_context(nc.semaphore("dsem_" + str(k))) for k in self.dma_keys}
        block = ctx.enter_context(nc.Block())
        streams = {e: [o for o in ops if o.eng == e] for e in self.ENGS}
        dma_keys = self.dma_keys

        def run_stream(ename, eh):
            waited = {}
            for op in streams[ename]:
                need = {}
                for d in op.deps:
                    p = ops[d]
                    if p.dmakey is not None:
                        lst = dma_keys[p.dmakey]
                        val = 16 * bisect.bisect_left(lst, op.idx)
                        key = ("d", p.dmakey)
                    else:
                        if p.eng == "pe" and ename == "pe":
                            continue
                        val = p.ticket
                        key = ("e", p.eng)
                    if need.get(key, 0) < val:
                        need[key] = val
                for key, val in need.items():
                    if waited.get(key, 0) >= val:
                        continue
                    waited[key] = val
                    s = dsem[key[1]] if key[0] == "d" else esem[key[1]]
                    eh.wait_ge(s, val)
                ins = op.emit(eh)
                if op.dmakey is not None:
                    ins.then_inc(dsem[op.dmakey], 16)
                elif op.needs_inc:
                    ins.then_inc(esem[ename], 1)
            if ename == "sp":
                for k, lst in dma_keys.items():
                    eh.wait_ge(dsem[k], 16 * len(lst))

        @block.tensor
        def _(e):
            run_stream("pe", e)

        @block.scalar
        def _(e):
            run_stream("act", e)

        @block.vector
        def _(e):
            run_stream("dve", e)

        @block.gpsimd
        def _(e):
            run_stream("pool", e)

        @block.sync
        def _(e):
            run_stream("sp", e)


D = 1024
T = 1024
NCH = 8
SEG = 256
NSEG = 4
N_IN = 9216
DFF = 4096
EPS = 1e-6
NW = 3
TWO_PI = 6.283185307179586

ROWS = [("b_in", 72), ("rcw", 32), ("rcb", 8), ("rgb", 32), ("lam", 16), ("sgg", 8), ("cfw", 248),
        ("cfb", 8), ("clg", 8), ("clb", 8), ("gn", 32), ("fcw", 192), ("fcb", 64), ("bmod", 48)]
COL = {}
_o = 0
for _n, _c in ROWS:
    COL[_n] = _o
    _o += _c
NROWS = _o
NBLK = (NROWS + 127) // 128
_o = NBLK * 128
for _n, _c in [("mod", 48), ("gs1", 8), ("gg1", 8), ("gs2", 8), ("gg2", 8), ("clh", 16), ("cl1", 16),
               ("w0f", 64), ("w2f", 64), ("hb", 32), ("tmp", 16), ("rcf", 32)]:
    COL[_n] = _o
    _o += _c
NPRM = _o

WNAMES = ["w_mod", "b_mod", "g_norm", "w_in", "b_in", "rnn_conv_w", "rnn_conv_b", "rg_w", "rg_b",
          "rg_lambda", "sg_norm_g", "sg_w", "sg_b", "cf_conv_w", "cf_conv_b", "cf_ln_g", "cf_ln_b",
          "w_branch", "w_out", "ffn_up", "ffn_conv_w", "ffn_conv_b", "ffn_down"]
WSHAPES = {"w_mod": [12, 128, 8 * 512], "b_mod": [6 * D], "g_norm": [4, D], "w_in": [D, N_IN], "b_in": [N_IN],
           "rnn_conv_w": [4, D], "rnn_conv_b": [D], "rg_w": [2, 2, 16, 64, 64], "rg_b": [2, 2, D],
           "rg_lambda": [2, D], "sg_norm_g": [D], "sg_w": [8, 128, 128], "sg_b": [8, 128],
           "cf_conv_w": [31, D], "cf_conv_b": [D], "cf_ln_g": [D], "cf_ln_b": [D],
           "w_branch": [3, D, D], "w_out": [D, D], "ffn_up": [D, 2 * DFF], "ffn_conv_w": [3, 2 * DFF],
           "ffn_conv_b": [2 * DFF], "ffn_down": [NCH, 128, 32 * 128]}


class WStream:
    def __init__(self, P, slots, plan):
        self.P = P
        self.slots = slots
        self.plan = plan
        self.rec = []
        self.i = 0
        self.issued = 0
        self.done = set()
        self.done_upto = -1

    def acquire(self, pieces):
        idx = self.i
        self.i += 1
        self.rec.append(pieces)
        if self.plan is not None:
            self._pump()
            assert self.issued > idx, "weight ring stalled: release earlier units first"
        return idx, self.slots[idx % NW]

    def release(self, idx):
        self.done.add(idx)
        while (self.done_upto + 1) in self.done:
            self.done_upto += 1
        if self.plan is not None:
            self._pump()

    def _pump(self):
        while self.issued < len(self.plan) and self.issued - NW <= self.done_upto:
            j = self.issued
            slot = self.slots[j % NW]
            for dstf, src in self.plan[j]:
                dst = dstf(slot)
                self.P.add("pool", (lambda d, s: lambda e: e.dma_start(out=d, in_=s))(dst, src),
                           writes=[dst], dma="w%d" % (j % NW))
            self.issued += 1


class _DryProg:
    def add(self, *a, **k):
        return None


def build_nc(depth=4, stop_after=None, dbg=()):
    nc = bass.Bass("TRN2", target_bir_lowering=False)
    from contextlib import ExitStack
    Lr = depth
    din = {}
    din["x"] = nc.dram_tensor("x", [T, D], F32, kind="ExternalInput").ap()
    din["cond"] = nc.dram_tensor("cond", [8, 128], F32, kind="ExternalInput").ap()
    din["flag"] = nc.dram_tensor("flag", [128, 1], F32, kind="ExternalInput").ap()
    din["h0"] = nc.dram_tensor("h0", [Lr * 16, 128], F32, kind="ExternalInput").ap()
    for n in WNAMES:
        din[n] = nc.dram_tensor(n, [Lr] + WSHAPES[n], F32, kind="ExternalInput").ap()
    y_out = nc.dram_tensor("y", [T, D], F32, kind="ExternalOutput").ap()
    st_out = nc.dram_tensor("st", [NSEG * Lr * 16, 128], F32, kind="ExternalOutput").ap()
    dbg_out = {n: nc.dram_tensor("dbg_" + n, [128, NCH * T], F32, kind="ExternalOutput").ap() for n in dbg}

    with ExitStack() as ctx:
        def sb(name, shape, dt=F32):
            return ctx.enter_context(nc.sbuf_tensor(name, shape, dt))

        xT = sb("xT", [128, NCH, T])
        prm = sb("prm", [128, NPRM])
        ident_f = sb("ident_f", [128, 128])
        ident_b = sb("ident_b", [128, 128], BF16)
        ones_b = sb("ones_b", [128, 128], BF16)
        one11 = sb("one11", [1, 1])
        nhalf = sb("nhalf", [128, 1])
        flag = sb("flag_sb", [128, 1])
        condT = sb("condT", [128, 8])
        scb = sb("scb", [128, 8], BF16)
        h0T = sb("h0T", [128, Lr * 16])
        ST = sb("ST", [128, NSEG, Lr, 2, 8])
        rgw = sb("rgw", [128, NCH, 4, 128], BF16)
        sgwT = sb("sgwT", [128, 8, 128], BF16)
        bsg = sb("bsg", [128, 8, 128])
        small = sb("small", [128, 32])
        gg2k = sb("gg2k", [128, 8])
        wslots = [sb("wslot%d" % i, [128, 4096], BF16) for i in range(NW)]
        AR = sb("AR", [128, 24576])
        ARb = AR[:].bitcast(BF16)
        def bfv(off_b, n1, n2):
            return ARb[:, off_b // 2: off_b // 2 + n1 * n2].rearrange("p (a b) -> p a b", b=n2)
        def f32v(off_b, n1, n2):
            return AR[:, off_b // 4: off_b // 4 + n1 * n2].rearrange("p (a b) -> p a b", b=n2)
        K16 = 16384
        h_b = bfv(0, NCH, T)
        big32 = f32v(K16, NCH, T)
        ya = bfv(3 * K16, NCH, T)
        yb = bfv(4 * K16, NCH, T)
        yc = bfv(5 * K16, NCH, T)
        svn = bfv(5 * K16, 8, 1024)
        mg = ya
        f_b = bfv(0, 32, T)
        h2_b = bfv(4 * K16, NCH, T)
        fo32 = f32v(4 * K16, NCH, T)
        TA = sb("TA", [128, T]); TB = sb("TB", [128, T]); TC = sb("TC", [128, T]); TD = sb("TD", [128, T])
        TR = sb("TR", [128, T])
        ones_f = sb("ones_f", [128, 128])
        TE = TR
        TM = TC
        zpf = sb("zpf", [128, NSEG * 260])
        zp = zpf[:].rearrange("p (s t) -> p s t", t=260)
        rows = zpf[:, 0:1024].rearrange("p (q c) -> p q c", c=128)
        brow = zpf[0:1, 0:1024]
        mrow = TC[0:1, 0:512]
        sqb = [sb("sqb%d" % i, [128, T], BF16) for i in range(2)]
        xcb = sqb[0]
        ycp = [sb("ycp%d" % i, [128, NSEG, 286], BF16) for i in range(2)]
        NDG = 4
        dg = [sb("dg%d" % i, [128, 128], BF16) for i in range(NDG)]
        PS = [ctx.enter_context(nc.psum_tensor("ps%d" % i, [128, T], F32)) for i in range(3)]
        PSX = ctx.enter_context(nc.psum_tensor("psx", [128, T], F32))
        PSS = [0]

        def emit_all(P, W):
            PSS[0] = 0
            sqi = [0]
            dgi = [0]

            ring4 = [False]

            def psn():
                if ring4[0]:
                    t = (PS + [PSX])[PSS[0] % 4]
                else:
                    t = PS[PSS[0] % 3]
                PSS[0] += 1
                return t

            def aps(*a):
                return [v for v in a if v is not None and not isinstance(v, (int, float))]

            def mm(out, lhsT, rhs, start, stop):
                P.add("pe", lambda e: e.matmul(out, lhsT=lhsT, rhs=rhs, start=start, stop=stop),
                      reads=[lhsT, rhs], writes=[out])

            def tr(out, in_, ident):
                P.add("pe", lambda e: e.transpose(out=out, in_=in_, identity=ident), reads=[in_, ident], writes=[out])

            def act(out, in_, func, bias=None, scale=None):
                kw = {}
                if bias is not None:
                    kw["bias"] = bias
                if scale is not None:
                    kw["scale"] = scale
                P.add("act", lambda e: e.activation(out=out, in_=in_, func=func, **kw),
                      reads=aps(in_, bias, scale), writes=[out])

            def tt(out, in0, in1, op, eng="dve"):
                P.add(eng, lambda e: e.tensor_tensor(out=out, in0=in0, in1=in1, op=op), reads=[in0, in1], writes=[out])

            def ts(out, in0, s1, s2, op0, op1=None, eng="dve"):
                if op1 is None:
                    P.add(eng, lambda e: e.tensor_scalar(out=out, in0=in0, scalar1=s1, scalar2=None, op0=op0),
                          reads=aps(in0, s1), writes=[out])
                else:
                    P.add(eng, lambda e: e.tensor_scalar(out=out, in0=in0, scalar1=s1, scalar2=s2, op0=op0, op1=op1),
                          reads=aps(in0, s1, s2), writes=[out])

            def stt(out, in0, sc, in1, op0, op1):
                P.add("dve", lambda e: e.scalar_tensor_tensor(out=out, in0=in0, scalar=sc, in1=in1, op0=op0, op1=op1),
                      reads=aps(in0, sc, in1), writes=[out])

            def cp(out, in_, eng="dve"):
                P.add(eng, lambda e: e.tensor_copy(out=out, in_=in_), reads=[in_], writes=[out])

            def memset(t, v, eng="dve"):
                P.add(eng, lambda e: e.memset(t, v), writes=[t])

            def dma(out, in_, key, eng="sp", rd=None, wr=None):
                P.add(eng, lambda e: e.dma_start(out=out, in_=in_), reads=rd or [], writes=wr or [], dma=key)

            def dump(name, ap3):
                if name not in dbg_out:
                    return
                for c in range(NCH):
                    cp(TA[:], ap3[:, c, :])
                    dma(dbg_out[name][:, c * T:(c + 1) * T], TA[:], "out", rd=[TA[:]])

            def pc(name, i=0, n=1):
                return prm[:, COL[name] + i: COL[name] + i + n]

            def seg4(ap):
                return ap.rearrange("p (s t) -> p s t", t=SEG)

            def wacq(view, K, n):
                idx, slot = W.acquire([((lambda K=K, n=n: lambda s: s[:, 0:K * n].rearrange("p (k n) -> p k n", n=n))(), view)])
                return idx, slot[:, 0:K * n].rearrange("p (k n) -> p k n", n=n)

            def wacq2(pieces, K):
                pl = [((lambda o=o, n=v.shape[2]: lambda s: s[:, :].rearrange("p (k n) -> p k n", n=512)[:, 0:K, o:o + n])(), v)
                      for o, v in pieces]
                idx, slot = W.acquire(pl)
                return idx, slot[:, :].rearrange("p (k n) -> p k n", n=512)

            def pipeline(gens, lag, maxact=2):
                active = []
                pending = list(gens)
                while active or pending:
                    if pending and len(active) < maxact and (not active or active[-1][1] >= lag):
                        active.append([pending.pop(0), 0])
                    for a in list(active):
                        try:
                            next(a[0])
                            a[1] += 1
                        except StopIteration:
                            active.remove(a)

            def proj_cols(ps_t, wu, col0, src_b, nk=8):
                for hf_ in range(2):
                    for k in range(nk):
                        mm(ps_t[:, hf_ * 512:(hf_ + 1) * 512], wu[:, k, col0:col0 + 128],
                           src_b[:, k, hf_ * 512:(hf_ + 1) * 512], k == 0, k == nk - 1)

            def rstd_from(ps_t, dst):
                act(dst[:], ps_t[:], AF.Ln, bias=float(EPS), scale=1.0 / D)
                act(dst[:], dst[:], AF.Exp, scale=-0.5)

            memset(ident_f[:], 0.0, "pool")
            P.add("pool", lambda e: e.affine_select(out=ident_f[:], in_=ident_f[:], compare_op=ALU.not_equal, fill=1.0,
                                                    base=0, pattern=[[-1, 128]], channel_multiplier=1),
                  reads=[ident_f[:]], writes=[ident_f[:]])
            cp(ident_b[:], ident_f[:])
            memset(ones_b[:], 1.0)
            memset(ones_f[:], 1.0)
            memset(one11[:], 1.0)
            memset(nhalf[:], -0.5)
            for i in range(2):
                memset(ycp[i][:], 0.0)
            memset(rgw[:], 0.0, "pool")
            memset(ST[:], 0.0)
            dma(flag[:], din["flag"], "in", wr=[flag[:]])
            dma(rows[0:8, 0, :], din["cond"], "in", wr=[rows[0:8, 0, :]])
            p0 = psn()
            tr(p0[:, 0:8], rows[0:8, 0, :], ident_f[0:8, 0:8])
            cp(condT[:], p0[:, 0:8])
            act(scb[:], condT[:], AF.Silu)
            nh = Lr * 16
            dma(rows[0:nh, 1, :], din["h0"], "in", wr=[rows[0:nh, 1, :]])
            p0 = psn()
            tr(p0[:, 0:nh], rows[0:nh, 1, :], ident_f[0:nh, 0:nh])
            cp(h0T[:], p0[:, 0:nh])
            for tq in range(8):
                stg = TA if tq % 2 == 0 else TB
                dma(stg[:], din["x"][tq * 128:(tq + 1) * 128, :], "in", wr=[stg[:]])
                p0 = psn()
                for k in range(NCH):
                    tr(p0[:, k * 128:(k + 1) * 128], stg[:, k * 128:(k + 1) * 128], ident_f[:])
                dst = xT[:, :, tq * 128:(tq + 1) * 128]
                src = p0[:].rearrange("p (k t) -> p k t", t=128)
                if tq % 2 == 0:
                    act(dst, src, AF.Copy)
                else:
                    cp(dst, src)
            ji = small[:, 0:2].bitcast(mybir.dt.int32)
            P.add("pool", lambda e: e.iota(ji, pattern=[[128, 2]], base=0, channel_multiplier=1), writes=[small[:, 0:2]])
            cp(small[:, 2:4], ji)
            act(small[:, 4:6], small[:, 2:4], AF.Exp, scale=-float(np.log(10000.0)) / 256.0)
            ti = TC[:].bitcast(mybir.dt.int32)
            for kind in range(4):
                if kind % 2 == 0:
                    pat = [[1, 16], [0, 64]] if kind == 0 else [[0, 16], [1, 64]]
                    P.add("pool", (lambda pat: lambda e: e.iota(ti, pattern=pat, base=0, channel_multiplier=0))(pat),
                          writes=[TC[:]])
                    cp(TD[:], ti)
                shift = 0.0 if kind % 2 == 0 else float(np.pi / 2)
                for m in range(2):
                    k = kind * 2 + m
                    ts(TA[:], TD[:], small[:, 4 + m:5 + m], shift, ALU.mult, ALU.add)
                    ts(TB[:], TA[:], 1.0 / TWO_PI, 12582912.0, ALU.mult, ALU.add)
                    ts(TB[:], TB[:], -12582912.0, None, ALU.add)
                    stt(TA[:], TB[:], -TWO_PI, TA[:], ALU.mult, ALU.add)
                    ts(TA[:], TA[:], 3.14159, -3.14159, ALU.min, ALU.max)
                    act(TB[:], TA[:], AF.Sin)
                    stt(xT[:, k, :], TB[:], flag[:], xT[:, k, :], ALU.mult, ALU.add)

            dump("xT0", xT)
            def layer_params(l):
                r0 = 0
                srcs = [din["b_in"][l].rearrange("(r c) -> r c", c=128),
                        din["rnn_conv_w"][l].rearrange("k (r c) -> (k r) c", c=128),
                        din["rnn_conv_b"][l].rearrange("(r c) -> r c", c=128),
                        din["rg_b"][l].rearrange("d k (r c) -> (d k r) c", c=128),
                        din["rg_lambda"][l].rearrange("d (r c) -> (d r) c", c=128),
                        din["sg_norm_g"][l].rearrange("(r c) -> r c", c=128),
                        din["cf_conv_w"][l].rearrange("k (r c) -> (k r) c", c=128),
                        din["cf_conv_b"][l].rearrange("(r c) -> r c", c=128),
                        din["cf_ln_g"][l].rearrange("(r c) -> r c", c=128),
                        din["cf_ln_b"][l].rearrange("(r c) -> r c", c=128),
                        din["g_norm"][l].rearrange("k (r c) -> (k r) c", c=128),
                        din["ffn_conv_w"][l].rearrange("k (r c) -> (k r) c", c=128),
                        din["ffn_conv_b"][l].rearrange("(r c) -> r c", c=128),
                        din["b_mod"][l].rearrange("(r c) -> r c", c=128)]
                for (nm, cnt), src in zip(ROWS, srcs):
                    done = 0
                    while done < cnt:
                        g = r0 + done
                        q, r = g // 128, g % 128
                        n = min(cnt - done, 128 - r)
                        dma(rows[r:r + n, q, :], src[done:done + n, :], "in", wr=[rows[r:r + n, q, :]])
                        done += n
                    r0 += cnt
                p0 = psn()
                for q in range(NBLK):
                    nr = min(128, NROWS - q * 128)
                    tr(p0[:, q * 128:q * 128 + nr], rows[0:nr, q, :], ident_f[0:nr, 0:nr])
                cp(prm[:, 0:NROWS], p0[:, 0:NROWS])
                for dk in range(4):
                    for hh in range(2):
                        src = din["rg_w"][l, dk // 2, dk % 2].rearrange("(c two) i j -> two i c j", two=2)[hh]
                        dst = rgw[hh * 64:(hh + 1) * 64, :, dk, hh * 64:(hh + 1) * 64]
                        dma(dst, src, "rgw", eng="pool", wr=[dst])
                sgv = TD[:].rearrange("p (g q) -> p g q", q=128)
                dma(sgv, din["sg_w"][l].rearrange("g p q -> p g q"), "in", wr=[TD[:]])
                p0 = psn()
                for g in range(8):
                    tr(p0[:, g * 128:(g + 1) * 128], sgv[:, g, :], ident_f[:])
                cp(sgwT[:].rearrange("p g q -> p (g q)"), p0[:])
                dma(brow[:], din["sg_b"][l].rearrange("g p -> (g p)").rearrange("(o n) -> o n", o=1), "in", wr=[brow[:]])
                p0 = psn()
                for hf_ in range(2):
                    mm(p0[:, hf_ * 512:(hf_ + 1) * 512], ones_f[0:1, :], brow[0:1, hf_ * 512:(hf_ + 1) * 512], True, True)
                cp(bsg[:].rearrange("p g q -> p (g q)"), p0[:])
                act(pc("tmp", 0, 16), pc("lam", 0, 16), AF.Exp, scale=-1.0)
                act(pc("tmp", 0, 16), pc("tmp", 0, 16), AF.Ln, bias=1.0)
                ts(pc("cl1", 0, 16), pc("tmp", 0, 16), -8.0, None, ALU.mult)
                ts(pc("clh", 0, 16), pc("tmp", 0, 16), -4.0, None, ALU.mult)
                ts(pc("hb", 0, 32), pc("rgb", 0, 32), 0.5, None, ALU.mult)
                ts(pc("w0f", 0, 64), pc("fcw", 0, 64), flag[:], None, ALU.mult)
                ts(pc("w2f", 0, 64), pc("fcw", 128, 64), flag[:], None, ALU.mult)
                ts(pc("rcf", 0, 32), pc("rcw", 0, 32), flag[:], None, ALU.mult)

            def layer_mod_gen(l):
                wv = din["w_mod"][l]
                for ug in range(12):
                    ui, u = wacq(wv[ug].rearrange("p (k n) -> p k n", n=512), 8, 512)
                    pm = psn()
                    for q in range(4):
                        for k in range(8):
                            mm(pm[:, q:q + 1], u[:, k, q * 128:(q + 1) * 128], scb[:, k:k + 1], k == 0, k == 7)
                    W.release(ui)
                    tt(pc("mod", ug * 4, 4), pm[:, 0:4], pc("bmod", ug * 4, 4), ALU.add)
                    yield
                stt(pc("gs1", 0, 8), pc("mod", 8, 8), 1.0, pc("gn", 0, 8), ALU.add, ALU.mult)
                tt(pc("gg1", 0, 8), pc("mod", 16, 8), pc("gn", 8, 8), ALU.mult)
                stt(pc("gs2", 0, 8), pc("mod", 32, 8), 1.0, pc("gn", 16, 8), ALU.add, ALU.mult)
                tt(pc("gg2", 0, 8), pc("mod", 40, 8), pc("gn", 24, 8), ALU.mult)
                yield

            def x_stats():
                pS = PSX
                for c in range(NCH):
                    q = sqb[sqi[0] % 2]
                    sqi[0] += 1
                    act(q[:], xT[:, c, :], AF.Square)
                    for hf_ in range(2):
                        mm(pS[:, hf_ * 512:(hf_ + 1) * 512], ones_b[:], q[:, hf_ * 512:(hf_ + 1) * 512], c == 0, c == NCH - 1)
                return pS

            def modnorm(dst_b, gsn, shoff):
                pS = x_stats()
                rstd_from(pS, TR)
                for c in range(NCH):
                    tmp = TA if c % 2 == 0 else TB
                    tt(tmp[:], xT[:, c, :], TR[:], ALU.mult)
                    act(dst_b[:, c, :], tmp[:], AF.Identity, bias=pc("mod", shoff + c), scale=pc(gsn, c))

            def proj_chunk(ps_t, wu, cc, src_b, nk=8):
                for hf_ in range(2):
                    for k in range(nk):
                        mm(ps_t[:, hf_ * 512:(hf_ + 1) * 512], wu[:, k, cc * 128:(cc + 1) * 128],
                           src_b[:, k, hf_ * 512:(hf_ + 1) * 512], k == 0, k == nk - 1)

            def residual_update(src32, ggn):
                for j in range(NCH):
                    tmp = TA if j % 2 == 0 else TB
                    tt(tmp[:], src32[:, j, :], TR[:], ALU.mult)
                    stt(xT[:, j, :], tmp[:], pc(ggn, j), xT[:, j, :], ALU.mult, ALU.add)

            layer_params(0)
            for _ in layer_mod_gen(0):
                pass
            for l in range(Lr):
                wi = din["w_in"][l].rearrange("(k p) n -> p k n", p=128)

                def win_unit(u):
                    return wacq(wi[:, :, u * 512:(u + 1) * 512], 8, 512)

                modnorm(h_b, "gs1", 0)

                dump("h", h_b)
                arB = f32v(4 * K16, NCH, T)
                SA = dict(z=TD[:], xc=(arB[:, 0, :], arB[:, 1, :]), r=(arB[:, 2, :], arB[:, 5, :]),
                          i=(arB[:, 3, :], arB[:, 6, :]), q=(arB[:, 4, :], arB[:, 7, :]), hf=TR[:],
                          xcb=(sqb[0][:], sqb[1][:]))
                unitsA = {}
                unitsC = {}
                ycc = big32

                def A_unit(c):
                    u, cc2 = c // 2, c % 2
                    if cc2 == 0:
                        unitsA[u] = wacq2([(0, wi[:, :, 2 * u * 128:(2 * u + 2) * 128]),
                                           (256, wi[:, :, 1024 + 2 * u * 128:1024 + (2 * u + 2) * 128])], 8)
                    return unitsA[u]

                def C_unit(c):
                    u, cc2 = c // 2, c % 2
                    if cc2 == 0:
                        unitsC[u] = wacq2([(0, wi[:, :, 4096 + 2 * u * 128:4096 + (2 * u + 2) * 128]),
                                           (256, wi[:, :, 5120 + 2 * u * 128:5120 + (2 * u + 2) * 128])], 8)
                    return unitsC[u]

                def C_front(c):
                    iu, wu = C_unit(c)
                    pg = psn()
                    proj_cols(pg, wu, 256 + (c % 2) * 128, h_b)
                    pa_ = psn()
                    proj_cols(pa_, wu, (c % 2) * 128, h_b)
                    if c % 2 == 1:
                        W.release(iu)
                    act(TA[:], pg[:], AF.Sigmoid, bias=pc("b_in", 40 + c))
                    yp = ycp[c % 2]
                    stt(yp[:, :, 15:271], seg4(pa_[:]), pc("b_in", 32 + c), seg4(TA[:]), ALU.add, ALU.mult)
                    ts(yp[:, 1:4, 0:15], yp[:, 0:3, 256:271], flag[:], None, ALU.mult)
                    ts(yp[:, 0:3, 271:286], yp[:, 1:4, 15:30], flag[:], None, ALU.mult)

                def C_conv(c):
                    yp = ycp[c % 2]
                    pcv = psn()
                    for k in range(31):
                        dgt = dg[dgi[0] % NDG]
                        dgi[0] += 1
                        ts(dgt[:], ident_b[:], pc("cfw", k * 8 + c), None, ALU.mult)
                        for hf_ in range(2):
                            mm(pcv[:, hf_ * 512:(hf_ + 1) * 512].rearrange("p (s t) -> p s t", t=SEG), dgt[:],
                               yp[:, 2 * hf_:2 * hf_ + 2, k:k + 256], k == 0, k == 30)
                    act(ycc[:, c, :], pcv[:], AF.Identity, bias=pc("cfb", c))
                    if c == 0:
                        act(TC[:], pcv[:], AF.Square, bias=pc("cfb", c))
                        cp(TB[:], ycc[:, c, :])
                    else:
                        act(TA[:], pcv[:], AF.Square, bias=pc("cfb", c))
                        tt(TB[:], TB[:], ycc[:, c, :], ALU.add)
                        tt(TC[:], TC[:], TA[:], ALU.add)

                def A_front_a(c):
                    iu, wu = A_unit(c)
                    p1 = psn()
                    proj_cols(p1, wu, (c % 2) * 128, h_b)
                    act(SA["z"], p1[:], AF.Identity, bias=pc("b_in", c))

                def A_front_b(c):
                    z, xc = SA["z"], SA["xc"][c % 2]
                    z4, xc4 = seg4(z), seg4(xc)
                    ts(xc, z, pc("rcw", 16 + c), pc("rcb", c), ALU.mult, ALU.add)
                    stt(xc4[:, :, 2:256], z4[:, :, 0:254], pc("rcw", c), xc4[:, :, 2:256], ALU.mult, ALU.add)
                    stt(xc4[:, 1:4, 0:2], z4[:, 0:3, 254:256], pc("rcf", c), xc4[:, 1:4, 0:2], ALU.mult, ALU.add)
                    stt(xc4[:, :, 1:256], z4[:, :, 0:255], pc("rcw", 8 + c), xc4[:, :, 1:256], ALU.mult, ALU.add)
                    stt(xc4[:, 1:4, 0:1], z4[:, 0:3, 255:256], pc("rcf", 8 + c), xc4[:, 1:4, 0:1], ALU.mult, ALU.add)
                    stt(xc4[:, :, 0:255], z4[:, :, 1:256], pc("rcw", 24 + c), xc4[:, :, 0:255], ALU.mult, ALU.add)
                    stt(xc4[:, 0:3, 255:256], z4[:, 1:4, 0:1], pc("rcf", 24 + c), xc4[:, 0:3, 255:256], ALU.mult, ALU.add)
                    cp(SA["xcb"][c % 2], xc)

                def A_gates(c):
                    xcb_ = SA["xcb"][c % 2]
                    for d in range(2):
                        pr_ = psn()
                        pi_ = psn()
                        for hf_ in range(2):
                            sl = slice(hf_ * 512, (hf_ + 1) * 512)
                            mm(pr_[:, sl], rgw[:, c, d * 2 + 0, :], xcb_[:, sl], True, True)
                            mm(pi_[:, sl], rgw[:, c, d * 2 + 1, :], xcb_[:, sl], True, True)
                        act(SA["r"][d], pr_[:], AF.Tanh, bias=pc("hb", (d * 2 + 0) * 8 + c), scale=0.5)
                        act(SA["i"][d], pi_[:], AF.Tanh, bias=pc("hb", (d * 2 + 1) * 8 + c), scale=0.5)
                    for d in range(2):
                        act(SA["q"][d], SA["r"][d], AF.Exp, bias=pc("cl1", d * 8 + c), scale=pc("cl1", d * 8 + c))
                        act(SA["r"][d], SA["r"][d], AF.Exp, bias=pc("clh", d * 8 + c), scale=pc("clh", d * 8 + c))
                    for d in range(2):
                        act(SA["q"][d], SA["q"][d], AF.Sqrt, bias=1.0, scale=-1.0)

                def A_scan(c):
                    xc, hf = SA["xc"][c % 2], SA["hf"]
                    for d in range(2):
                        a_, u_, s_ = SA["r"][d], SA["i"][d], SA["q"][d]
                        stt(u_, u_, 1.0, s_, ALU.add, ALU.mult)
                        stt(u_, u_, 0.5, xc, ALU.mult, ALU.mult)
                        col = (l * 2 + d) * 8 + c
                        if d == 0:
                            bnd = a_[:, 256:1024:256]
                            ts(bnd, bnd, flag[:], None, ALU.mult)
                            P.add("dve", (lambda a_=a_, u_=u_, hf=hf, col=col: lambda e: e.tensor_tensor_scan(
                                out=hf, data0=a_, data1=u_, initial=h0T[:, col:col + 1], op0=ALU.mult, op1=ALU.add))(),
                                reads=[a_, u_, h0T[:]], writes=[hf])
                            cp(ST[:, :, l, 0, c], hf[:, 255:1024:256])
                        else:
                            bnd = a_[:, 255:1023:256]
                            ts(bnd, bnd, flag[:], None, ALU.mult)
                            P.add("dve", (lambda a_=a_, u_=u_, s_=s_, col=col: lambda e: e.tensor_tensor_scan(
                                out=s_[:, ::-1], data0=a_[:, ::-1], data1=u_[:, ::-1], initial=h0T[:, col:col + 1],
                                op0=ALU.mult, op1=ALU.add))(),
                                reads=[a_, u_, h0T[:]], writes=[s_])
                            cp(ST[:, :, l, 1, c], s_[:, 0:1024:256])
                            tt(hf, hf, s_, ALU.add)

                def A_out(c):
                    iu, wu = unitsA[c // 2]
                    p2 = psn()
                    proj_cols(p2, wu, 256 + (c % 2) * 128, h_b)
                    if c % 2 == 1:
                        W.release(iu)
                    act(SA["r"][0], p2[:], AF.Gelu_apprx_tanh, bias=pc("b_in", 8 + c))
                    tt(ya[:, c, :], SA["hf"], SA["r"][0], ALU.mult)

                ring4[0] = True
                C_front(0)
                A_front_a(0)
                C_conv(0)
                A_front_b(0)
                for c in range(NCH):
                    if c + 1 < NCH:
                        C_front(c + 1)
                        A_front_a(c + 1)
                    A_gates(c)
                    if c + 1 < NCH:
                        C_conv(c + 1)
                        A_front_b(c + 1)
                    A_scan(c)
                    A_out(c)
                dump("ya", ya)

                ring4[0] = False
                pS1 = PSX
                pS2 = psn()
                for hf_ in range(2):
                    sl = slice(hf_ * 512, (hf_ + 1) * 512)
                    mm(pS1[:, sl], ones_f[:], TB[:, sl], True, True)
                    mm(pS2[:, sl], ones_f[:], TC[:, sl], True, True)
                act(TM[:], pS1[:], AF.Identity, scale=1.0 / D)
                act(TD[:], pS1[:], AF.Square, scale=1.0 / D)
                stt(TR[:], pS2[:], 1.0 / D, TD[:], ALU.mult, ALU.subtract)
                act(TR[:], TR[:], AF.Ln, bias=float(EPS))
                act(TR[:], TR[:], AF.Exp, scale=-0.5)
                stt(TM[:], TM[:], -1.0, TR[:], ALU.mult, ALU.mult)

                def ctail_gen():
                    for c in range(NCH):
                        tt(TD[:], ycc[:, c, :], TR[:], ALU.mult)
                        tt(TD[:], TD[:], TM[:], ALU.add)
                        act(yc[:, c, :], TD[:], AF.Silu, bias=pc("clb", c), scale=pc("clg", c))
                        yield

                ring4[0] = True
                svn2 = bfv(K16, 8, 1024)
                dma(brow[:], din["b_in"][l, 3072:4096].rearrange("(o n) -> o n", o=1), "in", wr=[brow[:]])
                iv0, wv0 = win_unit(6)
                iv1, wv1 = win_unit(7)
                def bsv_gen():
                    for tb in range(4):
                        for tq in (2 * tb, 2 * tb + 1):
                            p1 = psn()
                            for hh, wv in enumerate((wv0, wv1)):
                                sl = slice(hh * 512, (hh + 1) * 512)
                                for k in range(8):
                                    mm(p1[:, sl], h_b[:, k, tq * 128:(tq + 1) * 128], wv[:, k, :], k == 0, False)
                                mm(p1[:, sl], ones_f[0:1, :], brow[0:1, sl], False, True)
                            svt = TA if tq % 2 == 0 else TB
                            act(svt[:], p1[:], AF.Gelu_apprx_tanh)
                            so = (tq % 2) * 16
                            P.add("dve", (lambda svt, so: lambda e: e.bn_stats(out=small[:, so:so + 6], in_=svt[:, 0:512]))(svt, so),
                                  reads=[svt[:, 0:512]], writes=[small[:, so:so + 6]])
                            P.add("dve", (lambda svt, so: lambda e: e.bn_stats(out=small[:, so + 6:so + 12], in_=svt[:, 512:1024]))(svt, so),
                                  reads=[svt[:, 512:1024]], writes=[small[:, so + 6:so + 12]])
                            P.add("dve", (lambda so: lambda e: e.bn_aggr(out=small[:, so + 12:so + 14], in_=small[:, so:so + 12]))(so),
                                  reads=[small[:, so:so + 12]], writes=[small[:, so + 12:so + 14]])
                        for tq in (2 * tb, 2 * tb + 1):
                            so = (tq % 2) * 16
                            act(small[:, so + 13:so + 14], small[:, so + 13:so + 14], AF.Sqrt, bias=float(EPS))
                        for tq in (2 * tb, 2 * tb + 1):
                            svt = TA if tq % 2 == 0 else TB
                            so = (tq % 2) * 16
                            P.add("dve", (lambda so: lambda e: e.reciprocal(out=small[:, so + 13:so + 14], in_=small[:, so + 13:so + 14]))(so),
                                  reads=[small[:, so + 13:so + 14]], writes=[small[:, so + 13:so + 14]])
                            stt(small[:, so + 14:so + 15], small[:, so + 12:so + 13], -1.0, small[:, so + 13:so + 14], ALU.mult, ALU.mult)
                            ts(svn2[:, tq, :], svt[:], small[:, so + 13:so + 14], small[:, so + 14:so + 15], ALU.mult, ALU.add)
                        yield

                ring4[0] = True
                pipeline([ctail_gen(), bsv_gen()], lag=1)
                dump("yc", yc)
                W.release(iv0)
                W.release(iv1)
                for ug in range(2):
                    iu, wu = win_unit(4 + ug)
                    for cc in range(4):
                        g = ug * 4 + cc
                        p2 = psn()
                        proj_chunk(p2, wu, cc, h_b)
                        act(TB[:], p2[:], AF.Gelu_apprx_tanh, bias=pc("b_in", 16 + g))
                        p1 = psn()
                        for n in range(8):
                            mm(p1[:, n * 128:(n + 1) * 128], svn2[:, n, g * 128:(g + 1) * 128], sgwT[:, g, :], True, True)
                        stt(TA[:].rearrange("p (n q) -> p n q", q=128), p1[:].rearrange("p (n q) -> p n q", q=128),
                            pc("sgg", g), bsg[:, g, :].unsqueeze(1).broadcast_to([128, 8, 128]), ALU.mult, ALU.add)
                        tt(yb[:, g, :], TA[:], TB[:], ALU.mult)
                    W.release(iu)
                dump("yb", yb)
                macc = big32
                ysrc = (ya, yb, yc)
                for br in range(3):
                    wbv = din["w_branch"][l, br].rearrange("(k p) n -> p k n", p=128)
                    for jp in range(4):
                        gc0 = 6144 + br * 1024 + jp * 256
                        imu, wmu = wacq2([(0, wi[:, :, gc0:gc0 + 256]), (256, wbv[:, :, jp * 256:(jp + 1) * 256])], 8)
                        for cc in range(2):
                            j = jp * 2 + cc
                            pg = psn()
                            proj_cols(pg, wmu, cc * 128, h_b)
                            pp = psn()
                            proj_cols(pp, wmu, 256 + cc * 128, ysrc[br])
                            sg_ = TA if j % 2 == 0 else TB
                            act(sg_[:], pg[:], AF.Sigmoid, bias=pc("b_in", 48 + br * 8 + j))
                            if br == 0:
                                tt(macc[:, j, :], sg_[:], pp[:], ALU.mult)
                            elif br == 1:
                                tt(sg_[:], sg_[:], pp[:], ALU.mult)
                                tt(macc[:, j, :], macc[:, j, :], sg_[:], ALU.add)
                            else:
                                tt(sg_[:], sg_[:], pp[:], ALU.mult)
                                tt(mg[:, j, :], macc[:, j, :], sg_[:], ALU.add)
                        W.release(imu)

                dump("mg", mg)
                ring4[0] = False
                of32 = big32
                wov = din["w_out"][l].rearrange("(k p) n -> p k n", p=128)
                pS = PSX
                for ug in range(2):
                    io, wo = wacq(wov[:, :, ug * 512:(ug + 1) * 512], 8, 512)
                    for cc in range(4):
                        j = ug * 4 + cc
                        p1 = psn()
                        proj_chunk(p1, wo, cc, mg)
                        act(of32[:, j, :], p1[:], AF.Copy)
                        q = sqb[sqi[0] % 2]
                        sqi[0] += 1
                        act(q[:], p1[:], AF.Square)
                        for hf_ in range(2):
                            sl = slice(hf_ * 512, (hf_ + 1) * 512)
                            mm(pS[:, sl], ones_b[:], q[:, sl], j == 0, j == 7)
                    W.release(io)
                rstd_from(pS, TR)
                residual_update(of32, "gg1")

                if stop_after == "s1":
                    break

                modnorm(h2_b, "gs2", 24)
                wup = din["ffn_up"][l].rearrange("(k p) n -> p k n", p=128)
                ring4[0] = True
                for ug in range(16):
                    ifu, wfu = wacq2([(0, wup[:, :, ug * 256:(ug + 1) * 256]),
                                      (256, wup[:, :, DFF + ug * 256: DFF + (ug + 1) * 256])], 8)
                    for cc in range(2):
                        i = ug * 2 + cc
                        Ys = []
                        for side in (0, 1):
                            cch = side * 32 + i
                            p1 = psn()
                            proj_cols(p1, wfu, side * 256 + cc * 128, h2_b)
                            Y = (TA, TB)[side] if i % 2 == 0 else (TC, TD)[side]
                            Ys.append(Y)
                            act(Y[:], p1[:], AF.Identity, bias=pc("fcb", cch), scale=pc("fcw", 64 + cch))
                            Y4 = seg4(Y[:])
                            p4 = seg4(p1[:])
                            stt(Y4[:, :, 1:256], p4[:, :, 0:255], pc("fcw", cch), Y4[:, :, 1:256], ALU.mult, ALU.add)
                            stt(Y4[:, :, 0:255], p4[:, :, 1:256], pc("fcw", 128 + cch), Y4[:, :, 0:255], ALU.mult, ALU.add)
                            stt(Y[:, 256:1024:256], p1[:, 255:1023:256], pc("w0f", cch), Y[:, 256:1024:256], ALU.mult, ALU.add)
                            stt(Y[:, 255:1023:256], p1[:, 256:1024:256], pc("w2f", cch), Y[:, 255:1023:256], ALU.mult, ALU.add)
                        act(Ys[0][:], Ys[0][:], AF.Gelu_apprx_tanh)
                        tt(f_b[:, i, :], Ys[0][:], Ys[1][:], ALU.mult, eng="pool")
                    W.release(ifu)
                ring4[0] = False
                cp(gg2k[:], pc("gg2", 0, 8))
                modg = None
                if l + 1 < Lr:
                    layer_params(l + 1)
                    modg = layer_mod_gen(l + 1)
                wdn = din["ffn_down"][l]
                pS = PSX
                nmod = 0
                for j in range(NCH):
                    while modg is not None and nmod < (13 * (j + 1) + 7) // 8:
                        next(modg, None)
                        nmod += 1
                    idn, wd = wacq(wdn[j].rearrange("p (k n) -> p k n", n=128), 32, 128)
                    p1 = psn()
                    proj_chunk(p1, wd, 0, f_b, nk=32)
                    W.release(idn)
                    act(fo32[:, j, :], p1[:], AF.Copy)
                    q = sqb[sqi[0] % 2]
                    sqi[0] += 1
                    act(q[:], p1[:], AF.Square)
                    for hf_ in range(2):
                        sl = slice(hf_ * 512, (hf_ + 1) * 512)
                        mm(pS[:, sl], ones_b[:], q[:, sl], j == 0, j == 7)
                rstd_from(pS, TR)
                for j in range(NCH):
                    tmp = TA if j % 2 == 0 else TB
                    tt(tmp[:], fo32[:, j, :], TR[:], ALU.mult)
                    stt(xT[:, j, :], tmp[:], gg2k[:, j:j + 1], xT[:, j, :], ALU.mult, ALU.add)

            for tq in range(8):
                p0 = psn()
                for k in range(NCH):
                    tr(p0[:, k * 128:(k + 1) * 128], xT[:, k, tq * 128:(tq + 1) * 128], ident_f[:])
                stg = TA if tq % 2 == 0 else TB
                if tq % 2 == 0:
                    act(stg[:], p0[:], AF.Copy)
                else:
                    cp(stg[:], p0[:])
                dma(y_out[tq * 128:(tq + 1) * 128, :], stg[:], "out", rd=[stg[:]])
            ncol = NSEG * Lr * 16
            stf = ST[:].rearrange("p s l d c -> p (s l d c)")
            nb = (ncol + 127) // 128
            p0 = psn()
            for q in range(nb):
                n = min(128, ncol - q * 128)
                tr(p0[0:n, q * 128:(q + 1) * 128], stf[:, q * 128:q * 128 + n], ident_f[:])
            for q in range(nb):
                n = min(128, ncol - q * 128)
                cp(TC[0:n, q * 128:(q + 1) * 128], p0[0:n, q * 128:(q + 1) * 128])
                dma(st_out[q * 128:q * 128 + n, :], TC[0:n, q * 128:(q + 1) * 128], "out", rd=[TC[0:n, q * 128:(q + 1) * 128]])

        Wd = WStream(_DryProg(), wslots, None)
        emit_all(_DryProg(), Wd)
        P = Prog(nc)
        W = WStream(P, wslots, Wd.rec)
        emit_all(P, W)
        P.finalize(ctx)
    return nc


_NC_CACHE = {}


def _run(inputs, depth=4, stop_after=None, dbg=(), ret_raw=False):
    key = (depth, stop_after, tuple(dbg))
    if key not in _NC_CACHE:
        _NC_CACHE[key] = build_nc(depth, stop_after, dbg)
    nc = _NC_CACHE[key]
    f32 = lambda a: np.ascontiguousarray(np.asarray(a, dtype=np.float32))
    xp = f32(inputs["x_prompt"])
    xs = f32(inputs["x_sample"])
    st = f32(inputs["state_rglru"])
    c = f32(inputs["c"])
    cctx = f32(inputs["c_ctx"])
    wts = {n: f32(inputs[n])[:depth] for n in WNAMES}
    wts["ffn_down"] = np.ascontiguousarray(
        wts["ffn_down"].reshape(depth, 32, 128, NCH, 128).transpose(0, 3, 2, 1, 4)).reshape(depth, NCH, 128, 32 * 128)
    wts["w_mod"] = np.ascontiguousarray(
        wts["w_mod"].reshape(depth, 8, 128, 12, 512).transpose(0, 3, 2, 1, 4)).reshape(depth, 12, 128, 8 * 512)
    in_maps = []
    for core in range(8):
        m = dict(wts)
        if core < 4:
            m["x"] = np.ascontiguousarray(xp[4 * core:4 * core + 4].reshape(T, D))
            m["cond"] = cctx.reshape(8, 128)
            m["flag"] = np.zeros((128, 1), np.float32)
            m["h0"] = np.zeros((depth * 16, 128), np.float32)
        else:
            j = core - 4
            m["x"] = xs[j]
            m["cond"] = np.ascontiguousarray(c[j].reshape(8, 128))
            m["flag"] = np.ones((128, 1), np.float32)
            m["h0"] = np.ascontiguousarray(st[j, :depth].reshape(depth * 16, 128))
        in_maps.append(m)
    res = run_bass_kernel_spmd(nc, in_maps, core_ids=list(range(8)))
    outs = res.results
    if ret_raw:
        return outs
    y_prompt = np.stack([outs[i]["y"] for i in range(4)]).reshape(16, 256, D)
    y_sample = np.stack([outs[4 + i]["y"] for i in range(4)])
    new_state = np.concatenate([outs[i]["st"].reshape(NSEG, depth, 2, D) for i in range(4)], axis=0)
    return (y_prompt.astype(np.float32), y_sample.astype(np.float32), new_state.astype(np.float32))


def kernel(**inputs):
    return _run(inputs, depth=4)
```

```python
import bisect
import numpy as np
import concourse.bass as bass
import concourse.mybir as mybir
from concourse.bass_utils import run_bass_kernel_spmd

F32 = mybir.dt.float32
BF16 = mybir.dt.bfloat16
AF = mybir.ActivationFunctionType
ALU = mybir.AluOpType


class _Op:
    __slots__ = ("eng", "emit", "deps", "idx", "ticket", "dmakey", "needs_inc")


def _region(ap):
    t = ap.tensor
    dims = ap.ap
    pstride = dims[0][0]
    off = ap.offset
    if pstride > 0:
        p0 = off // pstride
        f0 = off % pstride
    else:
        p0 = 0
        f0 = off
    lo = f0
    hi = f0
    for st, cnt in dims[1:]:
        ext = st * (cnt - 1)
        if ext < 0:
            lo += ext
        else:
            hi += ext
    return (t.name, p0, p0 + dims[0][1], lo, hi + 1)


def _ovl(a, b):
    return a[1] < b[2] and b[1] < a[2] and a[3] < b[4] and b[3] < a[4]


def _covers(a, b):
    return a[1] <= b[1] and a[2] >= b[2] and a[3] <= b[3] and a[4] >= b[4]


class Prog:
    ENGS = ("pe", "act", "dve", "pool", "sp")

    def __init__(self, nc):
        self.nc = nc
        self.ops = []
        self.hist = {}
        self.dma_keys = {}

    def add(self, eng, emit, reads=(), writes=(), dma=None):
        op = _Op()
        op.eng = eng
        op.emit = emit
        op.idx = len(self.ops)
        op.dmakey = dma
        op.needs_inc = dma is not None
        op.ticket = None
        deps = set()
        rr = [_region(a) for a in reads]
        wr = [_region(a) for a in writes]
        for r in rr:
            h = self.hist.get(r[0])
            if h:
                for (reg, i, e, _d) in h[0]:
                    if _ovl(reg, r):
                        deps.add(i)
        for w in wr:
            h = self.hist.get(w[0])
            if h:
                for (reg, i, e, _d) in h[0]:
                    if _ovl(reg, w):
                        deps.add(i)
                for (reg, i, e, _d) in h[1]:
                    if _ovl(reg, w):
                        deps.add(i)
        for r in rr:
            h = self.hist.setdefault(r[0], [[], []])
            if dma is None:
                h[1] = [x for x in h[1] if not (x[2] == eng and not x[3] and _covers(r, x[0]))]
            h[1].append((r, op.idx, eng, dma is not None))
        for w in wr:
            h = self.hist.setdefault(w[0], [[], []])
            h[0] = [x for x in h[0] if not _covers(w, x[0])]
            h[1] = [x for x in h[1] if not _covers(w, x[0])]
            h[0].append((w, op.idx, eng, dma is not None))
        deps.discard(op.idx)
        op.deps = deps
        if dma is not None:
            self.dma_keys.setdefault(dma, []).append(op.idx)
        self.ops.append(op)
        return op

    def finalize(self, ctx):
        nc = self.nc
        ops = self.ops
        for op in ops:
            latest = {}
            for d in op.deps:
                p = ops[d]
                if p.dmakey is not None:
                    continue
                if p.eng == "pe" and op.eng == "pe":
                    continue
                if p.eng == "pool":
                    p.needs_inc = True
                    continue
                q = latest.get(p.eng)
                if q is None or q.idx < p.idx:
                    latest[p.eng] = p
            for p in latest.values():
                p.needs_inc = True
        cnt = {e: 0 for e in self.ENGS}
        for op in ops:
            if op.dmakey is None and op.needs_inc:
                cnt[op.eng] += 1
                op.ticket = cnt[op.eng]
        esem = {e: ctx.enter_context(nc.semaphore("sem_" + e)) for e in self.ENGS}
        dsem = {k: ctx.enter_context(nc.semaphore("dsem_" + str(k))) for k in self.dma_keys}
        block = ctx.enter_context(nc.Block())
        streams = {e: [o for o in ops if o.eng == e] for e in self.ENGS}
        dma_keys = self.dma_keys

        def run_stream(ename, eh):
            waited = {}
            for op in streams[ename]:
                need = {}
                for d in op.deps:
                    p = ops[d]
                    if p.dmakey is not None:
                        lst = dma_keys[p.dmakey]
                        val = 16 * bisect.bisect_left(lst, op.idx)
                        key = ("d", p.dmakey)
                    else:
                        if p.eng == "pe" and ename == "pe":
                            continue
                        if p.ticket is None:
                            continue
                        val = p.ticket
                        key = ("e", p.eng)
                    if need.get(key, 0) < val:
                        need[key] = val
                for key, val in need.items():
                    if waited.get(key, 0) >= val:
                        continue
                    waited[key] = val
                    s = dsem[key[1]] if key[0] == "d" else esem[key[1]]
                    eh.wait_ge(s, val)
                ins = op.emit(eh)
                if op.dmakey is not None:
                    ins.then_inc(dsem[op.dmakey], 16)
                elif op.needs_inc:
                    ins.then_inc(esem[ename], 1)
            if ename == "sp":
                for k, lst in dma_keys.items():
                    eh.wait_ge(dsem[k], 16 * len(lst))

        @block.tensor
        def _(e):
            run_stream("pe", e)

        @block.scalar
        def _(e):
            run_stream("act", e)

        @block.vector
        def _(e):
            run_stream("dve", e)

        @block.gpsimd
        def _(e):
            run_stream("pool", e)

        @block.sync
        def _(e):
            run_stream("sp", e)


D = 1024
T = 1024
NCH = 8
SEG = 256
NSEG = 4
N_IN = 9216
DFF = 4096
EPS = 1e-6
NW = 3
TWO_PI = 6.283185307179586

ROWS = [("b_in", 72), ("rcw", 32), ("rcb", 8), ("rgb", 32), ("lam", 16), ("sgg", 8), ("cfw", 248),
        ("cfb", 8), ("clg", 8), ("clb", 8), ("gn", 32), ("fcw", 192), ("fcb", 64), ("bmod", 48)]
COL = {}
_o = 0
for _n, _c in ROWS:
    COL[_n] = _o
    _o += _c
NROWS = _o
NBLK = (NROWS + 127) // 128
_o = NBLK * 128
for _n, _c in [("mod", 48), ("gs1", 8), ("gg1", 8), ("gs2", 8), ("gg2", 8), ("clh", 16), ("cl1", 16),
               ("w0f", 64), ("w2f", 64), ("hb", 32), ("tmp", 16), ("rcf", 32)]:
    COL[_n] = _o
    _o += _c
NPRM = _o

WNAMES = ["w_mod", "b_mod", "g_norm", "w_in", "b_in", "rnn_conv_w", "rnn_conv_b", "rg_w", "rg_b",
          "rg_lambda", "sg_norm_g", "sg_w", "sg_b", "cf_conv_w", "cf_conv_b", "cf_ln_g", "cf_ln_b",
          "w_branch", "w_out", "ffn_up", "ffn_conv_w", "ffn_conv_b", "ffn_down"]
WSHAPES = {"w_mod": [12, 128, 8 * 512], "b_mod": [6 * D], "g_norm": [4, D], "w_in": [D, N_IN], "b_in": [N_IN],
           "rnn_conv_w": [4, D], "rnn_conv_b": [D], "rg_w": [2, 2, 16, 64, 64], "rg_b": [2, 2, D],
           "rg_lambda": [2, D], "sg_norm_g": [D], "sg_w": [8, 128, 128], "sg_b": [8, 128],
           "cf_conv_w": [31, D], "cf_conv_b": [D], "cf_ln_g": [D], "cf_ln_b": [D],
           "w_branch": [3, D, D], "w_out": [D, D], "ffn_up": [D, 2 * DFF], "ffn_conv_w": [3, 2 * DFF],
           "ffn_conv_b": [2 * DFF], "ffn_down": [NCH, 128, 32 * 128]}


class WStream:
    def __init__(self, P, slots, plan):
        self.P = P
        self.slots = slots
        self.plan = plan
        self.rec = []
        self.i = 0
        self.issued = 0
        self.done = set()
        self.done_upto = -1

    def acquire(self, pieces):
        idx = self.i
        self.i += 1
        self.rec.append(pieces)
        if self.plan is not None:
            self._pump()
            assert self.issued > idx, "weight ring stalled: release earlier units first"
        return idx, self.slots[idx % NW]

    def release(self, idx):
        self.done.add(idx)
        while (self.done_upto + 1) in self.done:
            self.done_upto += 1
        if self.plan is not None:
            self._pump()

    def _pump(self):
        while self.issued < len(self.plan) and self.issued - NW <= self.done_upto:
            j = self.issued
            slot = self.slots[j % NW]
            for dstf, src in self.plan[j]:
                dst = dstf(slot)
                self.P.add("pool", (lambda d, s: lambda e: e.dma_start(out=d, in_=s))(dst, src),
                           writes=[dst], dma="w%d" % (j % NW))
            self.issued += 1


class _DryProg:
    def add(self, *a, **k):
        return None


def build_nc(depth=4, stop_after=None, dbg=()):
    nc = bass.Bass("TRN2", target_bir_lowering=False)
    from contextlib import ExitStack
    Lr = depth
    din = {}
    din["x"] = nc.dram_tensor("x", [T, D], F32, kind="ExternalInput").ap()
    din["cond"] = nc.dram_tensor("cond", [8, 128], F32, kind="ExternalInput").ap()
    din["flag"] = nc.dram_tensor("flag", [128, 1], F32, kind="ExternalInput").ap()
    din["h0"] = nc.dram_tensor("h0", [Lr * 16, 128], F32, kind="ExternalInput").ap()
    for n in WNAMES:
        din[n] = nc.dram_tensor(n, [Lr] + WSHAPES[n], F32, kind="ExternalInput").ap()
    y_out = nc.dram_tensor("y", [T, D], F32, kind="ExternalOutput").ap()
    st_out = nc.dram_tensor("st", [NSEG * Lr * 16, 128], F32, kind="ExternalOutput").ap()
    dbg_out = {n: nc.dram_tensor("dbg_" + n, [128, NCH * T], F32, kind="ExternalOutput").ap() for n in dbg}

    with ExitStack() as ctx:
        def sb(name, shape, dt=F32):
            return ctx.enter_context(nc.sbuf_tensor(name, shape, dt))

        xT = sb("xT", [128, NCH, T])
        prm = sb("prm", [128, NPRM])
        ident_f = sb("ident_f", [128, 128])
        ident_b = sb("ident_b", [128, 128], BF16)
        ones_b = sb("ones_b", [128, 128], BF16)
        one11 = sb("one11", [1, 1])
        nhalf = sb("nhalf", [128, 1])
        flag = sb("flag_sb", [128, 1])
        condT = sb("condT", [128, 8])
        scb = sb("scb", [128, 8], BF16)
        h0T = sb("h0T", [128, Lr * 16])
        ST = sb("ST", [128, NSEG, Lr, 2, 8])
        rgw = sb("rgw", [128, NCH, 4, 128], BF16)
        sgwT = sb("sgwT", [128, 8, 128], BF16)
        bsg = sb("bsg", [128, 8, 128])
        small = sb("small", [128, 32])
        gg2k = sb("gg2k", [128, 8])
        wslots = [sb("wslot%d" % i, [128, 4096], BF16) for i in range(NW)]
        AR = sb("AR", [128, 24576])
        ARb = AR[:].bitcast(BF16)
        def bfv(off_b, n1, n2):
            return ARb[:, off_b // 2: off_b // 2 + n1 * n2].rearrange("p (a b) -> p a b", b=n2)
        def f32v(off_b, n1, n2):
            return AR[:, off_b // 4: off_b // 4 + n1 * n2].rearrange("p (a b) -> p a b", b=n2)
        K16 = 16384
        h_b = bfv(0, NCH, T)
        big32 = f32v(K16, NCH, T)
        ya = bfv(3 * K16, NCH, T)
        yb = bfv(4 * K16, NCH, T)
        yc = bfv(5 * K16, NCH, T)
        svn = bfv(5 * K16, 8, 1024)
        mg = ya
        f_b = bfv(0, 32, T)
        h2_b = bfv(4 * K16, NCH, T)
        fo32 = f32v(4 * K16, NCH, T)
        TA = sb("TA", [128, T]); TB = sb("TB", [128, T]); TC = sb("TC", [128, T]); TD = sb("TD", [128, T])
        TR = sb("TR", [128, T])
        ones_f = sb("ones_f", [128, 128])
        TE = TR
        TM = TC
        zpf = sb("zpf", [128, NSEG * 260])
        zp = zpf[:].rearrange("p (s t) -> p s t", t=260)
        rows = zpf[:, 0:1024].rearrange("p (q c) -> p q c", c=128)
        brow = zpf[0:1, 0:1024]
        mrow = TC[0:1, 0:512]
        sqb = [sb("sqb%d" % i, [128, T], BF16) for i in range(2)]
        xcb = sqb[0]
        ycp = [sb("ycp%d" % i, [128, NSEG, 286], BF16) for i in range(2)]
        NDG = 4
        dg = [sb("dg%d" % i, [128, 128], BF16) for i in range(NDG)]
        PS = [ctx.enter_context(nc.psum_tensor("ps%d" % i, [128, T], F32)) for i in range(3)]
        PSX = ctx.enter_context(nc.psum_tensor("psx", [128, T], F32))
        PSS = [0]

        def emit_all(P, W):
            PSS[0] = 0
            sqi = [0]
            dgi = [0]

            ring4 = [False]

            def psn():
                if ring4[0]:
                    t = (PS + [PSX])[PSS[0] % 4]
                else:
                    t = PS[PSS[0] % 3]
                PSS[0] += 1
                return t

            def aps(*a):
                return [v for v in a if v is not None and not isinstance(v, (int, float))]

            def mm(out, lhsT, rhs, start, stop):
                P.add("pe", lambda e: e.matmul(out, lhsT=lhsT, rhs=rhs, start=start, stop=stop),
                      reads=[lhsT, rhs], writes=[out])

            def tr(out, in_, ident):
                P.add("pe", lambda e: e.transpose(out=out, in_=in_, identity=ident), reads=[in_, ident], writes=[out])

            def act(out, in_, func, bias=None, scale=None):
                kw = {}
                if bias is not None:
                    kw["bias"] = bias
                if scale is not None:
                    kw["scale"] = scale
                P.add("act", lambda e: e.activation(out=out, in_=in_, func=func, **kw),
                      reads=aps(in_, bias, scale), writes=[out])

            def tt(out, in0, in1, op, eng="dve"):
                P.add(eng, lambda e: e.tensor_tensor(out=out, in0=in0, in1=in1, op=op), reads=[in0, in1], writes=[out])

            def ts(out, in0, s1, s2, op0, op1=None, eng="dve"):
                if op1 is None:
                    P.add(eng, lambda e: e.tensor_scalar(out=out, in0=in0, scalar1=s1, scalar2=None, op0=op0),
                          reads=aps(in0, s1), writes=[out])
                else:
                    P.add(eng, lambda e: e.tensor_scalar(out=out, in0=in0, scalar1=s1, scalar2=s2, op0=op0, op1=op1),
                          reads=aps(in0, s1, s2), writes=[out])

            def stt(out, in0, sc, in1, op0, op1):
                P.add("dve", lambda e: e.scalar_tensor_tensor(out=out, in0=in0, scalar=sc, in1=in1, op0=op0, op1=op1),
                      reads=aps(in0, sc, in1), writes=[out])

            def cp(out, in_, eng="dve"):
                P.add(eng, lambda e: e.tensor_copy(out=out, in_=in_), reads=[in_], writes=[out])

            def memset(t, v, eng="dve"):
                P.add(eng, lambda e: e.memset(t, v), writes=[t])

            def dma(out, in_, key, eng="sp", rd=None, wr=None):
                P.add(eng, lambda e: e.dma_start(out=out, in_=in_), reads=rd or [], writes=wr or [], dma=key)

            def dump(name, ap3):
                if name not in dbg_out:
                    return
                for c in range(NCH):
                    cp(TA[:], ap3[:, c, :])
                    dma(dbg_out[name][:, c * T:(c + 1) * T], TA[:], "out", rd=[TA[:]])

            def pc(name, i=0, n=1):
                return prm[:, COL[name] + i: COL[name] + i + n]

            def seg4(ap):
                return ap.rearrange("p (s t) -> p s t", t=SEG)

            def wacq(view, K, n):
                idx, slot = W.acquire([((lambda K=K, n=n: lambda s: s[:, 0:K * n].rearrange("p (k n) -> p k n", n=n))(), view)])
                return idx, slot[:, 0:K * n].rearrange("p (k n) -> p k n", n=n)

            def wacq2(pieces, K):
                pl = [((lambda o=o, n=v.shape[2]: lambda s: s[:, :].rearrange("p (k n) -> p k n", n=512)[:, 0:K, o:o + n])(), v)
                      for o, v in pieces]
                idx, slot = W.acquire(pl)
                return idx, slot[:, :].rearrange("p (k n) -> p k n", n=512)

            def pipeline(gens, lag, maxact=2):
                active = []
                pending = list(gens)
                while active or pending:
                    if pending and len(active) < maxact and (not active or active[-1][1] >= lag):
                        active.append([pending.pop(0), 0])
                    for a in list(active):
                        try:
                            next(a[0])
                            a[1] += 1
                        except StopIteration:
                            active.remove(a)

            def proj_cols(ps_t, wu, col0, src_b, nk=8):
                for hf_ in range(2):
                    for k in range(nk):
                        mm(ps_t[:, hf_ * 512:(hf_ + 1) * 512], wu[:, k, col0:col0 + 128],
                           src_b[:, k, hf_ * 512:(hf_ + 1) * 512], k == 0, k == nk - 1)

            def rstd_from(ps_t, dst):
                act(dst[:], ps_t[:], AF.Ln, bias=float(EPS), scale=1.0 / D)
                act(dst[:], dst[:], AF.Exp, scale=-0.5)

            memset(ident_f[:], 0.0, "pool")
            P.add("pool", lambda e: e.affine_select(out=ident_f[:], in_=ident_f[:], compare_op=ALU.not_equal, fill=1.0,
                                                    base=0, pattern=[[-1, 128]], channel_multiplier=1),
                  reads=[ident_f[:]], writes=[ident_f[:]])
            cp(ident_b[:], ident_f[:])
            memset(ones_b[:], 1.0)
            memset(ones_f[:], 1.0)
            memset(one11[:], 1.0)
            memset(nhalf[:], -0.5)
            for i in range(2):
                memset(ycp[i][:], 0.0)
            memset(rgw[:], 0.0, "pool")
            memset(ST[:], 0.0)
            dma(flag[:], din["flag"], "in", wr=[flag[:]])
            dma(rows[0:8, 0, :], din["cond"], "in", wr=[rows[0:8, 0, :]])
            p0 = psn()
            tr(p0[:, 0:8], rows[0:8, 0, :], ident_f[0:8, 0:8])
            cp(condT[:], p0[:, 0:8])
            act(scb[:], condT[:], AF.Silu)
            nh = Lr * 16
            dma(rows[0:nh, 1, :], din["h0"], "in", wr=[rows[0:nh, 1, :]])
            p0 = psn()
            tr(p0[:, 0:nh], rows[0:nh, 1, :], ident_f[0:nh, 0:nh])
            cp(h0T[:], p0[:, 0:nh])
            for tq in range(8):
                stg = TA if tq % 2 == 0 else TB
                dma(stg[:], din["x"][tq * 128:(tq + 1) * 128, :], "in", wr=[stg[:]])
                p0 = psn()
                for k in range(NCH):
                    tr(p0[:, k * 128:(k + 1) * 128], stg[:, k * 128:(k + 1) * 128], ident_f[:])
                dst = xT[:, :, tq * 128:(tq + 1) * 128]
                src = p0[:].rearrange("p (k t) -> p k t", t=128)
                if tq % 2 == 0:
                    act(dst, src, AF.Copy)
                else:
                    cp(dst, src)
            ji = small[:, 0:2].bitcast(mybir.dt.int32)
            P.add("pool", lambda e: e.iota(ji, pattern=[[128, 2]], base=0, channel_multiplier=1), writes=[small[:, 0:2]])
            cp(small[:, 2:4], ji)
            act(small[:, 4:6], small[:, 2:4], AF.Exp, scale=-float(np.log(10000.0)) / 256.0)
            ti = TC[:].bitcast(mybir.dt.int32)
            for kind in range(4):
                if kind % 2 == 0:
                    pat = [[1, 16], [0, 64]] if kind == 0 else [[0, 16], [1, 64]]
                    P.add("pool", (lambda pat: lambda e: e.iota(ti, pattern=pat, base=0, channel_multiplier=0))(pat),
                          writes=[TC[:]])
                    cp(TD[:], ti)
                shift = 0.0 if kind % 2 == 0 else float(np.pi / 2)
                for m in range(2):
                    k = kind * 2 + m
                    ts(TA[:], TD[:], small[:, 4 + m:5 + m], shift, ALU.mult, ALU.add)
                    ts(TB[:], TA[:], 1.0 / TWO_PI, 12582912.0, ALU.mult, ALU.add)
                    ts(TB[:], TB[:], -12582912.0, None, ALU.add)
                    stt(TA[:], TB[:], -TWO_PI, TA[:], ALU.mult, ALU.add)
                    ts(TA[:], TA[:], 3.14159, -3.14159, ALU.min, ALU.max)
                    act(TB[:], TA[:], AF.Sin)
                    stt(xT[:, k, :], TB[:], flag[:], xT[:, k, :], ALU.mult, ALU.add)

            dump("xT0", xT)
            def layer_params(l):
                r0 = 0
                srcs = [din["b_in"][l].rearrange("(r c) -> r c", c=128),
                        din["rnn_conv_w"][l].rearrange("k (r c) -> (k r) c", c=128),
                        din["rnn_conv_b"][l].rearrange("(r c) -> r c", c=128),
                        din["rg_b"][l].rearrange("d k (r c) -> (d k r) c", c=128),
                        din["rg_lambda"][l].rearrange("d (r c) -> (d r) c", c=128),
                        din["sg_norm_g"][l].rearrange("(r c) -> r c", c=128),
                        din["cf_conv_w"][l].rearrange("k (r c) -> (k r) c", c=128),
                        din["cf_conv_b"][l].rearrange("(r c) -> r c", c=128),
                        din["cf_ln_g"][l].rearrange("(r c) -> r c", c=128),
                        din["cf_ln_b"][l].rearrange("(r c) -> r c", c=128),
                        din["g_norm"][l].rearrange("k (r c) -> (k r) c", c=128),
                        din["ffn_conv_w"][l].rearrange("k (r c) -> (k r) c", c=128),
                        din["ffn_conv_b"][l].rearrange("(r c) -> r c", c=128),
                        din["b_mod"][l].rearrange("(r c) -> r c", c=128)]
                for (nm, cnt), src in zip(ROWS, srcs):
                    done = 0
                    while done < cnt:
                        g = r0 + done
                        q, r = g // 128, g % 128
                        n = min(cnt - done, 128 - r)
                        dma(rows[r:r + n, q, :], src[done:done + n, :], "in", wr=[rows[r:r + n, q, :]])
                        done += n
                    r0 += cnt
                p0 = psn()
                for q in range(NBLK):
                    nr = min(128, NROWS - q * 128)
                    tr(p0[:, q * 128:q * 128 + nr], rows[0:nr, q, :], ident_f[0:nr, 0:nr])
                cp(prm[:, 0:NROWS], p0[:, 0:NROWS])
                for dk in range(4):
                    for hh in range(2):
                        src = din["rg_w"][l, dk // 2, dk % 2].rearrange("(c two) i j -> two i c j", two=2)[hh]
                        dst = rgw[hh * 64:(hh + 1) * 64, :, dk, hh * 64:(hh + 1) * 64]
                        dma(dst, src, "rgw", eng="pool", wr=[dst])
                sgv = TD[:].rearrange("p (g q) -> p g q", q=128)
                dma(sgv, din["sg_w"][l].rearrange("g p q -> p g q"), "in", wr=[TD[:]])
                p0 = psn()
                for g in range(8):
                    tr(p0[:, g * 128:(g + 1) * 128], sgv[:, g, :], ident_f[:])
                cp(sgwT[:].rearrange("p g q -> p (g q)"), p0[:])
                dma(brow[:], din["sg_b"][l].rearrange("g p -> (g p)").rearrange("(o n) -> o n", o=1), "in", wr=[brow[:]])
                p0 = psn()
                for hf_ in range(2):
                    mm(p0[:, hf_ * 512:(hf_ + 1) * 512], ones_f[0:1, :], brow[0:1, hf_ * 512:(hf_ + 1) * 512], True, True)
                cp(bsg[:].rearrange("p g q -> p (g q)"), p0[:])
                act(pc("tmp", 0, 16), pc("lam", 0, 16), AF.Exp, scale=-1.0)
                act(pc("tmp", 0, 16), pc("tmp", 0, 16), AF.Ln, bias=1.0)
                ts(pc("cl1", 0, 16), pc("tmp", 0, 16), -8.0, None, ALU.mult)
                ts(pc("clh", 0, 16), pc("tmp", 0, 16), -4.0, None, ALU.mult)
                ts(pc("hb", 0, 32), pc("rgb", 0, 32), 0.5, None, ALU.mult)
                ts(pc("w0f", 0, 64), pc("fcw", 0, 64), flag[:], None, ALU.mult)
                ts(pc("w2f", 0, 64), pc("fcw", 128, 64), flag[:], None, ALU.mult)
                ts(pc("rcf", 0, 32), pc("rcw", 0, 32), flag[:], None, ALU.mult)

            def layer_mod_gen(l):
                wv = din["w_mod"][l]
                for ug in range(12):
                    ui, u = wacq(wv[ug].rearrange("p (k n) -> p k n", n=512), 8, 512)
                    pm = psn()
                    for q in range(4):
                        for k in range(8):
                            mm(pm[:, q:q + 1], u[:, k, q * 128:(q + 1) * 128], scb[:, k:k + 1], k == 0, k == 7)
                    W.release(ui)
                    tt(pc("mod", ug * 4, 4), pm[:, 0:4], pc("bmod", ug * 4, 4), ALU.add)
                    yield
                stt(pc("gs1", 0, 8), pc("mod", 8, 8), 1.0, pc("gn", 0, 8), ALU.add, ALU.mult)
                tt(pc("gg1", 0, 8), pc("mod", 16, 8), pc("gn", 8, 8), ALU.mult)
                stt(pc("gs2", 0, 8), pc("mod", 32, 8), 1.0, pc("gn", 16, 8), ALU.add, ALU.mult)
                tt(pc("gg2", 0, 8), pc("mod", 40, 8), pc("gn", 24, 8), ALU.mult)
                yield

            def x_stats():
                pS = PSX
                for c in range(NCH):
                    q = sqb[sqi[0] % 2]
                    sqi[0] += 1
                    act(q[:], xT[:, c, :], AF.Square)
                    for hf_ in range(2):
                        mm(pS[:, hf_ * 512:(hf_ + 1) * 512], ones_b[:], q[:, hf_ * 512:(hf_ + 1) * 512], c == 0, c == NCH - 1)
                return pS

            def modnorm(dst_b, gsn, shoff):
                pS = x_stats()
                rstd_from(pS, TR)
                for c in range(NCH):
                    tmp = TA if c % 2 == 0 else TB
                    tt(tmp[:], xT[:, c, :], TR[:], ALU.mult)
                    act(dst_b[:, c, :], tmp[:], AF.Identity, bias=pc("mod", shoff + c), scale=pc(gsn, c))

            def proj_chunk(ps_t, wu, cc, src_b, nk=8):
                for hf_ in range(2):
                    for k in range(nk):
                        mm(ps_t[:, hf_ * 512:(hf_ + 1) * 512], wu[:, k, cc * 128:(cc + 1) * 128],
                           src_b[:, k, hf_ * 512:(hf_ + 1) * 512], k == 0, k == nk - 1)

            def residual_update(src32, ggn):
                for j in range(NCH):
                    tmp = TA if j % 2 == 0 else TB
                    tt(tmp[:], src32[:, j, :], TR[:], ALU.mult)
                    stt(xT[:, j, :], tmp[:], pc(ggn, j), xT[:, j, :], ALU.mult, ALU.add)

            layer_params(0)
            for _ in layer_mod_gen(0):
                pass
            for l in range(Lr):
                wi = din["w_in"][l].rearrange("(k p) n -> p k n", p=128)

                def win_unit(u):
                    return wacq(wi[:, :, u * 512:(u + 1) * 512], 8, 512)

                modnorm(h_b, "gs1", 0)

                dump("h", h_b)
                arB = f32v(4 * K16, NCH, T)
                SA = dict(z=TD[:], xc=(arB[:, 0, :], arB[:, 1, :]), r=(arB[:, 2, :], arB[:, 5, :]),
                          i=(arB[:, 3, :], arB[:, 6, :]), q=(arB[:, 4, :], arB[:, 7, :]), hf=TR[:],
                          xcb=(sqb[0][:], sqb[1][:]))
                unitsA = {}
                unitsC = {}
                ycc = big32

                def A_unit(c):
                    u, cc2 = c // 2, c % 2
                    if cc2 == 0:
                        unitsA[u] = wacq2([(0, wi[:, :, 2 * u * 128:(2 * u + 2) * 128]),
                                           (256, wi[:, :, 1024 + 2 * u * 128:1024 + (2 * u + 2) * 128])], 8)
                    return unitsA[u]

                def C_unit(c):
                    u, cc2 = c // 2, c % 2
                    if cc2 == 0:
                        unitsC[u] = wacq2([(0, wi[:, :, 4096 + 2 * u * 128:4096 + (2 * u + 2) * 128]),
                                           (256, wi[:, :, 5120 + 2 * u * 128:5120 + (2 * u + 2) * 128])], 8)
                    return unitsC[u]

                def C_front(c):
                    iu, wu = C_unit(c)
                    pg = psn()
                    proj_cols(pg, wu, 256 + (c % 2) * 128, h_b)
                    pa_ = psn()
                    proj_cols(pa_, wu, (c % 2) * 128, h_b)
                    if c % 2 == 1:
                        W.release(iu)
                    act(TA[:], pg[:], AF.Sigmoid, bias=pc("b_in", 40 + c))
                    yp = ycp[c % 2]
                    stt(yp[:, :, 15:271], seg4(pa_[:]), pc("b_in", 32 + c), seg4(TA[:]), ALU.add, ALU.mult)
                    ts(yp[:, 1:4, 0:15], yp[:, 0:3, 256:271], flag[:], None, ALU.mult)
                    ts(yp[:, 0:3, 271:286], yp[:, 1:4, 15:30], flag[:], None, ALU.mult)

                def C_conv(c):
                    yp = ycp[c % 2]
                    pcv = psn()
                    for k in range(31):
                        dgt = dg[dgi[0] % NDG]
                        dgi[0] += 1
                        ts(dgt[:], ident_b[:], pc("cfw", k * 8 + c), None, ALU.mult)
                        for hf_ in range(2):
                            mm(pcv[:, hf_ * 512:(hf_ + 1) * 512].rearrange("p (s t) -> p s t", t=SEG), dgt[:],
                               yp[:, 2 * hf_:2 * hf_ + 2, k:k + 256], k == 0, k == 30)
                    act(ycc[:, c, :], pcv[:], AF.Identity, bias=pc("cfb", c))
                    if c == 0:
                        act(TC[:], pcv[:], AF.Square, bias=pc("cfb", c))
                        cp(TB[:], ycc[:, c, :])
                    else:
                        act(TA[:], pcv[:], AF.Square, bias=pc("cfb", c))
                        tt(TB[:], TB[:], ycc[:, c, :], ALU.add)
                        tt(TC[:], TC[:], TA[:], ALU.add)

                def A_front_a(c):
                    iu, wu = A_unit(c)
                    p1 = psn()
                    proj_cols(p1, wu, (c % 2) * 128, h_b)
                    act(SA["z"], p1[:], AF.Identity, bias=pc("b_in", c))

                def A_front_b(c):
                    z, xc = SA["z"], SA["xc"][c % 2]
                    z4, xc4 = seg4(z), seg4(xc)
                    ts(xc, z, pc("rcw", 16 + c), pc("rcb", c), ALU.mult, ALU.add)
                    stt(xc4[:, :, 2:256], z4[:, :, 0:254], pc("rcw", c), xc4[:, :, 2:256], ALU.mult, ALU.add)
                    stt(xc4[:, 1:4, 0:2], z4[:, 0:3, 254:256], pc("rcf", c), xc4[:, 1:4, 0:2], ALU.mult, ALU.add)
                    stt(xc4[:, :, 1:256], z4[:, :, 0:255], pc("rcw", 8 + c), xc4[:, :, 1:256], ALU.mult, ALU.add)
                    stt(xc4[:, 1:4, 0:1], z4[:, 0:3, 255:256], pc("rcf", 8 + c), xc4[:, 1:4, 0:1], ALU.mult, ALU.add)
                    stt(xc4[:, :, 0:255], z4[:, :, 1:256], pc("rcw", 24 + c), xc4[:, :, 0:255], ALU.mult, ALU.add)
                    stt(xc4[:, 0:3, 255:256], z4[:, 1:4, 0:1], pc("rcf", 24 + c), xc4[:, 0:3, 255:256], ALU.mult, ALU.add)
                    cp(SA["xcb"][c % 2], xc)

                def A_gates(c):
                    xcb_ = SA["xcb"][c % 2]
                    for d in range(2):
                        pr_ = psn()
                        pi_ = psn()
                        for hf_ in range(2):
                            sl = slice(hf_ * 512, (hf_ + 1) * 512)
                            mm(pr_[:, sl], rgw[:, c, d * 2 + 0, :], xcb_[:, sl], True, True)
                            mm(pi_[:, sl], rgw[:, c, d * 2 + 1, :], xcb_[:, sl], True, True)
                        act(SA["r"][d], pr_[:], AF.Tanh, bias=pc("hb", (d * 2 + 0) * 8 + c), scale=0.5)
                        act(SA["i"][d], pi_[:], AF.Tanh, bias=pc("hb", (d * 2 + 1) * 8 + c), scale=0.5)
                    for d in range(2):
                        act(SA["q"][d], SA["r"][d], AF.Exp, bias=pc("cl1", d * 8 + c), scale=pc("cl1", d * 8 + c))
                        act(SA["r"][d], SA["r"][d], AF.Exp, bias=pc("clh", d * 8 + c), scale=pc("clh", d * 8 + c))
                    for d in range(2):
                        act(SA["q"][d], SA["q"][d], AF.Sqrt, bias=1.0, scale=-1.0)

                def A_scan(c):
                    xc, hf = SA["xc"][c % 2], SA["hf"]
                    for d in range(2):
                        a_, u_, s_ = SA["r"][d], SA["i"][d], SA["q"][d]
                        stt(u_, u_, 1.0, s_, ALU.add, ALU.mult)
                        stt(u_, u_, 0.5, xc, ALU.mult, ALU.mult)
                        col = (l * 2 + d) * 8 + c
                        if d == 0:
                            bnd = a_[:, 256:1024:256]
                            ts(bnd, bnd, flag[:], None, ALU.mult)
                            P.add("dve", (lambda a_=a_, u_=u_, hf=hf, col=col: lambda e: e.tensor_tensor_scan(
                                out=hf, data0=a_, data1=u_, initial=h0T[:, col:col + 1], op0=ALU.mult, op1=ALU.add))(),
                                reads=[a_, u_, h0T[:]], writes=[hf])
                            cp(ST[:, :, l, 0, c], hf[:, 255:1024:256])
                        else:
                            bnd = a_[:, 255:1023:256]
                            ts(bnd, bnd, flag[:], None, ALU.mult)
                            P.add("dve", (lambda a_=a_, u_=u_, s_=s_, col=col: lambda e: e.tensor_tensor_scan(
                                out=s_[:, ::-1], data0=a_[:, ::-1], data1=u_[:, ::-1], initial=h0T[:, col:col + 1],
                                op0=ALU.mult, op1=ALU.add))(),
                                reads=[a_, u_, h0T[:]], writes=[s_])
                            cp(ST[:, :, l, 1, c], s_[:, 0:1024:256])
                            tt(hf, hf, s_, ALU.add)

                def A_out(c):
                    iu, wu = unitsA[c // 2]
                    p2 = psn()
                    proj_cols(p2, wu, 256 + (c % 2) * 128, h_b)
                    if c % 2 == 1:
                        W.release(iu)
                    act(SA["r"][0], p2[:], AF.Gelu_apprx_tanh, bias=pc("b_in", 8 + c))
                    tt(ya[:, c, :], SA["hf"], SA["r"][0], ALU.mult)

                ring4[0] = True
                C_front(0)
                A_front_a(0)
                C_conv(0)
                A_front_b(0)
                for c in range(NCH):
                    if c + 1 < NCH:
                        C_front(c + 1)
                        A_front_a(c + 1)
                    A_gates(c)
                    if c + 1 < NCH:
                        C_conv(c + 1)
                        A_front_b(c + 1)
                    A_scan(c)
                    A_out(c)
                dump("ya", ya)

                ring4[0] = False
                pS1 = PSX
                pS2 = psn()
                for hf_ in range(2):
                    sl = slice(hf_ * 512, (hf_ + 1) * 512)
                    mm(pS1[:, sl], ones_f[:], TB[:, sl], True, True)
                    mm(pS2[:, sl], ones_f[:], TC[:, sl], True, True)
                act(TM[:], pS1[:], AF.Identity, scale=1.0 / D)
                act(TD[:], pS1[:], AF.Square, scale=1.0 / D)
                stt(TR[:], pS2[:], 1.0 / D, TD[:], ALU.mult, ALU.subtract)
                act(TR[:], TR[:], AF.Ln, bias=float(EPS))
                act(TR[:], TR[:], AF.Exp, scale=-0.5)
                stt(TM[:], TM[:], -1.0, TR[:], ALU.mult, ALU.mult)

                def ctail_gen():
                    for c in range(NCH):
                        tt(TD[:], ycc[:, c, :], TR[:], ALU.mult)
                        tt(TD[:], TD[:], TM[:], ALU.add)
                        act(yc[:, c, :], TD[:], AF.Silu, bias=pc("clb", c), scale=pc("clg", c))
                        yield

                ring4[0] = True
                svn2 = bfv(K16, 8, 1024)
                dma(brow[:], din["b_in"][l, 3072:4096].rearrange("(o n) -> o n", o=1), "in", wr=[brow[:]])
                iv0, wv0 = win_unit(6)
                iv1, wv1 = win_unit(7)
                def bsv_gen():
                    for tb in range(4):
                        for tq in (2 * tb, 2 * tb + 1):
                            p1 = psn()
                            for hh, wv in enumerate((wv0, wv1)):
                                sl = slice(hh * 512, (hh + 1) * 512)
                                for k in range(8):
                                    mm(p1[:, sl], h_b[:, k, tq * 128:(tq + 1) * 128], wv[:, k, :], k == 0, False)
                                mm(p1[:, sl], ones_f[0:1, :], brow[0:1, sl], False, True)
                            svt = TA if tq % 2 == 0 else TB
                            act(svt[:], p1[:], AF.Gelu_apprx_tanh)
                            so = (tq % 2) * 16
                            P.add("dve", (lambda svt, so: lambda e: e.bn_stats(out=small[:, so:so + 6], in_=svt[:, 0:512]))(svt, so),
                                  reads=[svt[:, 0:512]], writes=[small[:, so:so + 6]])
                            P.add("dve", (lambda svt, so: lambda e: e.bn_stats(out=small[:, so + 6:so + 12], in_=svt[:, 512:1024]))(svt, so),
                                  reads=[svt[:, 512:1024]], writes=[small[:, so + 6:so + 12]])
                            P.add("dve", (lambda so: lambda e: e.bn_aggr(out=small[:, so + 12:so + 14], in_=small[:, so:so + 12]))(so),
                                  reads=[small[:, so:so + 12]], writes=[small[:, so + 12:so + 14]])
                        for tq in (2 * tb, 2 * tb + 1):
                            so = (tq % 2) * 16
                            act(small[:, so + 13:so + 14], small[:, so + 13:so + 14], AF.Sqrt, bias=float(EPS))
                        for tq in (2 * tb, 2 * tb + 1):
                            svt = TA if tq % 2 == 0 else TB
                            so = (tq % 2) * 16
                            P.add("dve", (lambda so: lambda e: e.reciprocal(out=small[:, so + 13:so + 14], in_=small[:, so + 13:so + 14]))(so),
                                  reads=[small[:, so + 13:so + 14]], writes=[small[:, so + 13:so + 14]])
                            stt(small[:, so + 14:so + 15], small[:, so + 12:so + 13], -1.0, small[:, so + 13:so + 14], ALU.mult, ALU.mult)
                            ts(svn2[:, tq, :], svt[:], small[:, so + 13:so + 14], small[:, so + 14:so + 15], ALU.mult, ALU.add)
                        yield

                ring4[0] = True
                pipeline([ctail_gen(), bsv_gen()], lag=1)
                dump("yc", yc)
                W.release(iv0)
                W.release(iv1)
                for ug in range(2):
                    iu, wu = win_unit(4 + ug)
                    for cc in range(4):
                        g = ug * 4 + cc
                        p2 = psn()
                        proj_chunk(p2, wu, cc, h_b)
                        act(TB[:], p2[:], AF.Gelu_apprx_tanh, bias=pc("b_in", 16 + g))
                        p1 = psn()
                        for n in range(8):
                            mm(p1[:, n * 128:(n + 1) * 128], svn2[:, n, g * 128:(g + 1) * 128], sgwT[:, g, :], True, True)
                        stt(TA[:].rearrange("p (n q) -> p n q", q=128), p1[:].rearrange("p (n q) -> p n q", q=128),
                            pc("sgg", g), bsg[:, g, :].unsqueeze(1).broadcast_to([128, 8, 128]), ALU.mult, ALU.add)
                        tt(yb[:, g, :], TA[:], TB[:], ALU.mult)
                    W.release(iu)
                dump("yb", yb)
                macc = big32
                ysrc = (ya, yb, yc)
                for br in range(3):
                    wbv = din["w_branch"][l, br].rearrange("(k p) n -> p k n", p=128)
                    for jp in range(4):
                        gc0 = 6144 + br * 1024 + jp * 256
                        imu, wmu = wacq2([(0, wi[:, :, gc0:gc0 + 256]), (256, wbv[:, :, jp * 256:(jp + 1) * 256])], 8)
                        for cc in range(2):
                            j = jp * 2 + cc
                            pg = psn()
                            proj_cols(pg, wmu, cc * 128, h_b)
                            pp = psn()
                            proj_cols(pp, wmu, 256 + cc * 128, ysrc[br])
                            sg_ = TA if j % 2 == 0 else TB
                            act(sg_[:], pg[:], AF.Sigmoid, bias=pc("b_in", 48 + br * 8 + j))
                            if br == 0:
                                tt(macc[:, j, :], sg_[:], pp[:], ALU.mult)
                            elif br == 1:
                                tt(sg_[:], sg_[:], pp[:], ALU.mult)
                                tt(macc[:, j, :], macc[:, j, :], sg_[:], ALU.add)
                            else:
                                tt(sg_[:], sg_[:], pp[:], ALU.mult)
                                tt(mg[:, j, :], macc[:, j, :], sg_[:], ALU.add)
                        W.release(imu)

                dump("mg", mg)
                ring4[0] = False
                of32 = big32
                wov = din["w_out"][l].rearrange("(k p) n -> p k n", p=128)
                pS = PSX
                for ug in range(2):
                    io, wo = wacq(wov[:, :, ug * 512:(ug + 1) * 512], 8, 512)
                    for cc in range(4):
                        j = ug * 4 + cc
                        p1 = psn()
                        proj_chunk(p1, wo, cc, mg)
                        act(of32[:, j, :], p1[:], AF.Copy)
                        q = sqb[sqi[0] % 2]
                        sqi[0] += 1
                        act(q[:], p1[:], AF.Square)
                        for hf_ in range(2):
                            sl = slice(hf_ * 512, (hf_ + 1) * 512)
                            mm(pS[:, sl], ones_b[:], q[:, sl], j == 0, j == 7)
                    W.release(io)
                rstd_from(pS, TR)
                residual_update(of32, "gg1")

                if stop_after == "s1":
                    break

                modnorm(h2_b, "gs2", 24)
                wup = din["ffn_up"][l].rearrange("(k p) n -> p k n", p=128)
                ring4[0] = True
                for ug in range(16):
                    ifu, wfu = wacq2([(0, wup[:, :, ug * 256:(ug + 1) * 256]),
                                      (256, wup[:, :, DFF + ug * 256: DFF + (ug + 1) * 256])], 8)
                    for cc in range(2):
                        i = ug * 2 + cc
                        Ys = []
                        for side in (0, 1):
                            cch = side * 32 + i
                            p1 = psn()
                            proj_cols(p1, wfu, side * 256 + cc * 128, h2_b)
                            Y = (TA, TB)[side] if i % 2 == 0 else (TC, TD)[side]
                            Ys.append(Y)
                            act(Y[:], p1[:], AF.Identity, bias=pc("fcb", cch), scale=pc("fcw", 64 + cch))
                            Y4 = seg4(Y[:])
                            p4 = seg4(p1[:])
                            stt(Y4[:, :, 1:256], p4[:, :, 0:255], pc("fcw", cch), Y4[:, :, 1:256], ALU.mult, ALU.add)
                            stt(Y4[:, :, 0:255], p4[:, :, 1:256], pc("fcw", 128 + cch), Y4[:, :, 0:255], ALU.mult, ALU.add)
                            stt(Y[:, 256:1024:256], p1[:, 255:1023:256], pc("w0f", cch), Y[:, 256:1024:256], ALU.mult, ALU.add)
                            stt(Y[:, 255:1023:256], p1[:, 256:1024:256], pc("w2f", cch), Y[:, 255:1023:256], ALU.mult, ALU.add)
                        act(Ys[0][:], Ys[0][:], AF.Gelu_apprx_tanh)
                        tt(f_b[:, i, :], Ys[0][:], Ys[1][:], ALU.mult, eng="pool")
                    W.release(ifu)
                ring4[0] = False
                cp(gg2k[:], pc("gg2", 0, 8))
                modg = None
                if l + 1 < Lr:
                    layer_params(l + 1)
                    modg = layer_mod_gen(l + 1)
                wdn = din["ffn_down"][l]
                pS = PSX
                nmod = 0
                for j in range(NCH):
                    while modg is not None and nmod < (13 * (j + 1) + 7) // 8:
                        next(modg, None)
                        nmod += 1
                    idn, wd = wacq(wdn[j].rearrange("p (k n) -> p k n", n=128), 32, 128)
                    p1 = psn()
                    proj_chunk(p1, wd, 0, f_b, nk=32)
                    W.release(idn)
                    act(fo32[:, j, :], p1[:], AF.Copy)
                    q = sqb[sqi[0] % 2]
                    sqi[0] += 1
                    act(q[:], p1[:], AF.Square)
                    for hf_ in range(2):
                        sl = slice(hf_ * 512, (hf_ + 1) * 512)
                        mm(pS[:, sl], ones_b[:], q[:, sl], j == 0, j == 7)
                rstd_from(pS, TR)
                for j in range(NCH):
                    tmp = TA if j % 2 == 0 else TB
                    tt(tmp[:], fo32[:, j, :], TR[:], ALU.mult)
                    stt(xT[:, j, :], tmp[:], gg2k[:, j:j + 1], xT[:, j, :], ALU.mult, ALU.add)

            for tq in range(8):
                p0 = psn()
                for k in range(NCH):
                    tr(p0[:, k * 128:(k + 1) * 128], xT[:, k, tq * 128:(tq + 1) * 128], ident_f[:])
                stg = TA if tq % 2 == 0 else TB
                if tq % 2 == 0:
                    act(stg[:], p0[:], AF.Copy)
                else:
                    cp(stg[:], p0[:])
                dma(y_out[tq * 128:(tq + 1) * 128, :], stg[:], "out", rd=[stg[:]])
            ncol = NSEG * Lr * 16
            stf = ST[:].rearrange("p s l d c -> p (s l d c)")
            nb = (ncol + 127) // 128
            p0 = psn()
            for q in range(nb):
                n = min(128, ncol - q * 128)
                tr(p0[0:n, q * 128:(q + 1) * 128], stf[:, q * 128:q * 128 + n], ident_f[:])
            for q in range(nb):
                n = min(128, ncol - q * 128)
                cp(TC[0:n, q * 128:(q + 1) * 128], p0[0:n, q * 128:(q + 1) * 128])
                dma(st_out[q * 128:q * 128 + n, :], TC[0:n, q * 128:(q + 1) * 128], "out", rd=[TC[0:n, q * 128:(q + 1) * 128]])

        Wd = WStream(_DryProg(), wslots, None)
        emit_all(_DryProg(), Wd)
        P = Prog(nc)
        W = WStream(P, wslots, Wd.rec)
        emit_all(P, W)
        P.finalize(ctx)
    return nc


_NC_CACHE = {}


def _run(inputs, depth=4, stop_after=None, dbg=(), ret_raw=False):
    key = (depth, stop_after, tuple(dbg))
    if key not in _NC_CACHE:
        _NC_CACHE[key] = build_nc(depth, stop_after, dbg)
    nc = _NC_CACHE[key]
    f32 = lambda a: np.ascontiguousarray(np.asarray(a, dtype=np.float32))
    xp = f32(inputs["x_prompt"])
    xs = f32(inputs["x_sample"])
    st = f32(inputs["state_rglru"])
    c = f32(inputs["c"])
    cctx = f32(inputs["c_ctx"])
    wts = {n: f32(inputs[n])[:depth] for n in WNAMES}
    wts["ffn_down"] = np.ascontiguousarray(
        wts["ffn_down"].reshape(depth, 32, 128, NCH, 128).transpose(0, 3, 2, 1, 4)).reshape(depth, NCH, 128, 32 * 128)
    wts["w_mod"] = np.ascontiguousarray(
        wts["w_mod"].reshape(depth, 8, 128, 12, 512).transpose(0, 3, 2, 1, 4)).reshape(depth, 12, 128, 8 * 512)
    in_maps = []
    for core in range(8):
        m = dict(wts)
        if core < 4:
            m["x"] = np.ascontiguousarray(xp[4 * core:4 * core + 4].reshape(T, D))
            m["cond"] = cctx.reshape(8, 128)
            m["flag"] = np.zeros((128, 1), np.float32)
            m["h0"] = np.zeros((depth * 16, 128), np.float32)
        else:
            j = core - 4
            m["x"] = xs[j]
            m["cond"] = np.ascontiguousarray(c[j].reshape(8, 128))
            m["flag"] = np.ones((128, 1), np.float32)
            m["h0"] = np.ascontiguousarray(st[j, :depth].reshape(depth * 16, 128))
        in_maps.append(m)
    res = run_bass_kernel_spmd(nc, in_maps, core_ids=list(range(8)))
    outs = res.results
    if ret_raw:
        return outs
    y_prompt = np.stack([outs[i]["y"] for i in range(4)]).reshape(16, 256, D)
    y_sample = np.stack([outs[4 + i]["y"] for i in range(4)])
    new_state = np.concatenate([outs[i]["st"].reshape(NSEG, depth, 2, D) for i in range(4)], axis=0)
    return (y_prompt.astype(np.float32), y_sample.astype(np.float32), new_state.astype(np.float32))


def kernel(**inputs):
    return _run(inputs, depth=4)
```

```python
import bisect
import numpy as np
import concourse.bass as bass
import concourse.mybir as mybir
from concourse.bass_utils import run_bass_kernel_spmd

F32 = mybir.dt.float32
BF16 = mybir.dt.bfloat16
AF = mybir.ActivationFunctionType
ALU = mybir.AluOpType


class _Op:
    __slots__ = ("eng", "emit", "deps", "idx", "ticket", "dmakey", "needs_inc")


def _region(ap):
    t = ap.tensor
    dims = ap.ap
    pstride = dims[0][0]
    off = ap.offset
    if pstride > 0:
        p0 = off // pstride
        f0 = off % pstride
    else:
        p0 = 0
        f0 = off
    lo = f0
    hi = f0
    for st, cnt in dims[1:]:
        ext = st * (cnt - 1)
        if ext < 0:
            lo += ext
        else:
            hi += ext
    return (t.name, p0, p0 + dims[0][1], lo, hi + 1)


def _ovl(a, b):
    return a[1] < b[2] and b[1] < a[2] and a[3] < b[4] and b[3] < a[4]


def _covers(a, b):
    return a[1] <= b[1] and a[2] >= b[2] and a[3] <= b[3] and a[4] >= b[4]


class Prog:
    ENGS = ("pe", "act", "dve", "pool", "sp")

    def __init__(self, nc):
        self.nc = nc
        self.ops = []
        self.hist = {}
        self.dma_keys = {}

    def add(self, eng, emit, reads=(), writes=(), dma=None):
        op = _Op()
        op.eng = eng
        op.emit = emit
        op.idx = len(self.ops)
        op.dmakey = dma
        op.needs_inc = dma is not None
        op.ticket = None
        deps = set()
        rr = [_region(a) for a in reads]
        wr = [_region(a) for a in writes]
        for r in rr:
            h = self.hist.get(r[0])
            if h:
                for (reg, i, e, _d) in h[0]:
                    if _ovl(reg, r):
                        deps.add(i)
        for w in wr:
            h = self.hist.get(w[0])
            if h:
                for (reg, i, e, _d) in h[0]:
                    if _ovl(reg, w):
                        deps.add(i)
                for (reg, i, e, _d) in h[1]:
                    if _ovl(reg, w):
                        deps.add(i)
        for r in rr:
            h = self.hist.setdefault(r[0], [[], []])
            if dma is None:
                h[1] = [x for x in h[1] if not (x[2] == eng and not x[3] and _covers(r, x[0]))]
            h[1].append((r, op.idx, eng, dma is not None))
        for w in wr:
            h = self.hist.setdefault(w[0], [[], []])
            h[0] = [x for x in h[0] if not _covers(w, x[0])]
            h[1] = [x for x in h[1] if not _covers(w, x[0])]
            h[0].append((w, op.idx, eng, dma is not None))
        deps.discard(op.idx)
        op.deps = deps
        if dma is not None:
            self.dma_keys.setdefault(dma, []).append(op.idx)
        self.ops.append(op)
        return op

    def finalize(self, ctx):
        nc = self.nc
        ops = self.ops
        for op in ops:
            for d in op.deps:
                p = ops[d]
                if p.eng == "pe" and op.eng == "pe":
                    continue
                p.needs_inc = True
        cnt = {e: 0 for e in self.ENGS}
        for op in ops:
            if op.dmakey is None and op.needs_inc:
                cnt[op.eng] += 1
                op.ticket = cnt[op.eng]
        esem = {e: ctx.enter_context(nc.semaphore("sem_" + e)) for e in self.ENGS}
        dsem = {k: ctx.enter_context(nc.semaphore("dsem_" + str(k))) for k in self.dma_keys}
        block = ctx.enter_context(nc.Block())
        streams = {e: [o for o in ops if o.eng == e] for e in self.ENGS}
        dma_keys = self.dma_keys

        def run_stream(ename, eh):
            waited = {}
            for op in streams[ename]:
                need = {}
                for d in op.deps:
                    p = ops[d]
                    if p.dmakey is not None:
                        lst = dma_keys[p.dmakey]
                        val = 16 * bisect.bisect_left(lst, op.idx)
                        key = ("d", p.dmakey)
                    else:
                        if p.eng == "pe" and ename == "pe":
                            continue
                        val = p.ticket
                        key = ("e", p.eng)
                    if need.get(key, 0) < val:
                        need[key] = val
                for key, val in need.items():
                    if waited.get(key, 0) >= val:
                        continue
                    waited[key] = val
                    s = dsem[key[1]] if key[0] == "d" else esem[key[1]]
                    eh.wait_ge(s, val)
                ins = op.emit(eh)
                if op.dmakey is not None:
                    ins.then_inc(dsem[op.dmakey], 16)
                elif op.needs_inc:
                    ins.then_inc(esem[ename], 1)
            if ename == "sp":
                for k, lst in dma_keys.items():
                    eh.wait_ge(dsem[k], 16 * len(lst))

        @block.tensor
        def _(e):
            run_stream("pe", e)

        @block.scalar
        def _(e):
            run_stream("act", e)

        @block.vector
        def _(e):
            run_stream("dve", e)

        @block.gpsimd
        def _(e):
            run_stream("pool", e)

        @block.sync
        def _(e):
            run_stream("sp", e)


D = 1024
T = 1024
NCH = 8
SEG = 256
NSEG = 4
N_IN = 9216
DFF = 4096
EPS = 1e-6
NW = 3
TWO_PI = 6.283185307179586

ROWS = [("b_in", 72), ("rcw", 32), ("rcb", 8), ("rgb", 32), ("lam", 16), ("sgg", 8), ("cfw", 248),
        ("cfb", 8), ("clg", 8), ("clb", 8), ("gn", 32), ("fcw", 192), ("fcb", 64), ("bmod", 48)]
COL = {}
_o = 0
for _n, _c in ROWS:
    COL[_n] = _o
    _o += _c
NROWS = _o
NBLK = (NROWS + 127) // 128
_o = NBLK * 128
for _n, _c in [("mod", 48), ("gs1", 8), ("gg1", 8), ("gs2", 8), ("gg2", 8), ("clh", 16), ("cl1", 16),
               ("w0f", 64), ("w2f", 64), ("hb", 32), ("tmp", 16), ("rcf", 32)]:
    COL[_n] = _o
    _o += _c
NPRM = _o

WNAMES = ["w_mod", "b_mod", "g_norm", "w_in", "b_in", "rnn_conv_w", "rnn_conv_b", "rg_w", "rg_b",
          "rg_lambda", "sg_norm_g", "sg_w", "sg_b", "cf_conv_w", "cf_conv_b", "cf_ln_g", "cf_ln_b",
          "w_branch", "w_out", "ffn_up", "ffn_conv_w", "ffn_conv_b", "ffn_down"]
WSHAPES = {"w_mod": [12, 128, 8 * 512], "b_mod": [6 * D], "g_norm": [4, D], "w_in": [D, N_IN], "b_in": [N_IN],
           "rnn_conv_w": [4, D], "rnn_conv_b": [D], "rg_w": [2, 2, 16, 64, 64], "rg_b": [2, 2, D],
           "rg_lambda": [2, D], "sg_norm_g": [D], "sg_w": [8, 128, 128], "sg_b": [8, 128],
           "cf_conv_w": [31, D], "cf_conv_b": [D], "cf_ln_g": [D], "cf_ln_b": [D],
           "w_branch": [3, D, D], "w_out": [D, D], "ffn_up": [D, 2 * DFF], "ffn_conv_w": [3, 2 * DFF],
           "ffn_conv_b": [2 * DFF], "ffn_down": [NCH, 128, 32 * 128]}


class WStream:
    def __init__(self, P, slots, plan):
        self.P = P
        self.slots = slots
        self.plan = plan
        self.rec = []
        self.i = 0
        self.issued = 0
        self.done = set()
        self.done_upto = -1

    def acquire(self, pieces):
        idx = self.i
        self.i += 1
        self.rec.append(pieces)
        if self.plan is not None:
            self._pump()
            assert self.issued > idx, "weight ring stalled: release earlier units first"
        return idx, self.slots[idx % NW]

    def release(self, idx):
        self.done.add(idx)
        while (self.done_upto + 1) in self.done:
            self.done_upto += 1
        if self.plan is not None:
            self._pump()

    def _pump(self):
        while self.issued < len(self.plan) and self.issued - NW <= self.done_upto:
            j = self.issued
            slot = self.slots[j % NW]
            for dstf, src in self.plan[j]:
                dst = dstf(slot)
                self.P.add("pool", (lambda d, s: lambda e: e.dma_start(out=d, in_=s))(dst, src),
                           writes=[dst], dma="w%d" % (j % NW))
            self.issued += 1


class _DryProg:
    def add(self, *a, **k):
        return None


def build_nc(depth=4, stop_after=None, dbg=()):
    nc = bass.Bass("TRN2", target_bir_lowering=False)
    from contextlib import ExitStack
    Lr = depth
    din = {}
    din["x"] = nc.dram_tensor("x", [T, D], F32, kind="ExternalInput").ap()
    din["cond"] = nc.dram_tensor("cond", [8, 128], F32, kind="ExternalInput").ap()
    din["flag"] = nc.dram_tensor("flag", [128, 1], F32, kind="ExternalInput").ap()
    din["h0"] = nc.dram_tensor("h0", [Lr * 16, 128], F32, kind="ExternalInput").ap()
    for n in WNAMES:
        din[n] = nc.dram_tensor(n, [Lr] + WSHAPES[n], F32, kind="ExternalInput").ap()
    y_out = nc.dram_tensor("y", [T, D], F32, kind="ExternalOutput").ap()
    st_out = nc.dram_tensor("st", [NSEG * Lr * 16, 128], F32, kind="ExternalOutput").ap()
    dbg_out = {n: nc.dram_tensor("dbg_" + n, [128, NCH * T], F32, kind="ExternalOutput").ap() for n in dbg}

    with ExitStack() as ctx:
        def sb(name, shape, dt=F32):
            return ctx.enter_context(nc.sbuf_tensor(name, shape, dt))

        xT = sb("xT", [128, NCH, T])
        prm = sb("prm", [128, NPRM])
        ident_f = sb("ident_f", [128, 128])
        ident_b = sb("ident_b", [128, 128], BF16)
        ones_b = sb("ones_b", [128, 128], BF16)
        one11 = sb("one11", [1, 1])
        nhalf = sb("nhalf", [128, 1])
        flag = sb("flag_sb", [128, 1])
        condT = sb("condT", [128, 8])
        scb = sb("scb", [128, 8], BF16)
        h0T = sb("h0T", [128, Lr * 16])
        ST = sb("ST", [128, NSEG, Lr, 2, 8])
        rgw = sb("rgw", [128, NCH, 4, 128], BF16)
        sgwT = sb("sgwT", [128, 8, 128], BF16)
        bsg = sb("bsg", [128, 8, 128])
        small = sb("small", [128, 32])
        gg2k = sb("gg2k", [128, 8])
        wslots = [sb("wslot%d" % i, [128, 4096], BF16) for i in range(NW)]
        AR = sb("AR", [128, 24576])
        ARb = AR[:].bitcast(BF16)
        def bfv(off_b, n1, n2):
            return ARb[:, off_b // 2: off_b // 2 + n1 * n2].rearrange("p (a b) -> p a b", b=n2)
        def f32v(off_b, n1, n2):
            return AR[:, off_b // 4: off_b // 4 + n1 * n2].rearrange("p (a b) -> p a b", b=n2)
        K16 = 16384
        h_b = bfv(0, NCH, T)
        big32 = f32v(K16, NCH, T)
        ya = bfv(3 * K16, NCH, T)
        yb = bfv(4 * K16, NCH, T)
        yc = bfv(5 * K16, NCH, T)
        svn = bfv(5 * K16, 8, 1024)
        mg = ya
        f_b = bfv(0, 32, T)
        h2_b = bfv(4 * K16, NCH, T)
        fo32 = f32v(4 * K16, NCH, T)
        TA = sb("TA", [128, T]); TB = sb("TB", [128, T]); TC = sb("TC", [128, T]); TD = sb("TD", [128, T])
        TR = sb("TR", [128, T])
        ones_f = sb("ones_f", [128, 128])
        TE = TR
        TM = TC
        zpf = sb("zpf", [128, NSEG * 260])
        zp = zpf[:].rearrange("p (s t) -> p s t", t=260)
        rows = zpf[:, 0:1024].rearrange("p (q c) -> p q c", c=128)
        brow = zpf[0:1, 0:1024]
        mrow = TC[0:1, 0:512]
        sqb = [sb("sqb%d" % i, [128, T], BF16) for i in range(2)]
        xcb = sqb[0]
        ycp = [sb("ycp%d" % i, [128, NSEG, 286], BF16) for i in range(2)]
        NDG = 4
        dg = [sb("dg%d" % i, [128, 128], BF16) for i in range(NDG)]
        PS = [ctx.enter_context(nc.psum_tensor("ps%d" % i, [128, T], F32)) for i in range(3)]
        PSX = ctx.enter_context(nc.psum_tensor("psx", [128, T], F32))
        PSS = [0]

        def emit_all(P, W):
            PSS[0] = 0
            sqi = [0]
            dgi = [0]

            ring4 = [False]

            def psn():
                if ring4[0]:
                    t = (PS + [PSX])[PSS[0] % 4]
                else:
                    t = PS[PSS[0] % 3]
                PSS[0] += 1
                return t

            def aps(*a):
                return [v for v in a if v is not None and not isinstance(v, (int, float))]

            def mm(out, lhsT, rhs, start, stop):
                P.add("pe", lambda e: e.matmul(out, lhsT=lhsT, rhs=rhs, start=start, stop=stop),
                      reads=[lhsT, rhs], writes=[out])

            def tr(out, in_, ident):
                P.add("pe", lambda e: e.transpose(out=out, in_=in_, identity=ident), reads=[in_, ident], writes=[out])

            def act(out, in_, func, bias=None, scale=None):
                kw = {}
                if bias is not None:
                    kw["bias"] = bias
                if scale is not None:
                    kw["scale"] = scale
                P.add("act", lambda e: e.activation(out=out, in_=in_, func=func, **kw),
                      reads=aps(in_, bias, scale), writes=[out])

            def tt(out, in0, in1, op, eng="dve"):
                P.add(eng, lambda e: e.tensor_tensor(out=out, in0=in0, in1=in1, op=op), reads=[in0, in1], writes=[out])

            def ts(out, in0, s1, s2, op0, op1=None, eng="dve"):
                if op1 is None:
                    P.add(eng, lambda e: e.tensor_scalar(out=out, in0=in0, scalar1=s1, scalar2=None, op0=op0),
                          reads=aps(in0, s1), writes=[out])
                else:
                    P.add(eng, lambda e: e.tensor_scalar(out=out, in0=in0, scalar1=s1, scalar2=s2, op0=op0, op1=op1),
                          reads=aps(in0, s1, s2), writes=[out])

            def stt(out, in0, sc, in1, op0, op1):
                P.add("dve", lambda e: e.scalar_tensor_tensor(out=out, in0=in0, scalar=sc, in1=in1, op0=op0, op1=op1),
                      reads=aps(in0, sc, in1), writes=[out])

            def cp(out, in_, eng="dve"):
                P.add(eng, lambda e: e.tensor_copy(out=out, in_=in_), reads=[in_], writes=[out])

            def memset(t, v, eng="dve"):
                P.add(eng, lambda e: e.memset(t, v), writes=[t])

            def dma(out, in_, key, eng="sp", rd=None, wr=None):
                P.add(eng, lambda e: e.dma_start(out=out, in_=in_), reads=rd or [], writes=wr or [], dma=key)

            def dump(name, ap3):
                if name not in dbg_out:
                    return
                for c in range(NCH):
                    cp(TA[:], ap3[:, c, :])
                    dma(dbg_out[name][:, c * T:(c + 1) * T], TA[:], "out", rd=[TA[:]])

            def pc(name, i=0, n=1):
                return prm[:, COL[name] + i: COL[name] + i + n]

            def seg4(ap):
                return ap.rearrange("p (s t) -> p s t", t=SEG)

            def wacq(view, K, n):
                idx, slot = W.acquire([((lambda K=K, n=n: lambda s: s[:, 0:K * n].rearrange("p (k n) -> p k n", n=n))(), view)])
                return idx, slot[:, 0:K * n].rearrange("p (k n) -> p k n", n=n)

            def wacq2(pieces, K):
                pl = [((lambda o=o, n=v.shape[2]: lambda s: s[:, :].rearrange("p (k n) -> p k n", n=512)[:, 0:K, o:o + n])(), v)
                      for o, v in pieces]
                idx, slot = W.acquire(pl)
                return idx, slot[:, :].rearrange("p (k n) -> p k n", n=512)

            def pipeline(gens, lag, maxact=2):
                active = []
                pending = list(gens)
                while active or pending:
                    if pending and len(active) < maxact and (not active or active[-1][1] >= lag):
                        active.append([pending.pop(0), 0])
                    for a in list(active):
                        try:
                            next(a[0])
                            a[1] += 1
                        except StopIteration:
                            active.remove(a)

            def proj_cols(ps_t, wu, col0, src_b, nk=8):
                for hf_ in range(2):
                    for k in range(nk):
                        mm(ps_t[:, hf_ * 512:(hf_ + 1) * 512], wu[:, k, col0:col0 + 128],
                           src_b[:, k, hf_ * 512:(hf_ + 1) * 512], k == 0, k == nk - 1)

            def rstd_from(ps_t, dst):
                act(dst[:], ps_t[:], AF.Ln, bias=float(EPS), scale=1.0 / D)
                act(dst[:], dst[:], AF.Exp, scale=-0.5)

            memset(ident_f[:], 0.0, "pool")
            P.add("pool", lambda e: e.affine_select(out=ident_f[:], in_=ident_f[:], compare_op=ALU.not_equal, fill=1.0,
                                                    base=0, pattern=[[-1, 128]], channel_multiplier=1),
                  reads=[ident_f[:]], writes=[ident_f[:]])
            cp(ident_b[:], ident_f[:])
            memset(ones_b[:], 1.0)
            memset(ones_f[:], 1.0)
            memset(one11[:], 1.0)
            memset(nhalf[:], -0.5)
            for i in range(2):
                memset(ycp[i][:], 0.0)
            memset(rgw[:], 0.0, "pool")
            memset(ST[:], 0.0)
            dma(flag[:], din["flag"], "in", wr=[flag[:]])
            dma(rows[0:8, 0, :], din["cond"], "in", wr=[rows[0:8, 0, :]])
            p0 = psn()
            tr(p0[:, 0:8], rows[0:8, 0, :], ident_f[0:8, 0:8])
            cp(condT[:], p0[:, 0:8])
            act(scb[:], condT[:], AF.Silu)
            nh = Lr * 16
            dma(rows[0:nh, 1, :], din["h0"], "in", wr=[rows[0:nh, 1, :]])
            p0 = psn()
            tr(p0[:, 0:nh], rows[0:nh, 1, :], ident_f[0:nh, 0:nh])
            cp(h0T[:], p0[:, 0:nh])
            for tq in range(8):
                stg = TA if tq % 2 == 0 else TB
                dma(stg[:], din["x"][tq * 128:(tq + 1) * 128, :], "in", wr=[stg[:]])
                p0 = psn()
                for k in range(NCH):
                    tr(p0[:, k * 128:(k + 1) * 128], stg[:, k * 128:(k + 1) * 128], ident_f[:])
                dst = xT[:, :, tq * 128:(tq + 1) * 128]
                src = p0[:].rearrange("p (k t) -> p k t", t=128)
                if tq % 2 == 0:
                    act(dst, src, AF.Copy)
                else:
                    cp(dst, src)
            ji = small[:, 0:2].bitcast(mybir.dt.int32)
            P.add("pool", lambda e: e.iota(ji, pattern=[[128, 2]], base=0, channel_multiplier=1), writes=[small[:, 0:2]])
            cp(small[:, 2:4], ji)
            act(small[:, 4:6], small[:, 2:4], AF.Exp, scale=-float(np.log(10000.0)) / 256.0)
            ti = TC[:].bitcast(mybir.dt.int32)
            for kind in range(4):
                if kind % 2 == 0:
                    pat = [[1, 16], [0, 64]] if kind == 0 else [[0, 16], [1, 64]]
                    P.add("pool", (lambda pat: lambda e: e.iota(ti, pattern=pat, base=0, channel_multiplier=0))(pat),
                          writes=[TC[:]])
                    cp(TD[:], ti)
                shift = 0.0 if kind % 2 == 0 else float(np.pi / 2)
                for m in range(2):
                    k = kind * 2 + m
                    ts(TA[:], TD[:], small[:, 4 + m:5 + m], shift, ALU.mult, ALU.add)
                    ts(TB[:], TA[:], 1.0 / TWO_PI, 12582912.0, ALU.mult, ALU.add)
                    ts(TB[:], TB[:], -12582912.0, None, ALU.add)
                    stt(TA[:], TB[:], -TWO_PI, TA[:], ALU.mult, ALU.add)
                    ts(TA[:], TA[:], 3.14159, -3.14159, ALU.min, ALU.max)
                    act(TB[:], TA[:], AF.Sin)
                    stt(xT[:, k, :], TB[:], flag[:], xT[:, k, :], ALU.mult, ALU.add)

            dump("xT0", xT)
            def layer_params(l):
                r0 = 0
                srcs = [din["b_in"][l].rearrange("(r c) -> r c", c=128),
                        din["rnn_conv_w"][l].rearrange("k (r c) -> (k r) c", c=128),
                        din["rnn_conv_b"][l].rearrange("(r c) -> r c", c=128),
                        din["rg_b"][l].rearrange("d k (r c) -> (d k r) c", c=128),
                        din["rg_lambda"][l].rearrange("d (r c) -> (d r) c", c=128),
                        din["sg_norm_g"][l].rearrange("(r c) -> r c", c=128),
                        din["cf_conv_w"][l].rearrange("k (r c) -> (k r) c", c=128),
                        din["cf_conv_b"][l].rearrange("(r c) -> r c", c=128),
                        din["cf_ln_g"][l].rearrange("(r c) -> r c", c=128),
                        din["cf_ln_b"][l].rearrange("(r c) -> r c", c=128),
                        din["g_norm"][l].rearrange("k (r c) -> (k r) c", c=128),
                        din["ffn_conv_w"][l].rearrange("k (r c) -> (k r) c", c=128),
                        din["ffn_conv_b"][l].rearrange("(r c) -> r c", c=128),
                        din["b_mod"][l].rearrange("(r c) -> r c", c=128)]
                for (nm, cnt), src in zip(ROWS, srcs):
                    done = 0
                    while done < cnt:
                        g = r0 + done
                        q, r = g // 128, g % 128
                        n = min(cnt - done, 128 - r)
                        dma(rows[r:r + n, q, :], src[done:done + n, :], "in", wr=[rows[r:r + n, q, :]])
                        done += n
                    r0 += cnt
                p0 = psn()
                for q in range(NBLK):
                    nr = min(128, NROWS - q * 128)
                    tr(p0[:, q * 128:q * 128 + nr], rows[0:nr, q, :], ident_f[0:nr, 0:nr])
                cp(prm[:, 0:NROWS], p0[:, 0:NROWS])
                for dk in range(4):
                    for hh in range(2):
                        src = din["rg_w"][l, dk // 2, dk % 2].rearrange("(c two) i j -> two i c j", two=2)[hh]
                        dst = rgw[hh * 64:(hh + 1) * 64, :, dk, hh * 64:(hh + 1) * 64]
                        dma(dst, src, "rgw", eng="pool", wr=[dst])
                sgv = TD[:].rearrange("p (g q) -> p g q", q=128)
                dma(sgv, din["sg_w"][l].rearrange("g p q -> p g q"), "in", wr=[TD[:]])
                p0 = psn()
                for g in range(8):
                    tr(p0[:, g * 128:(g + 1) * 128], sgv[:, g, :], ident_f[:])
                cp(sgwT[:].rearrange("p g q -> p (g q)"), p0[:])
                dma(brow[:], din["sg_b"][l].rearrange("g p -> (g p)").rearrange("(o n) -> o n", o=1), "in", wr=[brow[:]])
                p0 = psn()
                for hf_ in range(2):
                    mm(p0[:, hf_ * 512:(hf_ + 1) * 512], ones_f[0:1, :], brow[0:1, hf_ * 512:(hf_ + 1) * 512], True, True)
                cp(bsg[:].rearrange("p g q -> p (g q)"), p0[:])
                act(pc("tmp", 0, 16), pc("lam", 0, 16), AF.Exp, scale=-1.0)
                act(pc("tmp", 0, 16), pc("tmp", 0, 16), AF.Ln, bias=1.0)
                ts(pc("cl1", 0, 16), pc("tmp", 0, 16), -8.0, None, ALU.mult)
                ts(pc("clh", 0, 16), pc("tmp", 0, 16), -4.0, None, ALU.mult)
                ts(pc("hb", 0, 32), pc("rgb", 0, 32), 0.5, None, ALU.mult)
                ts(pc("w0f", 0, 64), pc("fcw", 0, 64), flag[:], None, ALU.mult)
                ts(pc("w2f", 0, 64), pc("fcw", 128, 64), flag[:], None, ALU.mult)
                ts(pc("rcf", 0, 32), pc("rcw", 0, 32), flag[:], None, ALU.mult)

            def layer_mod_gen(l):
                wv = din["w_mod"][l]
                for ug in range(12):
                    ui, u = wacq(wv[ug].rearrange("p (k n) -> p k n", n=512), 8, 512)
                    pm = psn()
                    for q in range(4):
                        for k in range(8):
                            mm(pm[:, q:q + 1], u[:, k, q * 128:(q + 1) * 128], scb[:, k:k + 1], k == 0, k == 7)
                    W.release(ui)
                    tt(pc("mod", ug * 4, 4), pm[:, 0:4], pc("bmod", ug * 4, 4), ALU.add)
                    yield
                stt(pc("gs1", 0, 8), pc("mod", 8, 8), 1.0, pc("gn", 0, 8), ALU.add, ALU.mult)
                tt(pc("gg1", 0, 8), pc("mod", 16, 8), pc("gn", 8, 8), ALU.mult)
                stt(pc("gs2", 0, 8), pc("mod", 32, 8), 1.0, pc("gn", 16, 8), ALU.add, ALU.mult)
                tt(pc("gg2", 0, 8), pc("mod", 40, 8), pc("gn", 24, 8), ALU.mult)
                yield

            def x_stats():
                pS = PSX
                for c in range(NCH):
                    q = sqb[sqi[0] % 2]
                    sqi[0] += 1
                    act(q[:], xT[:, c, :], AF.Square)
                    for hf_ in range(2):
                        mm(pS[:, hf_ * 512:(hf_ + 1) * 512], ones_b[:], q[:, hf_ * 512:(hf_ + 1) * 512], c == 0, c == NCH - 1)
                return pS

            def modnorm(dst_b, gsn, shoff):
                pS = x_stats()
                rstd_from(pS, TR)
                for c in range(NCH):
                    tmp = TA if c % 2 == 0 else TB
                    tt(tmp[:], xT[:, c, :], TR[:], ALU.mult)
                    act(dst_b[:, c, :], tmp[:], AF.Identity, bias=pc("mod", shoff + c), scale=pc(gsn, c))

            def proj_chunk(ps_t, wu, cc, src_b, nk=8):
                for hf_ in range(2):
                    for k in range(nk):
                        mm(ps_t[:, hf_ * 512:(hf_ + 1) * 512], wu[:, k, cc * 128:(cc + 1) * 128],
                           src_b[:, k, hf_ * 512:(hf_ + 1) * 512], k == 0, k == nk - 1)

            def residual_update(src32, ggn):
                for j in range(NCH):
                    tmp = TA if j % 2 == 0 else TB
                    tt(tmp[:], src32[:, j, :], TR[:], ALU.mult)
                    stt(xT[:, j, :], tmp[:], pc(ggn, j), xT[:, j, :], ALU.mult, ALU.add)

            layer_params(0)
            for _ in layer_mod_gen(0):
                pass
            for l in range(Lr):
                wi = din["w_in"][l].rearrange("(k p) n -> p k n", p=128)

                def win_unit(u):
                    return wacq(wi[:, :, u * 512:(u + 1) * 512], 8, 512)

                modnorm(h_b, "gs1", 0)

                dump("h", h_b)
                arB = f32v(4 * K16, NCH, T)
                SA = dict(z=TD[:], xc=(arB[:, 0, :], arB[:, 1, :]), r=(arB[:, 2, :], arB[:, 5, :]),
                          i=(arB[:, 3, :], arB[:, 6, :]), q=(arB[:, 4, :], arB[:, 7, :]), hf=TR[:],
                          xcb=(sqb[0][:], sqb[1][:]))
                unitsA = {}
                unitsC = {}
                ycc = big32

                def A_unit(c):
                    u, cc2 = c // 2, c % 2
                    if cc2 == 0:
                        unitsA[u] = wacq2([(0, wi[:, :, 2 * u * 128:(2 * u + 2) * 128]),
                                           (256, wi[:, :, 1024 + 2 * u * 128:1024 + (2 * u + 2) * 128])], 8)
                    return unitsA[u]

                def C_unit(c):
                    u, cc2 = c // 2, c % 2
                    if cc2 == 0:
                        unitsC[u] = wacq2([(0, wi[:, :, 4096 + 2 * u * 128:4096 + (2 * u + 2) * 128]),
                                           (256, wi[:, :, 5120 + 2 * u * 128:5120 + (2 * u + 2) * 128])], 8)
                    return unitsC[u]

                def C_front(c):
                    iu, wu = C_unit(c)
                    pg = psn()
                    proj_cols(pg, wu, 256 + (c % 2) * 128, h_b)
                    pa_ = psn()
                    proj_cols(pa_, wu, (c % 2) * 128, h_b)
                    if c % 2 == 1:
                        W.release(iu)
                    act(TA[:], pg[:], AF.Sigmoid, bias=pc("b_in", 40 + c))
                    yp = ycp[c % 2]
                    stt(yp[:, :, 15:271], seg4(pa_[:]), pc("b_in", 32 + c), seg4(TA[:]), ALU.add, ALU.mult)
                    ts(yp[:, 1:4, 0:15], yp[:, 0:3, 256:271], flag[:], None, ALU.mult)
                    ts(yp[:, 0:3, 271:286], yp[:, 1:4, 15:30], flag[:], None, ALU.mult)

                def C_conv(c):
                    yp = ycp[c % 2]
                    pcv = psn()
                    for k in range(31):
                        dgt = dg[dgi[0] % NDG]
                        dgi[0] += 1
                        ts(dgt[:], ident_b[:], pc("cfw", k * 8 + c), None, ALU.mult)
                        for hf_ in range(2):
                            mm(pcv[:, hf_ * 512:(hf_ + 1) * 512].rearrange("p (s t) -> p s t", t=SEG), dgt[:],
                               yp[:, 2 * hf_:2 * hf_ + 2, k:k + 256], k == 0, k == 30)
                    act(ycc[:, c, :], pcv[:], AF.Identity, bias=pc("cfb", c))
                    if c == 0:
                        act(TC[:], pcv[:], AF.Square, bias=pc("cfb", c))
                        cp(TB[:], ycc[:, c, :])
                    else:
                        act(TA[:], pcv[:], AF.Square, bias=pc("cfb", c))

                def C_lnacc(c):
                    tt(TB[:], TB[:], ycc[:, c, :], ALU.add)
                    tt(TC[:], TC[:], TA[:], ALU.add)

                def A_front_a(c):
                    iu, wu = A_unit(c)
                    p1 = psn()
                    proj_cols(p1, wu, (c % 2) * 128, h_b)
                    act(SA["z"], p1[:], AF.Identity, bias=pc("b_in", c))

                def A_front_b(c):
                    z, xc = SA["z"], SA["xc"][c % 2]
                    z4, xc4 = seg4(z), seg4(xc)
                    ts(xc, z, pc("rcw", 16 + c), pc("rcb", c), ALU.mult, ALU.add)
                    stt(xc4[:, :, 2:256], z4[:, :, 0:254], pc("rcw", c), xc4[:, :, 2:256], ALU.mult, ALU.add)
                    stt(xc4[:, 1:4, 0:2], z4[:, 0:3, 254:256], pc("rcf", c), xc4[:, 1:4, 0:2], ALU.mult, ALU.add)
                    stt(xc4[:, :, 1:256], z4[:, :, 0:255], pc("rcw", 8 + c), xc4[:, :, 1:256], ALU.mult, ALU.add)
                    stt(xc4[:, 1:4, 0:1], z4[:, 0:3, 255:256], pc("rcf", 8 + c), xc4[:, 1:4, 0:1], ALU.mult, ALU.add)
                    stt(xc4[:, :, 0:255], z4[:, :, 1:256], pc("rcw", 24 + c), xc4[:, :, 0:255], ALU.mult, ALU.add)
                    stt(xc4[:, 0:3, 255:256], z4[:, 1:4, 0:1], pc("rcf", 24 + c), xc4[:, 0:3, 255:256], ALU.mult, ALU.add)
                    cp(SA["xcb"][c % 2], xc)

                def A_gates(c):
                    xcb_ = SA["xcb"][c % 2]
                    for d in range(2):
                        pr_ = psn()
                        pi_ = psn()
                        for hf_ in range(2):
                            sl = slice(hf_ * 512, (hf_ + 1) * 512)
                            mm(pr_[:, sl], rgw[:, c, d * 2 + 0, :], xcb_[:, sl], True, True)
                            mm(pi_[:, sl], rgw[:, c, d * 2 + 1, :], xcb_[:, sl], True, True)
                        act(SA["r"][d], pr_[:], AF.Tanh, bias=pc("hb", (d * 2 + 0) * 8 + c), scale=0.5)
                        act(SA["i"][d], pi_[:], AF.Tanh, bias=pc("hb", (d * 2 + 1) * 8 + c), scale=0.5)
                    for d in range(2):
                        act(SA["q"][d], SA["r"][d], AF.Exp, bias=pc("cl1", d * 8 + c), scale=pc("cl1", d * 8 + c))
                        act(SA["r"][d], SA["r"][d], AF.Exp, bias=pc("clh", d * 8 + c), scale=pc("clh", d * 8 + c))
                    for d in range(2):
                        act(SA["q"][d], SA["q"][d], AF.Sqrt, bias=1.0, scale=-1.0)

                def A_scan(c):
                    xc, hf = SA["xc"][c % 2], SA["hf"]
                    for d in range(2):
                        a_, u_, s_ = SA["r"][d], SA["i"][d], SA["q"][d]
                        stt(u_, u_, 1.0, s_, ALU.add, ALU.mult)
                        stt(u_, u_, 0.5, xc, ALU.mult, ALU.mult)
                        col = (l * 2 + d) * 8 + c
                        if d == 0:
                            bnd = a_[:, 256:1024:256]
                            ts(bnd, bnd, flag[:], None, ALU.mult)
                            P.add("dve", (lambda a_=a_, u_=u_, hf=hf, col=col: lambda e: e.tensor_tensor_scan(
                                out=hf, data0=a_, data1=u_, initial=h0T[:, col:col + 1], op0=ALU.mult, op1=ALU.add))(),
                                reads=[a_, u_, h0T[:]], writes=[hf])
                            cp(ST[:, :, l, 0, c], hf[:, 255:1024:256])
                        else:
                            bnd = a_[:, 255:1023:256]
                            ts(bnd, bnd, flag[:], None, ALU.mult)
                            P.add("dve", (lambda a_=a_, u_=u_, s_=s_, col=col: lambda e: e.tensor_tensor_scan(
                                out=s_[:, ::-1], data0=a_[:, ::-1], data1=u_[:, ::-1], initial=h0T[:, col:col + 1],
                                op0=ALU.mult, op1=ALU.add))(),
                                reads=[a_, u_, h0T[:]], writes=[s_])
                            cp(ST[:, :, l, 1, c], s_[:, 0:1024:256])
                            tt(hf, hf, s_, ALU.add)

                def A_out(c):
                    iu, wu = unitsA[c // 2]
                    p2 = psn()
                    proj_cols(p2, wu, 256 + (c % 2) * 128, h_b)
                    if c % 2 == 1:
                        W.release(iu)
                    act(SA["r"][0], p2[:], AF.Gelu_apprx_tanh, bias=pc("b_in", 8 + c))
                    tt(ya[:, c, :], SA["hf"], SA["r"][0], ALU.mult)

                ring4[0] = True
                C_front(0)
                A_front_a(0)
                C_conv(0)
                A_front_b(0)
                for c in range(NCH):
                    if c + 1 < NCH:
                        C_front(c + 1)
                        A_front_a(c + 1)
                    A_gates(c)
                    if c + 1 < NCH:
                        C_conv(c + 1)
                        A_front_b(c + 1)
                    A_scan(c)
                    A_out(c)
                    if c + 1 < NCH:
                        C_lnacc(c + 1)
                dump("ya", ya)

                ring4[0] = False
                pS1 = PSX
                pS2 = psn()
                for hf_ in range(2):
                    sl = slice(hf_ * 512, (hf_ + 1) * 512)
                    mm(pS1[:, sl], ones_f[:], TB[:, sl], True, True)
                    mm(pS2[:, sl], ones_f[:], TC[:, sl], True, True)
                act(TM[:], pS1[:], AF.Identity, scale=1.0 / D)
                act(TD[:], pS1[:], AF.Square, scale=1.0 / D)
                stt(TR[:], pS2[:], 1.0 / D, TD[:], ALU.mult, ALU.subtract)
                act(TR[:], TR[:], AF.Ln, bias=float(EPS))
                act(TR[:], TR[:], AF.Exp, scale=-0.5)
                stt(TM[:], TM[:], -1.0, TR[:], ALU.mult, ALU.mult)

                def ctail_gen():
                    for c in range(NCH):
                        tt(TD[:], ycc[:, c, :], TR[:], ALU.mult)
                        tt(TD[:], TD[:], TM[:], ALU.add)
                        act(yc[:, c, :], TD[:], AF.Silu, bias=pc("clb", c), scale=pc("clg", c))
                        yield

                ring4[0] = True
                svn2 = bfv(K16, 8, 1024)
                dma(brow[:], din["b_in"][l, 3072:4096].rearrange("(o n) -> o n", o=1), "in", wr=[brow[:]])
                iv0, wv0 = win_unit(6)
                iv1, wv1 = win_unit(7)
                def bsv_gen():
                    for tb in range(4):
                        for tq in (2 * tb, 2 * tb + 1):
                            p1 = psn()
                            for hh, wv in enumerate((wv0, wv1)):
                                sl = slice(hh * 512, (hh + 1) * 512)
                                for k in range(8):
                                    mm(p1[:, sl], h_b[:, k, tq * 128:(tq + 1) * 128], wv[:, k, :], k == 0, False)
                                mm(p1[:, sl], ones_f[0:1, :], brow[0:1, sl], False, True)
                            svt = TA if tq % 2 == 0 else TB
                            act(svt[:], p1[:], AF.Gelu_apprx_tanh)
                            so = (tq % 2) * 16
                            P.add("dve", (lambda svt, so: lambda e: e.bn_stats(out=small[:, so:so + 6], in_=svt[:, 0:512]))(svt, so),
                                  reads=[svt[:, 0:512]], writes=[small[:, so:so + 6]])
                            P.add("dve", (lambda svt, so: lambda e: e.bn_stats(out=small[:, so + 6:so + 12], in_=svt[:, 512:1024]))(svt, so),
                                  reads=[svt[:, 512:1024]], writes=[small[:, so + 6:so + 12]])
                            P.add("dve", (lambda so: lambda e: e.bn_aggr(out=small[:, so + 12:so + 14], in_=small[:, so:so + 12]))(so),
                                  reads=[small[:, so:so + 12]], writes=[small[:, so + 12:so + 14]])
                        for tq in (2 * tb, 2 * tb + 1):
                            so = (tq % 2) * 16
                            act(small[:, so + 13:so + 14], small[:, so + 13:so + 14], AF.Sqrt, bias=float(EPS))
                        for tq in (2 * tb, 2 * tb + 1):
                            svt = TA if tq % 2 == 0 else TB
                            so = (tq % 2) * 16
                            P.add("dve", (lambda so: lambda e: e.reciprocal(out=small[:, so + 13:so + 14], in_=small[:, so + 13:so + 14]))(so),
                                  reads=[small[:, so + 13:so + 14]], writes=[small[:, so + 13:so + 14]])
                            stt(small[:, so + 14:so + 15], small[:, so + 12:so + 13], -1.0, small[:, so + 13:so + 14], ALU.mult, ALU.mult)
                            ts(svn2[:, tq, :], svt[:], small[:, so + 13:so + 14], small[:, so + 14:so + 15], ALU.mult, ALU.add)
                        yield

                ring4[0] = True
                pipeline([ctail_gen(), bsv_gen()], lag=1)
                dump("yc", yc)
                W.release(iv0)
                W.release(iv1)
                for ug in range(2):
                    iu, wu = win_unit(4 + ug)
                    for cc in range(4):
                        g = ug * 4 + cc
                        p2 = psn()
                        proj_chunk(p2, wu, cc, h_b)
                        act(TB[:], p2[:], AF.Gelu_apprx_tanh, bias=pc("b_in", 16 + g))
                        p1 = psn()
                        for n in range(8):
                            mm(p1[:, n * 128:(n + 1) * 128], svn2[:, n, g * 128:(g + 1) * 128], sgwT[:, g, :], True, True)
                        stt(TA[:].rearrange("p (n q) -> p n q", q=128), p1[:].rearrange("p (n q) -> p n q", q=128),
                            pc("sgg", g), bsg[:, g, :].unsqueeze(1).broadcast_to([128, 8, 128]), ALU.mult, ALU.add)
                        tt(yb[:, g, :], TA[:], TB[:], ALU.mult)
                    W.release(iu)
                dump("yb", yb)
                macc = big32
                ysrc = (ya, yb, yc)
                for br in range(3):
                    wbv = din["w_branch"][l, br].rearrange("(k p) n -> p k n", p=128)
                    for jp in range(4):
                        gc0 = 6144 + br * 1024 + jp * 256
                        imu, wmu = wacq2([(0, wi[:, :, gc0:gc0 + 256]), (256, wbv[:, :, jp * 256:(jp + 1) * 256])], 8)
                        for cc in range(2):
                            j = jp * 2 + cc
                            pg = psn()
                            proj_cols(pg, wmu, cc * 128, h_b)
                            pp = psn()
                            proj_cols(pp, wmu, 256 + cc * 128, ysrc[br])
                            sg_ = TA if j % 2 == 0 else TB
                            act(sg_[:], pg[:], AF.Sigmoid, bias=pc("b_in", 48 + br * 8 + j))
                            if br == 0:
                                tt(macc[:, j, :], sg_[:], pp[:], ALU.mult)
                            elif br == 1:
                                tt(sg_[:], sg_[:], pp[:], ALU.mult)
                                tt(macc[:, j, :], macc[:, j, :], sg_[:], ALU.add)
                            else:
                                tt(sg_[:], sg_[:], pp[:], ALU.mult)
                                tt(mg[:, j, :], macc[:, j, :], sg_[:], ALU.add)
                        W.release(imu)

                dump("mg", mg)
                ring4[0] = False
                of32 = big32
                wov = din["w_out"][l].rearrange("(k p) n -> p k n", p=128)
                pS = PSX
                for ug in range(2):
                    io, wo = wacq(wov[:, :, ug * 512:(ug + 1) * 512], 8, 512)
                    for cc in range(4):
                        j = ug * 4 + cc
                        p1 = psn()
                        proj_chunk(p1, wo, cc, mg)
                        act(of32[:, j, :], p1[:], AF.Copy)
                        q = sqb[sqi[0] % 2]
                        sqi[0] += 1
                        act(q[:], p1[:], AF.Square)
                        for hf_ in range(2):
                            sl = slice(hf_ * 512, (hf_ + 1) * 512)
                            mm(pS[:, sl], ones_b[:], q[:, sl], j == 0, j == 7)
                    W.release(io)
                rstd_from(pS, TR)
                residual_update(of32, "gg1")

                if stop_after == "s1":
                    break

                modnorm(h2_b, "gs2", 24)
                wup = din["ffn_up"][l].rearrange("(k p) n -> p k n", p=128)
                ring4[0] = True
                for ug in range(16):
                    ifu, wfu = wacq2([(0, wup[:, :, ug * 256:(ug + 1) * 256]),
                                      (256, wup[:, :, DFF + ug * 256: DFF + (ug + 1) * 256])], 8)
                    for cc in range(2):
                        i = ug * 2 + cc
                        Ys = []
                        for side in (0, 1):
                            cch = side * 32 + i
                            p1 = psn()
                            proj_cols(p1, wfu, side * 256 + cc * 128, h2_b)
                            Y = (TA, TB)[side] if i % 2 == 0 else (TC, TD)[side]
                            Ys.append(Y)
                            act(Y[:], p1[:], AF.Identity, bias=pc("fcb", cch), scale=pc("fcw", 64 + cch))
                            Y4 = seg4(Y[:])
                            p4 = seg4(p1[:])
                            stt(Y4[:, :, 1:256], p4[:, :, 0:255], pc("fcw", cch), Y4[:, :, 1:256], ALU.mult, ALU.add)
                            stt(Y4[:, :, 0:255], p4[:, :, 1:256], pc("fcw", 128 + cch), Y4[:, :, 0:255], ALU.mult, ALU.add)
                            stt(Y[:, 256:1024:256], p1[:, 255:1023:256], pc("w0f", cch), Y[:, 256:1024:256], ALU.mult, ALU.add)
                            stt(Y[:, 255:1023:256], p1[:, 256:1024:256], pc("w2f", cch), Y[:, 255:1023:256], ALU.mult, ALU.add)
                        act(Ys[0][:], Ys[0][:], AF.Gelu_apprx_tanh)
                        tt(f_b[:, i, :], Ys[0][:], Ys[1][:], ALU.mult, eng="pool")
                    W.release(ifu)
                ring4[0] = False
                cp(gg2k[:], pc("gg2", 0, 8))
                modg = None
                if l + 1 < Lr:
                    layer_params(l + 1)
                    modg = layer_mod_gen(l + 1)
                wdn = din["ffn_down"][l]
                pS = PSX
                nmod = 0
                for j in range(NCH):
                    while modg is not None and nmod < (13 * (j + 1) + 7) // 8:
                        next(modg, None)
                        nmod += 1
                    idn, wd = wacq(wdn[j].rearrange("p (k n) -> p k n", n=128), 32, 128)
                    p1 = psn()
                    proj_chunk(p1, wd, 0, f_b, nk=32)
                    W.release(idn)
                    act(fo32[:, j, :], p1[:], AF.Copy)
                    q = sqb[sqi[0] % 2]
                    sqi[0] += 1
                    act(q[:], p1[:], AF.Square)
                    for hf_ in range(2):
                        sl = slice(hf_ * 512, (hf_ + 1) * 512)
                        mm(pS[:, sl], ones_b[:], q[:, sl], j == 0, j == 7)
                rstd_from(pS, TR)
                for j in range(NCH):
                    tmp = TA if j % 2 == 0 else TB
                    tt(tmp[:], fo32[:, j, :], TR[:], ALU.mult)
                    stt(xT[:, j, :], tmp[:], gg2k[:, j:j + 1], xT[:, j, :], ALU.mult, ALU.add)

            for tq in range(8):
                p0 = psn()
                for k in range(NCH):
                    tr(p0[:, k * 128:(k + 1) * 128], xT[:, k, tq * 128:(tq + 1) * 128], ident_f[:])
                stg = TA if tq % 2 == 0 else TB
                if tq % 2 == 0:
                    act(stg[:], p0[:], AF.Copy)
                else:
                    cp(stg[:], p0[:])
                dma(y_out[tq * 128:(tq + 1) * 128, :], stg[:], "out", rd=[stg[:]])
            ncol = NSEG * Lr * 16
            stf = ST[:].rearrange("p s l d c -> p (s l d c)")
            nb = (ncol + 127) // 128
            p0 = psn()
            for q in range(nb):
                n = min(128, ncol - q * 128)
                tr(p0[0:n, q * 128:(q + 1) * 128], stf[:, q * 128:q * 128 + n], ident_f[:])
            for q in range(nb):
                n = min(128, ncol - q * 128)
                cp(TC[0:n, q * 128:(q + 1) * 128], p0[0:n, q * 128:(q + 1) * 128])
                dma(st_out[q * 128:q * 128 + n, :], TC[0:n, q * 128:(q + 1) * 128], "out", rd=[TC[0:n, q * 128:(q + 1) * 128]])

        Wd = WStream(_DryProg(), wslots, None)
        emit_all(_DryProg(), Wd)
        P = Prog(nc)
        W = WStream(P, wslots, Wd.rec)
        emit_all(P, W)
        P.finalize(ctx)
    return nc


_NC_CACHE = {}


def _run(inputs, depth=4, stop_after=None, dbg=(), ret_raw=False):
    key = (depth, stop_after, tuple(dbg))
    if key not in _NC_CACHE:
        _NC_CACHE[key] = build_nc(depth, stop_after, dbg)
    nc = _NC_CACHE[key]
    f32 = lambda a: np.ascontiguousarray(np.asarray(a, dtype=np.float32))
    xp = f32(inputs["x_prompt"])
    xs = f32(inputs["x_sample"])
    st = f32(inputs["state_rglru"])
    c = f32(inputs["c"])
    cctx = f32(inputs["c_ctx"])
    wts = {n: f32(inputs[n])[:depth] for n in WNAMES}
    wts["ffn_down"] = np.ascontiguousarray(
        wts["ffn_down"].reshape(depth, 32, 128, NCH, 128).transpose(0, 3, 2, 1, 4)).reshape(depth, NCH, 128, 32 * 128)
    wts["w_mod"] = np.ascontiguousarray(
        wts["w_mod"].reshape(depth, 8, 128, 12, 512).transpose(0, 3, 2, 1, 4)).reshape(depth, 12, 128, 8 * 512)
    in_maps = []
    for core in range(8):
        m = dict(wts)
        if core < 4:
            m["x"] = np.ascontiguousarray(xp[4 * core:4 * core + 4].reshape(T, D))
            m["cond"] = cctx.reshape(8, 128)
            m["flag"] = np.zeros((128, 1), np.float32)
            m["h0"] = np.zeros((depth * 16, 128), np.float32)
        else:
            j = core - 4
            m["x"] = xs[j]
            m["cond"] = np.ascontiguousarray(c[j].reshape(8, 128))
            m["flag"] = np.ones((128, 1), np.float32)
            m["h0"] = np.ascontiguousarray(st[j, :depth].reshape(depth * 16, 128))
        in_maps.append(m)
    res = run_bass_kernel_spmd(nc, in_maps, core_ids=list(range(8)))
    outs = res.results
    if ret_raw:
        return outs
    y_prompt = np.stack([outs[i]["y"] for i in range(4)]).reshape(16, 256, D)
    y_sample = np.stack([outs[4 + i]["y"] for i in range(4)])
    new_state = np.concatenate([outs[i]["st"].reshape(NSEG, depth, 2, D) for i in range(4)], axis=0)
    return (y_prompt.astype(np.float32), y_sample.astype(np.float32), new_state.astype(np.float32))


def kernel(**inputs):
    return _run(inputs, depth=4)
```
